# Optimizing a Trainium2 kernel written in Bass

```python
import math
import jax, jax.numpy as jnp
from jax import lax
import numpy as np

D_MODEL = 1024
BATCH = 8
SEQ = 8192
DEPTH = 4

N_MEM = 256
BLOCK_Q = 128
MIX_WIDTH = D_MODEL
MEM_WIDTH = D_MODEL // 4
MAIN_WIDTH = MIX_WIDTH - MEM_WIDTH
MEM_HEADS = 4
MEM_DIM = MEM_WIDTH // MEM_HEADS
MLA_V = 64
MLA_HEADS = MAIN_WIDTH // MLA_V
MLA_NOPE = 64
MLA_ROPE = 32
MLA_QK = MLA_NOPE + MLA_ROPE
Q_LORA = 384
KV_LORA = 256
ROPE_THETA = 10000.0
MAX_POS_OFFSET = 4096
FOX_DIM = 64
FOX_HEADS = MAIN_WIDTH // FOX_DIM
EPS = 1e-6
N_A = (DEPTH + 1) // 2
N_B = DEPTH // 2
A_IN = Q_LORA + KV_LORA + MLA_ROPE + MEM_WIDTH + MIX_WIDTH
B_IN = 3 * MAIN_WIDTH + FOX_HEADS + MEM_WIDTH + MIX_WIDTH

kernel_name = "hybrid_mla_fox_memory_trunk"


def rms_norm(x, g):
    xf = x.astype(jnp.float32)
    y = xf * lax.rsqrt(jnp.mean(xf * xf, axis=-1, keepdims=True) + EPS)
    return (y * g.astype(jnp.float32)).astype(x.dtype)


def split_cols(a, sizes):
    idx = np.cumsum(sizes)[:-1].tolist()
    return jnp.split(a, idx, axis=-1)


def rope(x, pos):
    half = x.shape[-1] // 2
    inv = ROPE_THETA ** (-jnp.arange(half, dtype=jnp.float32) / half)
    ang = pos.astype(jnp.float32)[..., None] * inv
    if x.ndim == 4:
        ang = ang[:, :, None, :]
    cos, sin = jnp.cos(ang), jnp.sin(ang)
    x1 = x[..., :half].astype(jnp.float32)
    x2 = x[..., half:].astype(jnp.float32)
    return jnp.concatenate([x1 * cos - x2 * sin, x2 * cos + x1 * sin], axis=-1).astype(x.dtype)


def causal_block_attention(q, k, v, scale, log_f_cum=None):
    B, S, H, Dk = q.shape
    nb = S // BLOCK_Q
    qb = q.reshape(B, nb, BLOCK_Q, H, Dk).swapaxes(0, 1)
    blk = jnp.arange(nb)
    key_pos = jnp.arange(S)
    if log_f_cum is not None:
        f_t = log_f_cum.transpose(0, 2, 1)
        fb = jnp.moveaxis(f_t.reshape(B, H, nb, BLOCK_Q), 2, 0)
        xs = (blk, qb, fb)
    else:
        xs = (blk, qb)

    def one_block(args):
        i, q_i = args[0], args[1]
        s = jnp.einsum('bqhd,bkhd->bhqk', q_i, k,
                       preferred_element_type=jnp.float32) * scale
        if log_f_cum is not None:
            s = s + (args[2][..., None].astype(jnp.float32) - f_t[:, :, None, :].astype(jnp.float32))
        q_pos = i * BLOCK_Q + jnp.arange(BLOCK_Q)
        mask = key_pos[None, :] <= q_pos[:, None]
        s = jnp.where(mask, s, -jnp.inf)
        p = jax.nn.softmax(s, axis=-1)
        return jnp.einsum('bhqk,bkhd->bqhd', p.astype(v.dtype), v)

    out = lax.map(one_block, xs)
    return out.swapaxes(0, 1).reshape(B, S, H, v.shape[-1])


def memory_attention(q, k, v):
    s = jnp.einsum('bshd,bmhd->bhsm', q, k,
                   preferred_element_type=jnp.float32) * (MEM_DIM ** -0.5)
    p = jax.nn.softmax(s, axis=-1)
    return jnp.einsum('bhsm,bmhd->bshd', p.astype(v.dtype), v)


def mla_mixer(c_q, c_kv, k_r, pos, q_norm, kv_norm, w_q_up, w_kv_up):
    B, S, _ = c_q.shape
    q = (rms_norm(c_q, q_norm) @ w_q_up).reshape(B, S, MLA_HEADS, MLA_QK)
    q_nope, q_rope = q[..., :MLA_NOPE], rope(q[..., MLA_NOPE:], pos)
    kv = (rms_norm(c_kv, kv_norm) @ w_kv_up).reshape(B, S, MLA_HEADS, MLA_NOPE + MLA_V)
    k_nope, v = kv[..., :MLA_NOPE], kv[..., MLA_NOPE:]
    k_rope = jnp.broadcast_to(rope(k_r, pos)[:, :, None, :], (B, S, MLA_HEADS, MLA_ROPE))
    qf = jnp.concatenate([q_nope, q_rope], axis=-1)
    kf = jnp.concatenate([k_nope, k_rope], axis=-1)
    out = causal_block_attention(qf, kf, v, MLA_QK ** -0.5)
    return out.reshape(B, S, MAIN_WIDTH)


def fox_mixer(q, k, v, f_logit, b_f):
    B, S, _ = q.shape
    log_f = jax.nn.log_sigmoid(f_logit.astype(jnp.float32) + b_f.astype(jnp.float32))
    log_f_cum = jnp.cumsum(log_f, axis=1)
    qh = q.reshape(B, S, FOX_HEADS, FOX_DIM)
    kh = k.reshape(B, S, FOX_HEADS, FOX_DIM)
    vh = v.reshape(B, S, FOX_HEADS, FOX_DIM)
    out = causal_block_attention(qh, kh, vh, FOX_DIM ** -0.5, log_f_cum)
    return out.reshape(B, S, MAIN_WIDTH)


def setup_inputs(seed: int = 0) -> dict:
    key = jax.random.key(seed)
    ks = jax.random.split(key, 16)
    f32 = jnp.float32

    def nrm(k, shape, fan_in):
        return jax.random.normal(k, shape, f32) * (fan_in ** -0.5)

    def gain(k, shape):
        return 1.0 + 0.01 * jax.random.normal(k, shape, f32)

    x = jax.random.normal(ks[0], (BATCH, SEQ, D_MODEL), f32)
    mem = jax.random.normal(ks[1], (BATCH, N_MEM, D_MODEL), f32)
    positions = (jax.random.randint(ks[2], (BATCH, 1), 0, MAX_POS_OFFSET, dtype=jnp.int32)
                 + jnp.arange(SEQ, dtype=jnp.int32)[None, :])
    return {
        "x": x,
        "mem": mem,
        "positions": positions,
        "norm_pre": gain(ks[3], (DEPTH, D_MODEL)),
        "norm_post": gain(ks[4], (DEPTH, D_MODEL)),
        "mem_norm": gain(ks[5], (D_MODEL,)),
        "w_mem_kv": nrm(ks[6], (DEPTH, D_MODEL, 2 * MEM_WIDTH), D_MODEL),
        "w_out": nrm(ks[7], (DEPTH, MIX_WIDTH, D_MODEL), MIX_WIDTH),
        "w_in_a": nrm(ks[8], (N_A, D_MODEL, A_IN), D_MODEL),
        "q_norm_a": gain(ks[9], (N_A, Q_LORA)),
        "kv_norm_a": gain(ks[10], (N_A, KV_LORA)),
        "w_q_up_a": nrm(ks[11], (N_A, Q_LORA, MLA_HEADS * MLA_QK), Q_LORA),
        "w_kv_up_a": nrm(ks[12], (N_A, KV_LORA, MLA_HEADS * (MLA_NOPE + MLA_V)), KV_LORA),
        "w_in_b": nrm(ks[13], (N_B, D_MODEL, B_IN), D_MODEL),
        "b_f": jax.random.uniform(ks[14], (N_B, FOX_HEADS), f32, 1.0, 4.0),
    }


def reference(x, mem, positions, norm_pre, norm_post, mem_norm, w_mem_kv, w_out,
              w_in_a, q_norm_a, kv_norm_a, w_q_up_a, w_kv_up_a, w_in_b, b_f):
    B, S, _ = x.shape
    M = mem.shape[1]
    mem_n = rms_norm(mem, mem_norm)
    h = x
    for i in range(DEPTH):
        j = i // 2
        hn = rms_norm(h, norm_pre[i])
        mkv = (mem_n @ w_mem_kv[i]).reshape(M, 2, MEM_HEADS, MEM_DIM) if False else \
            (mem_n @ w_mem_kv[i]).reshape(B, M, 2, MEM_HEADS, MEM_DIM)
        k_m, v_m = mkv[:, :, 0], mkv[:, :, 1]
        if i % 2 == 0:
            proj = hn @ w_in_a[j]
            c_q, c_kv, k_r, q_m, gate = split_cols(
                proj, [Q_LORA, KV_LORA, MLA_ROPE, MEM_WIDTH, MIX_WIDTH])
            main = mla_mixer(c_q, c_kv, k_r, positions, q_norm_a[j], kv_norm_a[j],
                             w_q_up_a[j], w_kv_up_a[j])
        else:
            proj = hn @ w_in_b[j]
            q, k, v, f_logit, q_m, gate = split_cols(
                proj, [MAIN_WIDTH, MAIN_WIDTH, MAIN_WIDTH, FOX_HEADS, MEM_WIDTH, MIX_WIDTH])
            main = fox_mixer(q, k, v, f_logit, b_f[j])
        mem_out = memory_attention(q_m.reshape(B, S, MEM_HEADS, MEM_DIM), k_m, v_m)
        y = jnp.concatenate([main, mem_out.reshape(B, S, MEM_WIDTH)], axis=-1) * jax.nn.silu(gate)
        h = h + rms_norm(y @ w_out[i], norm_post[i])
    return h
```

```python
import math
import os
from contextlib import ExitStack

import numpy as np
import concourse.bass as bass
import concourse.mybir as mybir
from concourse.bass_utils import run_bass_kernel_spmd

F32 = mybir.dt.float32
BF16 = mybir.dt.bfloat16
I32 = mybir.dt.int32
AF = mybir.ActivationFunctionType
ALU = mybir.AluOpType

D = 1024
NMEM = 256
EPS = 1e-6
A_IN = 1952
B_IN = 3596
NEG = -30000.0
TWO_PI = 2.0 * math.pi

ENGS = ("pe", "act", "dve", "pool", "sp")


class _Rec:
    def __init__(self):
        self.call = None

    def __getattr__(self, name):
        def f(*args, **kwargs):
            self.call = (name, args, kwargs)
            return self
        return f


class DSem:
    __slots__ = ("sem", "total", "name")

    def __init__(self, sem, name):
        self.sem = sem
        self.total = 0
        self.name = name


class Buf:
    __slots__ = ("name", "w", "r", "ds")

    def __init__(self, name):
        self.name = name
        self.w = None
        self.r = []
        self.ds = None


class Prog:
    def __init__(self, nc, stack, n_dsem=72):
        self.nc = nc
        self.q = {e: [] for e in ENGS}
        self.sem = {e: stack.enter_context(nc.semaphore("s_" + e)) for e in ENGS}
        self.cnt = {e: 0 for e in ENGS}
        self.seen = {e: {} for e in ENGS}
        self.free_ds = [DSem(stack.enter_context(nc.semaphore("dq%d" % i)), "dq%d" % i) for i in range(n_dsem)]
        self.all_ds = list(self.free_ds)
        self.phase_bufs = []
        self.ninst = 0
        self.embed_default = True
        self.bar_sem = stack.enter_context(nc.semaphore("s_bar"))
        self.bar_total = 0
        self.bar_src = nc.dram_tensor("bar_src", [1, 16], F32).ap()[:, :]
        self.bar_dst = stack.enter_context(nc.sbuf_tensor("bar_dst", [1, 16], F32))[:]

    def _deps(self, eng, reads, writes):
        deps = {}

        def add(tok):
            if tok is None:
                return
            key, sem, val, ds = tok
            if ds is not None:
                val = ds.total
            if key == eng and eng == "pe":
                return
            cur = deps.get(key)
            if cur is None or cur[1] < val:
                deps[key] = (sem, val)

        for b in reads:
            add(b.w)
        for b in writes:
            add(b.w)
            for t in b.r:
                add(t)
        out = []
        seen = self.seen[eng]
        for key, (sem, val) in deps.items():
            if seen.get(key, 0) >= val:
                continue
            seen[key] = val
            out.append((sem, val))
        return out

    def _mark(self, tok, reads, writes):
        for b in writes:
            b.w = tok
            b.r = []
        for b in reads:
            b.r.append(tok)
            if len(b.r) > 16:
                last = {}
                for t in b.r:
                    k = t[0]
                    if k not in last or last[k][2] < t[2]:
                        last[k] = t
                b.r = list(last.values())

    def op(self, eng, fn, reads=(), writes=(), sig=True, embed=None):
        waits = self._deps(eng, reads, writes)
        if sig:
            self.cnt[eng] += 1
            val = self.cnt[eng]
        else:
            val = self.cnt[eng] + 1
        sem = self.sem[eng]
        tok = (eng, sem, val, None)
        self.ninst += 1 + len(waits)
        rec = _Rec()
        fn(rec)
        call = rec.call
        if embed is None:
            embed = (eng in ("act", "dve", "pool")) and call[2].get("accum_out") is None and self.embed_default
        emb = None
        if embed and waits:
            csems = [self.sem[e_] for e_ in ("pe", "act", "dve", "pool")]
            for wi in range(len(waits) - 1, -1, -1):
                if any(waits[wi][0] is cs_ for cs_ in csems):
                    emb = waits[wi]
                    waits = waits[:wi] + waits[wi + 1:]
                    break

        def emit(e, waits=waits, call=call, sig=sig, sem=sem, emb=emb):
            for (s, v) in waits:
                e.wait_ge(s, v)
            ins = getattr(e, call[0])(*call[1], **call[2])
            if emb is not None:
                ins = ins._wait_ge(emb[0], emb[1])
            if sig:
                ins.then_inc(sem, 1)

        self.q[eng].append(emit)
        self._mark(tok, reads, writes)
        return tok

    def dma(self, owner, out, in_, reads=(), writes=(), eng="sp", tag=None, **kw):
        import os
        if tag is not None and tag in os.environ.get("KDMA_POOL", "").split(","):
            eng = "pool"
        if tag is not None and tag in os.environ.get("KDMA_ACT", "").split(","):
            eng = "act"
        if tag is not None and tag in os.environ.get("KDMA_SKIP", "").split(","):
            return None
        waits = self._deps(eng, reads, writes)
        if owner.ds is None:
            owner.ds = self.free_ds.pop()
            self.phase_bufs.append(owner)
        ds = owner.ds
        ds.total += 16
        tok = (ds.name, ds.sem, ds.total, ds)
        self.ninst += 1 + len(waits)

        def emit(e, waits=waits, sem=ds.sem, out=out, in_=in_, kw=kw):
            for (s, v) in waits:
                e.wait_ge(s, v)
            e.dma_start(out=out, in_=in_, **kw).then_inc(sem, 16)

        self.q[eng].append(emit)
        self._mark(tok, reads, writes)
        return tok

    def barrier(self):
        finals = [(d.sem, d.total, d.name) for d in self.all_ds if d.total > 0]
        finals += [(self.sem[e], self.cnt[e], e) for e in ("pe", "act", "dve", "pool") if self.cnt[e] > 0]
        self.bar_total += 16
        bv = self.bar_total
        bsem = self.bar_sem
        bsrc, bdst = self.bar_src, self.bar_dst

        def emit_sp(e):
            for (s, v, _) in finals:
                e.wait_ge(s, v)
            e.dma_start(out=bdst, in_=bsrc).then_inc(bsem, 16)
            e.wait_ge(bsem, bv)

        self.q["sp"].append(emit_sp)
        for eng in ("pe", "act", "dve", "pool"):
            self.q[eng].append(lambda e: e.wait_ge(bsem, bv))
        for eng in ENGS:
            for (_, v, key) in finals:
                self.seen[eng][key] = v
        for b in self.phase_bufs:
            self.free_ds.append(b.ds)
            b.ds = None
        self.phase_bufs = []

    def flush(self):
        import os
        if os.environ.get("KCOUNT"):
            class _C:
                def __init__(s): s.n = 0
                def __getattr__(s, name):
                    def f(*a, **k):
                        s.n += 1
                        return s
                    return f
            for en in ENGS:
                c = _C()
                for f in self.q[en]:
                    f(c)
                self.tot = getattr(self, "tot", {})
                self.tot[en] = self.tot.get(en, 0) + c.n
            print("COUNT", self.tot, flush=True)
        q = self.q
        with self.nc.Block() as block:
            @block.tensor
            def _(e):
                for f in q["pe"]:
                    f(e)

            @block.scalar
            def _(e):
                for f in q["act"]:
                    f(e)

            @block.vector
            def _(e):
                for f in q["dve"]:
                    f(e)

            @block.gpsimd
            def _(e):
                for f in q["pool"]:
                    f(e)

            @block.sync
            def _(e):
                for f in q["sp"]:
                    f(e)
        self.q = {e: [] for e in ENGS}


class Ctx:
    pass


class StopBuild(Exception):
    pass


def ck(n):
    import os
    if n > float(os.environ.get("KA", "99")):
        raise StopBuild()


_UID = [0]


def U(name):
    _UID[0] += 1
    return "%s_%d" % (name, _UID[0])


def build(S, layer_types, debug_out=None):
    NT = S // 512
    NKT = S // 128
    depth = len(layer_types)
    n_a = max(1, sum(1 for t in layer_types if t == "a"))
    n_b = max(1, sum(1 for t in layer_types if t == "b"))
    nc = bass.Bass("TRN2", target_bir_lowering=False)
    dt_in = lambda name, shape, dt=F32: nc.dram_tensor(name, shape, dt, kind="ExternalInput").ap()
    x = dt_in("x", [S, D])
    mem = dt_in("mem", [NMEM, D])
    pos = dt_in("pos", [1, S], I32)
    norm_pre = dt_in("norm_pre", [depth, D])
    norm_post = dt_in("norm_post", [depth, D])
    mem_norm = dt_in("mem_norm", [1, D])
    w_mem_kv = dt_in("w_mem_kv", [depth, D, 512])
    w_out = dt_in("w_out", [depth, D, D])
    w_in_a = dt_in("w_in_a", [n_a, D, A_IN])
    q_norm_a = dt_in("q_norm_a", [n_a, 384])
    kv_norm_a = dt_in("kv_norm_a", [n_a, 256])
    w_q_up_a = dt_in("w_q_up_a", [n_a, 384, 1152])
    w_kv_up_a = dt_in("w_kv_up_a", [n_a, 256, 1536])
    w_in_b = dt_in("w_in_b", [n_b, D, B_IN])
    b_f = dt_in("b_f", [n_b, 12])
    ropec = dt_in("ropec", [128, 1])
    out = nc.dram_tensor("out", [S, D], F32, kind="ExternalOutput").ap()

    if debug_out:
        scr = lambda name, shape, dt: nc.dram_tensor(name, shape, dt, kind="ExternalOutput").ap()
    else:
        scr = lambda name, shape, dt: nc.dram_tensor(name, shape, dt).ap()
    hbuf = scr("hbuf", [S, D], F32)
    QT = scr("QT", [12, 96, S], BF16)
    KT = scr("KT", [12, 96, S], BF16)
    VS = scr("VS", [12, 128, NKT, 65], BF16)
    SG = scr("SG", [16, 64, S], BF16)
    QM = scr("QM", [4, 64, S], BF16)
    YT = scr("YT", [D, S], BF16)
    COS = scr("COS", [128, S], F32)
    SIN = scr("SIN", [128, S], F32)

    with ExitStack() as gst:
        P = Prog(nc, gst)
        gsb = lambda name, shape, dt: gst.enter_context(nc.sbuf_tensor(name, shape, dt))

        ident = gsb("ident", [128, 128], BF16)
        ones_bf = gsb("ones_bf", [128, 128], BF16)
        ones_f = gsb("ones_f", [128, 128], F32)
        maskb = gsb("maskb", [128, 128], BF16)
        gpre = gsb("gpre", [128, depth, 8], F32)
        gq = gsb("gq", [128, n_a, 3], F32)
        gkv = gsb("gkv", [128, n_a, 2], F32)
        gmem = gsb("gmem", [128, 8], F32)
        negbf = gsb("negbf", [12, n_b], F32)
        ropec_sb = gsb("ropec_sb", [128, 1], F32)
        memnT = gsb("memnT", [128, 8, NMEM], BF16)
        km_sb = gsb("km_sb", [64, 4, NMEM], BF16)
        vm_sb = gsb("vm_sb", [128, 2, 4, 65], BF16)
        b_const = Buf("const")
        b_memnT = Buf("memnT")
        b_km = Buf("km")
        b_vm = Buf("vm")

        with ExitStack() as st:
            sb = lambda name, shape, dt: st.enter_context(nc.sbuf_tensor(U(name), shape, dt))
            ps = lambda name, shape, dt: st.enter_context(nc.psum_tensor(U(name), shape, dt))
            tmpf = sb("tmpf", [128, 128], F32)
            b_tmpf = Buf("tmpf")
            P.op("pool", lambda e: e.memset(tmpf[:], 0.0), writes=[b_tmpf])
            P.op("pool", lambda e: e.affine_select(out=tmpf[:], in_=tmpf[:], pattern=[[-1, 128]],
                                                   compare_op=ALU.not_equal, fill=1.0, base=0, channel_multiplier=1),
                 reads=[b_tmpf], writes=[b_tmpf])
            P.op("pool", lambda e: e.tensor_copy(out=ident[:], in_=tmpf[:]), reads=[b_tmpf], writes=[b_const])
            tmpm = sb("tmpm", [128, 128], F32)
            b_tmpm = Buf("tmpm")
            P.op("pool", lambda e: e.memset(tmpm[:], 0.0), writes=[b_tmpm])
            P.op("pool", lambda e: e.affine_select(out=tmpm[:], in_=tmpm[:], pattern=[[1, 128]],
                                                   compare_op=ALU.is_ge, fill=NEG, base=0, channel_multiplier=-1),
                 reads=[b_tmpm], writes=[b_tmpm])
            P.op("pool", lambda e: e.tensor_copy(out=maskb[:], in_=tmpm[:]), reads=[b_tmpm], writes=[b_const])
            P.op("pool", lambda e: e.memset(ones_bf[:], 1.0), writes=[b_const])
            P.op("pool", lambda e: e.memset(ones_f[:], 1.0), writes=[b_const])
            P.op("pool", lambda e: e.memset(vm_sb[:], 1.0), writes=[b_vm])
            import os as _os
            _sstop = _os.environ.get("KSSTOP", "")
            b_g = Buf("gains")
            for l in range(depth if _sstop != "s1" else 0):
                P.dma(b_g, gpre[:, l, :], norm_pre[l, :].rearrange("(c p) -> p c", p=128), writes=[b_const],
                      allow_slow_non_contiguous=True)
            for j in range(n_a):
                P.dma(b_g, gq[:, j, :], q_norm_a[j, :].rearrange("(c p) -> p c", p=128), writes=[b_const],
                      allow_slow_non_contiguous=True)
                P.dma(b_g, gkv[:, j, :], kv_norm_a[j, :].rearrange("(c p) -> p c", p=128), writes=[b_const],
                      allow_slow_non_contiguous=True)
            P.dma(b_g, gmem[:, :], mem_norm[0, :].rearrange("(c p) -> p c", p=128), writes=[b_const],
                  allow_slow_non_contiguous=True)
            P.dma(b_g, negbf[:, :], b_f.rearrange("j h -> h j"), writes=[b_const], allow_slow_non_contiguous=True)
            P.dma(b_g, ropec_sb[:, :], ropec[:, :], writes=[b_const])
            P.op("dve", lambda e: e.tensor_scalar(out=negbf[:], in0=negbf[:], scalar1=-1.0, scalar2=None, op0=ALU.mult),
                 reads=[b_const], writes=[b_const])

            memt = sb("memt", [128, 2, D], F32)
            b_memt = Buf("memt")
            P.dma(b_memt, memt[:], mem.rearrange("(s p) d -> p s d", p=128), writes=[b_memt])
            junk = sb("junk", [128, D], BF16)
            b_junk = Buf("junk")
            mss = sb("mss", [128, 2], F32)
            b_mss = Buf("mss")
            mems = sb("mems", [128, 2, D], BF16)
            b_mems = Buf("mems")
            for s in range(2):
                P.op("act", lambda e, s=s: e.activation(out=junk[:], in_=memt[:, s, :], func=AF.Square,
                                                        accum_out=mss[:, s:s + 1]),
                     reads=[b_memt], writes=[b_junk, b_mss])
            mrs = sb("mrs", [128, 2], F32)
            b_mrs = Buf("mrs")
            P.op("dve", lambda e: e.tensor_scalar(out=mrs[:], in0=mss[:], scalar1=1.0 / D, scalar2=EPS, op0=ALU.mult, op1=ALU.add),
                 reads=[b_mss], writes=[b_mrs])
            P.op("act", lambda e: e.activation(out=mrs[:], in_=mrs[:], func=AF.Sqrt), reads=[b_mrs], writes=[b_mrs])
            P.op("dve", lambda e: e.reciprocal(out=mrs[:], in_=mrs[:]), reads=[b_mrs], writes=[b_mrs])
            tpm = ps("tpm", [128, 1024], BF16)
            b_tpm = Buf("tpm")
            for s in range(2):
                P.op("dve", lambda e, s=s: e.tensor_scalar(out=mems[:, s, :], in0=memt[:, s, :], scalar1=mrs[:, s:s + 1],
                                                           scalar2=None, op0=ALU.mult),
                     reads=[b_memt, b_mrs], writes=[b_mems])
                for c in range(8):
                    P.op("pe", lambda e, s=s, c=c: e.transpose(out=tpm[:, c * 128:(c + 1) * 128],
                                                               in_=mems[:, s, c * 128:(c + 1) * 128], identity=ident[:]),
                         reads=[b_mems, b_const], writes=[b_tpm], sig=(c == 7))
                P.op("dve", lambda e, s=s: e.tensor_copy(out=memnT[:, :, s * 128:(s + 1) * 128],
                                                         in_=tpm[:, :].rearrange("p (c t) -> p c t", c=8)),
                     reads=[b_tpm], writes=[b_memnT])

            if "a" in layer_types and _sstop not in ("s1", "s2", "s3"):
                CH = 1024 if S % 1024 == 0 else 512
                posi = sb("posi", [128, CH], I32)
                ang = sb("ang", [128, CH], F32)
                kf = sb("kf", [128, CH], F32)
                ki = sb("ki", [128, CH], I32)
                mk = sb("mk", [128, CH], F32)
                r2 = sb("r2", [128, CH], F32)
                sn = sb("sn", [128, CH], F32)
                cs_ = sb("cs_", [128, CH], F32)
                b_posi, b_ang, b_kf, b_ki, b_mk, b_r2, b_sn, b_cs = (Buf(n) for n in
                                                                     ("posi", "ang", "kf", "ki", "mk", "r2", "sn", "cs_"))
                PI_SAFE = 3.1415925
                C1 = 6.28125
                C2 = TWO_PI - 6.28125

                def wrap(r, b_r):
                    P.op("dve", lambda e: e.tensor_scalar(out=mk[:], in0=r[:], scalar1=math.pi, scalar2=None, op0=ALU.is_gt),
                         reads=[b_r], writes=[b_mk])
                    P.op("dve", lambda e: e.scalar_tensor_tensor(out=r[:], in0=mk[:], scalar=-TWO_PI, in1=r[:],
                                                                 op0=ALU.mult, op1=ALU.add),
                         reads=[b_mk, b_r], writes=[b_r])
                    P.op("dve", lambda e: e.tensor_scalar(out=mk[:], in0=r[:], scalar1=-math.pi, scalar2=None, op0=ALU.is_lt),
                         reads=[b_r], writes=[b_mk])
                    P.op("dve", lambda e: e.scalar_tensor_tensor(out=r[:], in0=mk[:], scalar=TWO_PI, in1=r[:],
                                                                 op0=ALU.mult, op1=ALU.add),
                         reads=[b_mk, b_r], writes=[b_r])
                    P.op("dve", lambda e: e.tensor_scalar(out=r[:], in0=r[:], scalar1=-PI_SAFE, scalar2=PI_SAFE,
                                                          op0=ALU.max, op1=ALU.min),
                         reads=[b_r], writes=[b_r])

                for ch in range(S // CH):
                    t0 = ch * CH
                    P.dma(b_posi, posi[:], pos[0, t0:t0 + CH].partition_broadcast(128), writes=[b_posi])
                    P.op("dve", lambda e: e.tensor_copy(out=ang[:], in_=posi[:]), reads=[b_posi], writes=[b_ang])
                    P.op("dve", lambda e: e.tensor_scalar(out=ang[:], in0=ang[:], scalar1=ropec_sb[:, 0:1], scalar2=None,
                                                          op0=ALU.mult), reads=[b_ang, b_const], writes=[b_ang])
                    P.op("dve", lambda e: e.tensor_scalar(out=ki[:], in0=ang[:], scalar1=1.0 / TWO_PI, scalar2=None,
                                                          op0=ALU.mult), reads=[b_ang], writes=[b_ki])
                    P.op("dve", lambda e: e.tensor_copy(out=kf[:], in_=ki[:]), reads=[b_ki], writes=[b_kf])
                    P.op("dve", lambda e: e.scalar_tensor_tensor(out=ang[:], in0=kf[:], scalar=-C1, in1=ang[:],
                                                                 op0=ALU.mult, op1=ALU.add),
                         reads=[b_kf, b_ang], writes=[b_ang])
                    P.op("dve", lambda e: e.scalar_tensor_tensor(out=ang[:], in0=kf[:], scalar=-C2, in1=ang[:],
                                                                 op0=ALU.mult, op1=ALU.add),
                         reads=[b_kf, b_ang], writes=[b_ang])
                    wrap(ang, b_ang)
                    P.op("dve", lambda e: e.tensor_scalar(out=r2[:], in0=ang[:], scalar1=0.5 * math.pi, scalar2=None,
                                                          op0=ALU.add), reads=[b_ang], writes=[b_r2])
                    wrap(r2, b_r2)
                    P.op("act", lambda e: e.activation(out=sn[:], in_=ang[:], func=AF.Sin), reads=[b_ang], writes=[b_sn])
                    P.op("act", lambda e: e.activation(out=cs_[:], in_=r2[:], func=AF.Sin), reads=[b_r2], writes=[b_cs])
                    P.dma(b_sn, SIN[:, t0:t0 + CH], sn[:], reads=[b_sn])
                    P.dma(b_cs, COS[:, t0:t0 + CH], cs_[:], reads=[b_cs])
            P.barrier()
            P.flush()

        import os as _os
        _stop = _os.environ.get("KSTOP", "")
        ia = ib = 0
        for l, lt in enumerate(layer_types):
            if _stop == "setup":
                break
            h_in = x if l == 0 else hbuf
            h_out = out if l == depth - 1 else hbuf
            if lt == "a":
                j = ia
                ia += 1
                DQ = 96
                scale = 96.0 ** -0.5
            else:
                j = ib
                ib += 1
                DQ = 70
                scale = 0.125
            with ExitStack() as st:
                sb = lambda name, shape, dt: st.enter_context(nc.sbuf_tensor(U(name), shape, dt))
                ps = lambda name, shape, dt: st.enter_context(nc.psum_tensor(U(name), shape, dt))
                NIN = A_IN if lt == "a" else B_IN
                w_in_d = w_in_a if lt == "a" else w_in_b
                w_in = sb("w_in", [128, 8, NIN], BF16)
                b_wd = Buf("wd")
                b_wp = Buf("wp")
                b_wa = Buf("wa")
                b_w = b_wp
                NSTG = 4
                stage = [sb("stage%d" % i, [128, 2048], F32) for i in range(NSTG)]
                b_stage = [Buf("stage%d" % i) for i in range(NSTG)]
                stg = [0]

                def stage_load(src_ap, ncols):
                    i = stg[0] % NSTG
                    stg[0] += 1
                    P.dma(b_stage[i], stage[i][:, 0:ncols], src_ap, writes=[b_stage[i]])
                    return stage[i], b_stage[i]

                alt = [0]

                def scale_to(out_ap, in_ap, g_ap, b_in, neg=False):
                    eng = ("dve", "act", "pool")[alt[0] % 3] if not neg else ("dve", "pool")[alt[0] % 2]
                    alt[0] += 1
                    bw_ = {"dve": b_wd, "pool": b_wp, "act": b_wa}[eng]
                    if eng == "act":
                        P.op("act", lambda e: e.activation(out=out_ap, in_=in_ap, func=AF.Copy, scale=g_ap),
                             reads=[b_in, b_const], writes=[bw_])
                        return
                    if neg:
                        P.op(eng, lambda e: e.tensor_scalar(out=out_ap, in0=in_ap, scalar1=g_ap, scalar2=-1.0,
                                                            op0=ALU.mult, op1=ALU.mult), reads=[b_in, b_const], writes=[bw_])
                    else:
                        P.op(eng, lambda e: e.tensor_scalar(out=out_ap, in0=in_ap, scalar1=g_ap, scalar2=None,
                                                            op0=ALU.mult), reads=[b_in, b_const], writes=[bw_])

                try:
                    if lt == "a":
                        wkr = sb("wkr", [128, 8, 96], BF16)
                        wkr_rot = sb("wkr_rot", [128, 8, 96], BF16)
                        wq_nope = sb("wq_nope", [128, 3, 768], BF16)
                        wq_rope = sb("wq_rope", [128, 3, 384], BF16)
                        wq_rot = sb("wq_rot", [128, 3, 384], BF16)
                        wkv_k = sb("wkv_k", [128, 2, 768], BF16)
                        wkv_v = sb("wkv_v", [128, 2, 768], BF16)
                        P.op("pool", lambda e: e.memset(wkr[:, :, 0:64], 0.0), writes=[b_wp])
                        P.op("pool", lambda e: e.memset(wkr_rot[:, :, 0:64], 0.0), writes=[b_wp])
                        for c in range(8):
                            stt, bst = stage_load(w_in_d[j, c * 128:(c + 1) * 128, :], A_IN)
                            g = gpre[:, l, c:c + 1]
                            scale_to(w_in[:, c, :], stt[:, 0:A_IN], g, bst)
                            scale_to(wkr[:, c, 64:96], stt[:, 640:672], g, bst)
                            scale_to(wkr_rot[:, c, 64:80], stt[:, 656:672], g, bst, neg=True)
                            scale_to(wkr_rot[:, c, 80:96], stt[:, 640:656], g, bst)
                        for c in range(3):
                            stt, bst = stage_load(w_q_up_a[j, c * 128:(c + 1) * 128, :], 1152)
                            g = gq[:, j, c:c + 1]
                            sv = stt[:, 0:1152].rearrange("p (h e) -> p h e", e=96)
                            scale_to(wq_nope[:, c, :].rearrange("p (h e) -> p h e", e=64), sv[:, :, 0:64], g, bst)
                            scale_to(wq_rope[:, c, :].rearrange("p (h e) -> p h e", e=32), sv[:, :, 64:96], g, bst)
                            rv = wq_rot[:, c, :].rearrange("p (h e) -> p h e", e=32)
                            scale_to(rv[:, :, 0:16], sv[:, :, 80:96], g, bst, neg=True)
                            scale_to(rv[:, :, 16:32], sv[:, :, 64:80], g, bst)
                        for c in range(2):
                            stt, bst = stage_load(w_kv_up_a[j, c * 128:(c + 1) * 128, :], 1536)
                            g = gkv[:, j, c:c + 1]
                            sv = stt[:, 0:1536].rearrange("p (h e) -> p h e", e=128)
                            scale_to(wkv_k[:, c, :].rearrange("p (h e) -> p h e", e=64), sv[:, :, 0:64], g, bst)
                            scale_to(wkv_v[:, c, :].rearrange("p (h e) -> p h e", e=64), sv[:, :, 64:128], g, bst)
                    else:
                        HB = B_IN // 2
                        for c in range(8):
                            for hf in range(2):
                                stt, bst = stage_load(w_in_d[j, c * 128:(c + 1) * 128, hf * HB:(hf + 1) * HB], HB)
                                scale_to(w_in[:, c, hf * HB:(hf + 1) * HB], stt[:, 0:HB], gpre[:, l, c:c + 1], bst)
                    ck(1)
                    wm = sb("wm", [128, 8, 512], BF16)
                    for c in range(8):
                        stt, bst = stage_load(w_mem_kv[l, c * 128:(c + 1) * 128, :], 512)
                        scale_to(wm[:, c, :], stt[:, 0:512], gmem[:, c:c + 1], bst)

                    NPS = 6
                    pbank = [ps("pb%d" % i, [128, 512], F32) for i in range(NPS)]
                    b_pbank = [Buf("pb%d" % i) for i in range(NPS)]
                    tp = [ps("tp%d" % i, [128, 1024], BF16) for i in range(2)]
                    b_tp = [Buf("tp%d" % i) for i in range(2)]
                    pbi = [0]

                    def next_bank():
                        i = pbi[0] % NPS
                        pbi[0] += 1
                        return pbank[i], b_pbank[i]

                    evi = [0]

                    def evac_copy(out_ap, in_ap, b_in, b_out, eng=None):
                        if eng is None:
                            eng = "act" if evi[0] % 2 == 0 else "dve"
                            evi[0] += 1
                        if eng == "act":
                            P.op("act", lambda e: e.activation(out=out_ap, in_=in_ap, func=AF.Copy), reads=[b_in], writes=[b_out])
                        else:
                            P.op("dve", lambda e: e.tensor_copy(out=out_ap, in_=in_ap), reads=[b_in], writes=[b_out])

                    for hm in range(4):
                        pb, bpb = next_bank()
                        for c in range(8):
                            P.op("pe", lambda e, c=c, hm=hm, pb=pb: e.matmul(pb[0:64, 0:NMEM], lhsT=wm[:, c, hm * 64:(hm + 1) * 64],
                                                                            rhs=memnT[:, c, :], start=(c == 0), stop=(c == 7)),
                                 reads=[b_wd, b_wp, b_wa, b_memnT], writes=[bpb], sig=(c == 7))
                        evac_copy(km_sb[:, hm, :], pb[0:64, 0:NMEM], bpb, b_km)
                    for kt in range(2):
                        pb, bpb = next_bank()
                        for c in range(8):
                            P.op("pe", lambda e, c=c, kt=kt, pb=pb: e.matmul(pb[:, 0:256], lhsT=memnT[:, c, kt * 128:(kt + 1) * 128],
                                                                            rhs=wm[:, c, 256:512], start=(c == 0), stop=(c == 7)),
                                 reads=[b_wd, b_wp, b_wa, b_memnT], writes=[bpb], sig=(c == 7))
                        evac_copy(vm_sb[:, kt, :, 0:64], pb[:, 0:256].rearrange("p (h d) -> p h d", d=64), bpb, b_vm)

                    ck(2)
                    NXT = 4
                    xt = [sb("xt%d" % i, [128, D], F32) for i in range(NXT)]
                    b_xt = [Buf("xt%d" % i) for i in range(NXT)]
                    xs = [sb("xs%d" % i, [128, D], BF16) for i in range(2)]
                    b_xs = [Buf("xs%d" % i) for i in range(2)]
                    junk = sb("junk", [128, D], BF16)
                    b_junk = Buf("junk")
                    ss = [sb("ss%d" % i, [128, 1], F32) for i in range(4)]
                    b_ss = [Buf("ss%d" % i) for i in range(4)]
                    hnT = [sb("hnT%d" % i, [128, 8, 512], BF16) for i in range(2)]
                    b_hnT = [Buf("hnT%d" % i) for i in range(2)]
                    qn_sb = sb("qn_sb", [128, 6, 512], BF16)
                    b_qn = [Buf("qn%d" % i) for i in range(6)]
                    kn_sb = sb("kn_sb", [128, 6, 512], BF16)
                    b_kn = [Buf("kn%d" % i) for i in range(6)]
                    v_sb = sb("v_sb", [128, 12, 4, 65], BF16)
                    b_v = Buf("v_sb")
                    P.op("pool", lambda e: e.memset(v_sb[:], 1.0), writes=[b_v])
                    sg_sb = sb("sg_sb", [128, 8, 512], BF16)
                    b_sg = [Buf("sg%d" % i) for i in range(8)]
                    qm_sb = sb("qm_sb", [128, 2, 512], BF16)
                    b_qm = [Buf("qm%d" % i) for i in range(2)]
                    if lt == "a":
                        cs = [sb("cs%d" % i, [128, 2, 512], F32) for i in range(2)]
                        b_cs = [Buf("cs%d" % i) for i in range(2)]
                        cq_sb = sb("cq_sb", [128, 5, 512], BF16)
                        sq_sb = sb("sq_sb", [128, 5, 512], BF16)
                        cn_sb = sb("cn_sb", [128, 5, 512], BF16)
                        b_cq = [Buf("cq%d" % i) for i in range(5)]
                        b_sq = [Buf("sq%d" % i) for i in range(5)]
                        b_cn = [Buf("cn%d" % i) for i in range(5)]
                        rsb = [sb("rsb%d" % i, [128, 512], F32) for i in range(2)]
                        b_rsb = [Buf("rsb%d" % i) for i in range(2)]
                        t1 = [sb("t1_%d" % i, [128, 512], F32) for i in range(2)]
                        t2 = [sb("t2_%d" % i, [128, 512], F32) for i in range(2)]
                        b_t1 = [Buf("t1_%d" % i) for i in range(2)]
                        b_t2 = [Buf("t2_%d" % i) for i in range(2)]
                        qro_sb = sb("qro_sb", [128, 3, 512], BF16)
                        b_qro = [Buf("qro%d" % i) for i in range(3)]
                        kro_sb = sb("kro_sb", [128, 512], BF16)
                        b_kro = Buf("kro")
                        rpi = [0]
                    else:
                        f1 = sb("f1", [12, 512], F32)
                        f2 = sb("f2", [12, 512], F32)
                        b_f1 = Buf("f1")
                        b_f2 = Buf("f2")
                        ones12 = sb("ones12", [12, 512], F32)
                        P.op("pool", lambda e: e.memset(ones12[:], 1.0), writes=[b_w])
                        Fc = [sb("Fc%d" % i, [12, 512], F32) for i in range(2)]
                        b_Fc = [Buf("Fc%d" % i) for i in range(2)]
                        x0 = sb("x0", [12, 512], F32)
                        x1 = sb("x1", [12, 512], F32)
                        b_x0 = Buf("x0")
                        b_x1 = Buf("x1")
                        fq = sb("fq", [12, 6, 512], BF16)
                        fk = sb("fk", [12, 6, 512], BF16)
                        b_fq = Buf("fq")
                        b_fk = Buf("fk")
                        P.op("pool", lambda e: e.memset(fq[:], 1.0), writes=[b_fq])
                        P.op("pool", lambda e: e.memset(fk[:], 1.0), writes=[b_fk])

                    def load_tile(tt):
                        import os
                        for s in range(4):
                            i = (tt * 4 + s) % NXT
                            r0 = tt * 512 + s * 128
                            if os.environ.get("KADDR0"):
                                r0 = s * 128
                            P.dma(b_xt[i], xt[i][:], h_in[r0:r0 + 128, :], writes=[b_xt[i]] + ([b_hnT[0], b_tp[0], b_tp[1]] if (os.environ.get("KSER") and tt > 0) else []), tag="xt%d" % tt)
                        if lt == "a":
                            k = tt % 2
                            P.dma(b_cs[k], cs[k][:, 0, :], COS[:, tt * 512:(tt + 1) * 512], writes=[b_cs[k]], tag="cs%d" % tt)
                            P.dma(b_cs[k], cs[k][:, 1, :], SIN[:, tt * 512:(tt + 1) * 512], writes=[b_cs[k]], tag="cs%d" % tt)

                    def mm_group(out_ap, bpb, lhs_fn, rhs_fn, nchunk, reads):
                        for c in range(nchunk):
                            P.op("pe", lambda e, c=c: e.matmul(out_ap, lhsT=lhs_fn(c), rhs=rhs_fn(c), start=(c == 0),
                                                               stop=(c == nchunk - 1)),
                                 reads=reads, writes=[bpb], sig=(c == nchunk - 1))

                    def prep(tt):
                        hT = hnT[tt % 2]
                        bhT = b_hnT[tt % 2]
                        for s in range(4):
                            i = (tt * 4 + s) % NXT
                            P.op("act", lambda e, i=i, s=s: e.activation(out=junk[:], in_=xt[i][:], func=AF.Square,
                                                                         accum_out=ss[s][:, 0:1]),
                                 reads=[b_xt[i]], writes=[b_junk, b_ss[s]])
                            P.op("dve", lambda e, s=s: e.tensor_scalar(out=ss[s][:], in0=ss[s][:], scalar1=1.0 / D, scalar2=EPS,
                                                                       op0=ALU.mult, op1=ALU.add), reads=[b_ss[s]], writes=[b_ss[s]])
                            P.op("act", lambda e, s=s: e.activation(out=ss[s][:], in_=ss[s][:], func=AF.Sqrt),
                                 reads=[b_ss[s]], writes=[b_ss[s]])
                            P.op("dve", lambda e, s=s: e.reciprocal(out=ss[s][:], in_=ss[s][:]), reads=[b_ss[s]], writes=[b_ss[s]])
                            ck(2.6)
                            k = s % 2
                            P.op("dve", lambda e, i=i, s=s, k=k: e.tensor_scalar(out=xs[k][:], in0=xt[i][:], scalar1=ss[s][:, 0:1],
                                                                                 scalar2=None, op0=ALU.mult),
                                 reads=[b_xt[i], b_ss[s]], writes=[b_xs[k]])
                            for c in range(8):
                                P.op("pe", lambda e, c=c, k=k: e.transpose(out=tp[k][:, c * 128:(c + 1) * 128],
                                                                           in_=xs[k][:, c * 128:(c + 1) * 128], identity=ident[:]),
                                     reads=[b_xs[k], b_const], writes=[b_tp[k]], sig=(c == 7))
                            ck(2.8)
                            evac_copy(hT[:, :, s * 128:(s + 1) * 128], tp[k][:, :].rearrange("p (c t) -> p c t", c=8),
                                      b_tp[k], bhT, eng="dve")
                            ck(2.9 + 0.01 * s)
                        if tt + 1 < NT:
                            load_tile(tt + 1)

                    ck(2.2)
                    load_tile(0)
                    ck(2.4)
                    prep(0)
                    for tt in range(NT):
                        t0 = tt * 512
                        tsl = slice(t0, t0 + 512)
                        hT = hnT[tt % 2]
                        bhT = b_hnT[tt % 2]
                        ck(3)

                        def inproj(col0, M, evac):
                            pb, bpb = next_bank()
                            mm_group(pb[0:M, :], bpb, lambda c: w_in[:, c, col0:col0 + M], lambda c: hT[:, c, :], 8, [b_wd, b_wp, b_wa, bhT])
                            evac(pb, bpb)

                        def store_heads(sb_t, bufs, dram, row0, nrow, per):
                            for g in range(len(bufs)):
                                for e_ in range(per):
                                    hh = g * per + e_
                                    P.dma(bufs[g], dram[hh, row0:row0 + nrow, tsl], sb_t[e_ * nrow:(e_ + 1) * nrow, g, :],
                                          reads=[bufs[g]], tag="heads")

                        def gate_and_qm(col_qm, col_g):
                            for pr in range(2):
                                inproj(col_qm + pr * 128, 128,
                                       lambda pb, bpb, pr=pr: evac_copy(qm_sb[:, pr, :], pb[:, :], bpb, b_qm[pr]))
                                for e_ in range(2):
                                    P.dma(b_qm[pr], QM[2 * pr + e_, :, tsl], qm_sb[e_ * 64:(e_ + 1) * 64, pr, :], reads=[b_qm[pr]], tag="qm")
                            for g in range(8):
                                def ev(pb, bpb, g=g):
                                    P.op("act", lambda e: e.activation(out=sg_sb[:, g, :], in_=pb[:, :], func=AF.Silu),
                                         reads=[bpb], writes=[b_sg[g]])
                                inproj(col_g + g * 128, 128, ev)
                                for e_ in range(2):
                                    P.dma(b_sg[g], SG[2 * g + e_, :, tsl], sg_sb[e_ * 64:(e_ + 1) * 64, g, :], reads=[b_sg[g]], tag="sg")

                        if lt == "a":
                            csk = cs[tt % 2]
                            bcsk = b_cs[tt % 2]
                            for g in range(5):
                                def ev(pb, bpb, g=g):
                                    P.op("dve", lambda e: e.tensor_copy(out=cq_sb[:, g, :], in_=pb[:, :]), reads=[bpb], writes=[b_cq[g]])
                                    P.op("act", lambda e: e.activation(out=sq_sb[:, g, :], in_=cq_sb[:, g, :], func=AF.Square),
                                         reads=[b_cq[g]], writes=[b_sq[g]])
                                inproj(g * 128, 128, ev)
                            ck(4)
                            pbA, bA = next_bank()
                            mm_group(pbA[0:96, :], bA, lambda c: wkr[:, c, :], lambda c: hT[:, c, :], 8, [b_wd, b_wp, b_wa, bhT])
                            pbB, bB = next_bank()
                            mm_group(pbB[0:96, :], bB, lambda c: wkr_rot[:, c, :], lambda c: hT[:, c, :], 8, [b_wd, b_wp, b_wa, bhT])
                            r = rpi[0] % 2
                            rpi[0] += 1
                            P.op("dve", lambda e, r=r, pbA=pbA: e.tensor_tensor(out=t1[r][64:96, :], in0=pbA[64:96, :], in1=csk[64:96, 0, :],
                                                                                op=ALU.mult), reads=[bA, bcsk], writes=[b_t1[r]])
                            P.op("dve", lambda e, r=r, pbB=pbB: e.tensor_tensor(out=t2[r][64:96, :], in0=pbB[64:96, :], in1=csk[64:96, 1, :],
                                                                                op=ALU.mult), reads=[bB, bcsk], writes=[b_t2[r]])
                            P.op("pool", lambda e, r=r: e.tensor_tensor(out=kro_sb[64:96, :], in0=t1[r][64:96, :], in1=t2[r][64:96, :],
                                                                        op=ALU.add), reads=[b_t1[r], b_t2[r]], writes=[b_kro])
                            for hh in range(12):
                                P.dma(b_kro, KT[hh, 64:96, tsl], kro_sb[64:96, :], reads=[b_kro], tag="kro")
                            ck(5)
                            for (g0, ng, dim) in ((0, 3, 384.0), (3, 2, 256.0)):
                                pb, bpb = next_bank()
                                mm_group(pb[:, :], bpb, lambda c: ones_bf[:, :], lambda c, g0=g0: sq_sb[:, g0 + c, :], ng,
                                         [b_const] + b_sq[g0:g0 + ng])
                                k = 0 if g0 == 0 else 1
                                P.op("dve", lambda e, k=k, pb=pb, dim=dim: e.tensor_scalar(out=rsb[k][:], in0=pb[:, :], scalar1=1.0 / dim,
                                                                                         scalar2=EPS, op0=ALU.mult, op1=ALU.add),
                                     reads=[bpb], writes=[b_rsb[k]])
                                P.op("act", lambda e, k=k: e.activation(out=rsb[k][:], in_=rsb[k][:], func=AF.Sqrt),
                                     reads=[b_rsb[k]], writes=[b_rsb[k]])
                                P.op("dve", lambda e, k=k: e.reciprocal(out=rsb[k][:], in_=rsb[k][:]), reads=[b_rsb[k]], writes=[b_rsb[k]])
                                for g in range(g0, g0 + ng):
                                    eng = "pool" if g % 2 == 0 else "dve"
                                    P.op(eng, lambda e, g=g, k=k: e.tensor_tensor(out=cn_sb[:, g, :], in0=cq_sb[:, g, :], in1=rsb[k][:],
                                                                                  op=ALU.mult),
                                         reads=[b_cq[g], b_rsb[k]], writes=[b_cn[g]])
                            ck(6)
                            for pr in range(6):
                                pb, bpb = next_bank()
                                mm_group(pb[:, :], bpb, lambda c, pr=pr: wq_nope[:, c, pr * 128:(pr + 1) * 128],
                                         lambda c: cn_sb[:, c, :], 3, [b_wd, b_wp, b_wa] + b_cn[0:3])
                                evac_copy(qn_sb[:, pr, :], pb[:, :], bpb, b_qn[pr])
                            store_heads(qn_sb, b_qn, QT, 0, 64, 2)
                            ck(7)
                            for g in range(3):
                                pbA, bA = next_bank()
                                mm_group(pbA[:, :], bA, lambda c, g=g: wq_rope[:, c, g * 128:(g + 1) * 128],
                                         lambda c: cn_sb[:, c, :], 3, [b_wd, b_wp, b_wa] + b_cn[0:3])
                                pbB, bB = next_bank()
                                mm_group(pbB[:, :], bB, lambda c, g=g: wq_rot[:, c, g * 128:(g + 1) * 128],
                                         lambda c: cn_sb[:, c, :], 3, [b_wd, b_wp, b_wa] + b_cn[0:3])
                                r = rpi[0] % 2
                                rpi[0] += 1
                                P.op("dve", lambda e, r=r, pbA=pbA: e.tensor_tensor(out=t1[r][:], in0=pbA[:, :], in1=csk[:, 0, :], op=ALU.mult),
                                     reads=[bA, bcsk], writes=[b_t1[r]])
                                P.op("dve", lambda e, r=r, pbB=pbB: e.tensor_tensor(out=t2[r][:], in0=pbB[:, :], in1=csk[:, 1, :], op=ALU.mult),
                                     reads=[bB, bcsk], writes=[b_t2[r]])
                                P.op("pool", lambda e, r=r, g=g: e.tensor_tensor(out=qro_sb[:, g, :], in0=t1[r][:], in1=t2[r][:], op=ALU.add),
                                     reads=[b_t1[r], b_t2[r]], writes=[b_qro[g]])
                            store_heads(qro_sb, b_qro, QT, 64, 32, 4)
                            ck(8)
                            for pr in range(6):
                                pb, bpb = next_bank()
                                mm_group(pb[:, :], bpb, lambda c, pr=pr: wkv_k[:, c, pr * 128:(pr + 1) * 128],
                                         lambda c: cn_sb[:, 3 + c, :], 2, [b_wd, b_wp, b_wa] + b_cn[3:5])
                                evac_copy(kn_sb[:, pr, :], pb[:, :], bpb, b_kn[pr])
                            store_heads(kn_sb, b_kn, KT, 0, 64, 2)
                            ck(9)
                            for s in range(4):
                                for hf in range(2):
                                    pb, bpb = next_bank()
                                    mm_group(pb[:, 0:384], bpb, lambda c, s=s: cn_sb[:, 3 + c, s * 128:(s + 1) * 128],
                                             lambda c, hf=hf: wkv_v[:, c, hf * 384:(hf + 1) * 384], 2, [b_wd, b_wp, b_wa] + b_cn[3:5])
                                    evac_copy(v_sb[:, hf * 6:(hf + 1) * 6, s, 0:64],
                                              pb[:, 0:384].rearrange("p (h d) -> p h d", d=64), bpb, b_v)
                            P.dma(b_v, VS[:, :, tt * 4:(tt + 1) * 4, :].rearrange("h p k d -> p h k d"), v_sb[:], reads=[b_v], tag="vs")
                            ck(10)
                            if tt + 1 < NT:
                                prep(tt + 1)
                            gate_and_qm(672, 928)
                        else:
                            for pr in range(6):
                                inproj(pr * 128, 128, lambda pb, bpb, pr=pr: evac_copy(qn_sb[:, pr, :], pb[:, :], bpb, b_qn[pr]))
                            store_heads(qn_sb, b_qn, QT, 0, 64, 2)
                            ck(4)
                            for pr in range(6):
                                inproj(768 + pr * 128, 128, lambda pb, bpb, pr=pr: evac_copy(kn_sb[:, pr, :], pb[:, :], bpb, b_kn[pr]))
                            store_heads(kn_sb, b_kn, KT, 0, 64, 2)
                            ck(5)
                            for s in range(4):
                                for hf in range(2):
                                    pb, bpb = next_bank()
                                    mm_group(pb[:, 0:384], bpb, lambda c, s=s: hT[:, c, s * 128:(s + 1) * 128],
                                             lambda c, hf=hf: w_in[:, c, 1536 + hf * 384:1536 + (hf + 1) * 384], 8, [b_wd, b_wp, b_wa, bhT])
                                    evac_copy(v_sb[:, hf * 6:(hf + 1) * 6, s, 0:64],
                                              pb[:, 0:384].rearrange("p (h d) -> p h d", d=64), bpb, b_v)
                            P.dma(b_v, VS[:, :, tt * 4:(tt + 1) * 4, :].rearrange("h p k d -> p h k d"), v_sb[:], reads=[b_v], tag="vs")
                            ck(6)
                            pb, bpb = next_bank()
                            mm_group(pb[0:12, :], bpb, lambda c: w_in[:, c, 2304:2316], lambda c: hT[:, c, :], 8, [b_wd, b_wp, b_wa, bhT])
                            P.op("act", lambda e, pb=pb: e.activation(out=f1[:], in_=pb[0:12, :], func=AF.Exp, scale=-1.0,
                                                                      bias=negbf[:, j:j + 1]), reads=[bpb, b_const], writes=[b_f1])
                            P.op("dve", lambda e: e.tensor_scalar(out=f1[:], in0=f1[:], scalar1=1.0, scalar2=None, op0=ALU.add),
                                 reads=[b_f1], writes=[b_f1])
                            P.op("act", lambda e: e.activation(out=f2[:], in_=f1[:], func=AF.Ln), reads=[b_f1], writes=[b_f2])
                            cur = Fc[tt % 2]
                            bcur = b_Fc[tt % 2]
                            prv = Fc[(tt + 1) % 2]
                            bprv = b_Fc[(tt + 1) % 2]
                            init = 0.0 if tt == 0 else prv[:, 511:512]
                            P.op("dve", lambda e, cur=cur, init=init: e.tensor_tensor_scan(out=cur[:], data0=ones12[:], data1=f2[:],
                                                                                           initial=init, op0=ALU.mult, op1=ALU.subtract),
                                 reads=[b_f2, b_wp, bprv], writes=[bcur])
                            inv_s = 1.0 / scale
                            P.op("dve", lambda e, cur=cur: e.tensor_scalar(out=x0[:], in0=cur[:], scalar1=inv_s, scalar2=None, op0=ALU.mult),
                                 reads=[bcur], writes=[b_x0])
                            P.op("dve", lambda e: e.tensor_copy(out=fq[:, 0, :], in_=x0[:]), reads=[b_x0], writes=[b_fq])
                            P.op("dve", lambda e: e.tensor_tensor(out=x1[:], in0=x0[:], in1=fq[:, 0, :], op=ALU.subtract),
                                 reads=[b_x0, b_fq], writes=[b_x1])
                            P.op("dve", lambda e: e.tensor_copy(out=fq[:, 1, :], in_=x1[:]), reads=[b_x1], writes=[b_fq])
                            P.op("dve", lambda e: e.tensor_tensor(out=x0[:], in0=x1[:], in1=fq[:, 1, :], op=ALU.subtract),
                                 reads=[b_x1, b_fq], writes=[b_x0])
                            P.op("dve", lambda e: e.tensor_copy(out=fq[:, 2, :], in_=x0[:]), reads=[b_x0], writes=[b_fq])
                            P.op("dve", lambda e: e.tensor_scalar(out=fk[:, 3:6, :], in0=fq[:, 0:3, :], scalar1=-1.0, scalar2=None,
                                                                  op0=ALU.mult), reads=[b_fq], writes=[b_fk])
                            ck(7)
                            P.dma(b_fq, QT[:, 64:70, tsl], fq[:], reads=[b_fq])
                            P.dma(b_fk, KT[:, 64:70, tsl], fk[:], reads=[b_fk])
                            ck(8)
                            if tt + 1 < NT:
                                prep(tt + 1)
                            gate_and_qm(2316, 2572)
                except StopBuild:
                    pass
                P.barrier()
                P.flush()

            if _stop == "A":
                break
            with ExitStack() as st:
                sb = lambda name, shape, dt: st.enter_context(nc.sbuf_tensor(U(name), shape, dt))
                ps = lambda name, shape, dt: st.enter_context(nc.psum_tensor(U(name), shape, dt))
                qT = [sb("qT%d" % i, [96, S], BF16) for i in range(2)]
                kT = [sb("kT%d" % i, [96, S], BF16) for i in range(2)]
                vv = [sb("vv%d" % i, [128, NKT, 65], BF16) for i in range(2)]
                sg = [sb("sg%d" % i, [64, S], BF16) for i in range(2)]
                b_qT = [Buf("qT%d" % i) for i in range(2)]
                b_kT = [Buf("kT%d" % i) for i in range(2)]
                b_vv = [Buf("vv%d" % i) for i in range(2)]
                b_sgh = [Buf("sgh%d" % i) for i in range(2)]
                NPT = 6
                pT = [sb("pT%d" % i, [128, 1024], BF16) for i in range(NPT)]
                b_pT = [Buf("pT%d" % i) for i in range(NPT)]
                rec = [sb("rec%d" % i, [128, 512], F32) for i in range(2)]
                b_rec = [Buf("rec%d" % i) for i in range(2)]
                rsum = [sb("rsum%d" % i, [128, 512], F32) for i in range(2)]
                b_rsum = [Buf("rsum%d" % i) for i in range(2)]
                rech = [sb("rech%d" % i, [128, 512], BF16) for i in range(2)]
                recl = [sb("recl%d" % i, [128, 512], BF16) for i in range(2)]
                b_rech = [Buf("rech%d" % i) for i in range(2)]
                tmpo = [sb("tmpo%d" % i, [64, 512], F32) for i in range(2)]
                b_tmpo = [Buf("tmpo%d" % i) for i in range(2)]
                ysb = [sb("ysb%d" % i, [64, 512], BF16) for i in range(3)]
                b_ysb = [Buf("ysb%d" % i) for i in range(3)]
                NSP = 3
                NOP = 1
                spp = [ps("spp%d" % i, [128, 1024], F32) for i in range(NSP)]
                b_spp = [Buf("spp%d" % i) for i in range(NSP)]
                ops_ = [ps("ops%d" % i, [128, 512], F32) for i in range(NOP)]
                b_ops = [Buf("ops%d" % i) for i in range(NOP)]
                bcp = ps("bcp", [128, 512], F32)
                b_bcp = Buf("bcp")

                def load_head(hd):
                    k = hd % 2
                    if hd < 12:
                        P.dma(b_qT[k], qT[k][0:DQ, :], QT[hd, 0:DQ, :], writes=[b_qT[k]])
                        P.dma(b_kT[k], kT[k][0:DQ, :], KT[hd, 0:DQ, :], writes=[b_kT[k]])
                        P.dma(b_vv[k], vv[k][:], VS[hd, :, :, :], writes=[b_vv[k]])
                    else:
                        P.dma(b_qT[k], qT[k][0:64, :], QM[hd - 12, :, :], writes=[b_qT[k]])
                    P.dma(b_sgh[k], sg[k][:], SG[hd, :, :], writes=[b_sgh[k]])

                PE_EMBED = True
                cnt = Ctx()
                cnt.g = 0
                cnt.blk = 0
                pending = []

                load_head(0)
                for hd in range(16):
                    k = hd % 2
                    causal = hd < 12
                    if causal:
                        dq = DQ
                        sc = scale
                        kT_ap = lambda kb, k=k, dq=dq: kT[k][0:dq, kb * 128:(kb + 1) * 128]
                        v_ap = lambda kb, k=k: vv[k][:, kb, :]
                        kv_reads = [b_kT[k], b_vv[k]]
                    else:
                        dq = 64
                        sc = 0.125
                        hm = hd - 12
                        kT_ap = lambda kb, hm=hm: km_sb[:, hm, kb * 128:(kb + 1) * 128]
                        v_ap = lambda kb, hm=hm: vm_sb[:, kb, hm, :]
                        kv_reads = [b_km, b_vm]
                    for qi in range(NT):
                        nkb = 4 * (qi + 1) if causal else 2
                        ob = cnt.blk % 2
                        cnt.blk += 1
                        o_ps = ops_[ob % NOP]
                        b_o = b_ops[ob % NOP]
                        q0 = qi * 512
                        ngrp = nkb // 2
                        grp_info = []

                        def emit_qk(g):
                            gi = cnt.g
                            cnt.g += 1
                            sp_t = spp[gi % NSP]
                            b_sp = b_spp[gi % NSP]
                            offs = []
                            for e_ in range(2):
                                kb = 2 * g + e_
                                o = (kb - 4 * qi) * 128 if (causal and kb >= 4 * qi) else 0
                                diag = causal and kb >= 4 * qi
                                lo = e_ * 512 + o
                                hi = (e_ + 1) * 512
                                P.op("pe", lambda e, kb=kb, lo=lo, hi=hi, o=o, diag=diag, sp_t=sp_t: e.matmul(
                                    sp_t[:, lo:hi], lhsT=kT_ap(kb), rhs=qT[k][0:dq, q0 + o:q0 + 512], start=True, stop=(not diag)),
                                     reads=[b_qT[k]] + kv_reads, writes=[b_sp], sig=(e_ == 1 and not diag), embed=PE_EMBED)
                                if diag:
                                    P.op("pe", lambda e, lo=lo, sp_t=sp_t: e.matmul(sp_t[:, lo:lo + 128], lhsT=ident[:], rhs=maskb[:],
                                                                                  start=False, stop=True),
                                         reads=[b_const], writes=[b_sp], sig=(e_ == 1))
                                offs.append((kb, o, lo, hi))
                            grp_info.append((gi, sp_t, b_sp, offs))

                        def emit_exp_pv(g):
                            gi, sp_t, b_sp, offs = grp_info[g]
                            pt = pT[gi % NPT]
                            b_pt = b_pT[gi % NPT]
                            if offs[0][1] == 0 and offs[1][1] == 0:
                                P.op("act", lambda e: e.activation(out=pt[:, :], in_=sp_t[:, :], func=AF.Exp, scale=sc),
                                     reads=[b_sp], writes=[b_pt])
                            else:
                                for (kb, o, lo, hi) in offs:
                                    P.op("act", lambda e, lo=lo, hi=hi: e.activation(out=pt[:, lo:hi], in_=sp_t[:, lo:hi],
                                                                                     func=AF.Exp, scale=sc),
                                         reads=[b_sp], writes=[b_pt])
                            for (kb, o, lo, hi) in offs:
                                last = (kb == nkb - 1)
                                P.op("pe", lambda e, kb=kb, o=o, lo=lo, hi=hi, last=last: e.matmul(
                                    o_ps[0:65, o:512], lhsT=v_ap(kb), rhs=pt[:, lo:hi], start=(kb == 0), stop=last),
                                     reads=[b_pt] + kv_reads, writes=[b_o], sig=last, embed=PE_EMBED)

                        def finalize(hd=hd, qi=qi, ob=ob, o_ps=o_ps, b_o=b_o, k=k):
                            rc = rec[ob]
                            b_rc = b_rec[ob]
                            tm = tmpo[ob]
                            b_tm = b_tmpo[ob]
                            rs_ = rsum[ob]
                            b_rs = b_rsum[ob]
                            P.op("dve", lambda e: e.tensor_tensor(out=tm[:], in0=o_ps[0:64, :], in1=sg[k][:, qi * 512:(qi + 1) * 512],
                                                                  op=ALU.mult), reads=[b_o, b_sgh[k]], writes=[b_tm])
                            P.op("dve", lambda e: e.tensor_copy(out=rs_[64:65, :], in_=o_ps[64:65, :]), reads=[b_o], writes=[b_rs])
                            P.op("dve", lambda e: e.reciprocal(out=rc[64:65, :], in_=rs_[64:65, :]), reads=[b_rs], writes=[b_rc])
                            rh = rech[ob]
                            rl = recl[ob]
                            b_rh = b_rech[ob]
                            P.op("dve", lambda e: e.tensor_copy(out=rh[64:65, :], in_=rc[64:65, :]), reads=[b_rc], writes=[b_rh])
                            P.op("dve", lambda e: e.tensor_tensor(out=rl[64:65, :], in0=rc[64:65, :], in1=rh[64:65, :], op=ALU.subtract),
                                 reads=[b_rc, b_rh], writes=[b_rh])
                            P.op("pe", lambda e: e.matmul(bcp[0:64, :], lhsT=ones_bf[64:65, 0:64], rhs=rh[64:65, :], start=True, stop=False),
                                 reads=[b_const, b_rh], writes=[b_bcp], sig=False)
                            P.op("pe", lambda e: e.matmul(bcp[0:64, :], lhsT=ones_bf[64:65, 0:64], rhs=rl[64:65, :], start=False, stop=True),
                                 reads=[b_const, b_rh], writes=[b_bcp])
                            yi = (hd * NT + qi) % 3
                            P.op("dve", lambda e: e.tensor_tensor(out=ysb[yi][:], in0=tm[:], in1=bcp[0:64, :], op=ALU.mult),
                                 reads=[b_tm, b_bcp], writes=[b_ysb[yi]])
                            P.dma(b_ysb[yi], YT[hd * 64:(hd + 1) * 64, qi * 512:(qi + 1) * 512], ysb[yi][:], reads=[b_ysb[yi]])

                        for g_ in range(min(NSP, ngrp)):
                            emit_qk(g_)
                        while pending:
                            pending.pop(0)()
                        if qi == 0 and hd + 1 < 16:
                            load_head(hd + 1)
                        for g in range(ngrp):
                            emit_exp_pv(g)
                            if g + NSP < ngrp:
                                emit_qk(g + NSP)
                        pending.append(finalize)
                while pending:
                    pending.pop(0)()
                P.barrier()
                P.flush()

            if _stop == "B":
                break
            with ExitStack() as st:
                sb = lambda name, shape, dt: st.enter_context(nc.sbuf_tensor(U(name), shape, dt))
                ps = lambda name, shape, dt: st.enter_context(nc.psum_tensor(U(name), shape, dt))
                wo = sb("wo", [128, 8, D], BF16)
                b_wo = Buf("wo")
                stage = [sb("stagec%d" % i, [128, D], F32) for i in range(2)]
                b_stage = [Buf("stagec%d" % i) for i in range(2)]
                for c in range(8):
                    i = c % 2
                    P.dma(b_stage[i], stage[i][:], w_out[l, c * 128:(c + 1) * 128, :], writes=[b_stage[i]])
                    eng = "dve" if c % 2 == 0 else "pool"
                    P.op(eng, lambda e, c=c, i=i: e.tensor_copy(out=wo[:, c, :], in_=stage[i][:]), reads=[b_stage[i]], writes=[b_wo])
                grow = sb("grow", [1, D], F32)
                b_grow = Buf("grow")
                gpost = sb("gpost", [128, D], F32)
                b_gpost = Buf("gpost")
                P.dma(b_grow, grow[:], norm_post[l:l + 1, :], writes=[b_grow])
                pso = [ps("pso%d" % i, [128, 1024], F32) for i in range(3)]
                b_pso = [Buf("pso%d" % i) for i in range(3)]
                for hf in range(2):
                    P.op("pe", lambda e, hf=hf: e.matmul(pso[0][:, hf * 512:(hf + 1) * 512], lhsT=ones_f[0:1, :],
                                                         rhs=grow[0:1, hf * 512:(hf + 1) * 512], start=True, stop=True),
                         reads=[b_const, b_grow], writes=[b_pso[0]], sig=(hf == 1))
                P.op("dve", lambda e: e.tensor_copy(out=gpost[:], in_=pso[0][:, :]), reads=[b_pso[0]], writes=[b_gpost])
                yT = [sb("yT%d" % i, [128, 8, 512], BF16) for i in range(2)]
                b_yT = [Buf("yT%d" % i) for i in range(2)]
                NXT = 6
                xt = [sb("xtc%d" % i, [128, D], F32) for i in range(NXT)]
                b_xt = [Buf("xtc%d" % i) for i in range(NXT)]
                tsb = [sb("tsb%d" % i, [128, D], F32) for i in range(2)]
                b_tsb = [Buf("tsb%d" % i) for i in range(2)]
                usb = [sb("usb%d" % i, [128, D], F32) for i in range(2)]
                b_usb = [Buf("usb%d" % i) for i in range(2)]
                ho = [sb("ho%d" % i, [128, D], F32) for i in range(3)]
                b_ho = [Buf("ho%d" % i) for i in range(3)]
                junk = sb("junkc", [128, D], BF16)
                b_junk = Buf("junkc")
                ssc = [sb("ssc%d" % i, [128, 1], F32) for i in range(4)]
                b_ssc = [Buf("ssc%d" % i) for i in range(4)]

                def load_c(tt):
                    k = tt % 2
                    P.dma(b_yT[k], yT[k][:], YT[:, tt * 512:(tt + 1) * 512].rearrange("(c p) t -> p c t", p=128), writes=[b_yT[k]])

                def load_x(it):
                    i = it % NXT
                    P.dma(b_xt[i], xt[i][:], h_in[it * 128:(it + 1) * 128, :], writes=[b_xt[i]])

                load_c(0)
                for it in range(min(4, NKT)):
                    load_x(it)
                for tt in range(NT):
                    k = tt % 2
                    if tt + 1 < NT:
                        load_c(tt + 1)
                    for s in range(4):
                        it = tt * 4 + s
                        if it + 4 < NKT:
                            load_x(it + 4)
                        pi = it % 3
                        po = pso[pi]
                        b_po = b_pso[pi]
                        for hf in range(2):
                            for c in range(8):
                                P.op("pe", lambda e, c=c, hf=hf, s=s, k=k, po=po: e.matmul(
                                    po[:, hf * 512:(hf + 1) * 512], lhsT=yT[k][:, c, s * 128:(s + 1) * 128],
                                    rhs=wo[:, c, hf * 512:(hf + 1) * 512], start=(c == 0), stop=(c == 7)),
                                     reads=[b_yT[k], b_wo], writes=[b_po], sig=(c == 7 and hf == 1))
                        sc_ = ssc[it % 4]
                        b_sc = b_ssc[it % 4]
                        ti = it % 2
                        P.op("act", lambda e, po=po, sc_=sc_: e.activation(out=junk[:], in_=po[:, :], func=AF.Square, accum_out=sc_[:, 0:1]),
                             reads=[b_po], writes=[b_junk, b_sc, b_tsb[ti]])
                        P.op("dve", lambda e, po=po, ti=ti: e.tensor_tensor(out=tsb[ti][:], in0=po[:, :], in1=gpost[:], op=ALU.mult),
                             reads=[b_po, b_gpost], writes=[b_tsb[ti]])
                        P.op("dve", lambda e, sc_=sc_: e.tensor_scalar(out=sc_[:], in0=sc_[:], scalar1=1.0 / D, scalar2=EPS,
                                                                       op0=ALU.mult, op1=ALU.add), reads=[b_sc], writes=[b_sc])
                        P.op("act", lambda e, sc_=sc_: e.activation(out=sc_[:], in_=sc_[:], func=AF.Sqrt), reads=[b_sc], writes=[b_sc])
                        P.op("dve", lambda e, sc_=sc_: e.reciprocal(out=sc_[:], in_=sc_[:]), reads=[b_sc], writes=[b_sc])
                        xi = it % NXT
                        hi_ = it % 3
                        P.op("act", lambda e, ti=ti, sc_=sc_: e.activation(out=usb[ti][:], in_=tsb[ti][:], func=AF.Copy, scale=sc_[:, 0:1]),
                             reads=[b_tsb[ti], b_sc], writes=[b_usb[ti]])
                        P.op("pool", lambda e, ti=ti, xi=xi, hi_=hi_: e.tensor_tensor(out=ho[hi_][:], in0=usb[ti][:], in1=xt[xi][:], op=ALU.add),
                             reads=[b_usb[ti], b_xt[xi]], writes=[b_ho[hi_]])
                        P.dma(b_ho[hi_], h_out[it * 128:(it + 1) * 128, :], ho[hi_][:], reads=[b_ho[hi_]])
                P.barrier()
                P.flush()
        P_ninst = P.ninst
    nc._ninst = P_ninst
    return nc


LAYER_TYPES = ["a", "b", "a", "b"]
_NC_CACHE = {}


def rope_const():
    half = 16
    inv = (10000.0 ** (-np.arange(half, dtype=np.float32) / half)).astype(np.float32)
    return np.tile(inv, 8).reshape(128, 1).astype(np.float32)


def make_in_maps(inputs, B):
    c = lambda a: np.ascontiguousarray(a)
    shared = {
        "norm_pre": c(inputs["norm_pre"]), "norm_post": c(inputs["norm_post"]),
        "mem_norm": c(inputs["mem_norm"]).reshape(1, D),
        "w_mem_kv": c(inputs["w_mem_kv"]), "w_out": c(inputs["w_out"]),
        "w_in_a": c(inputs["w_in_a"]), "q_norm_a": c(inputs["q_norm_a"]), "kv_norm_a": c(inputs["kv_norm_a"]),
        "w_q_up_a": c(inputs["w_q_up_a"]), "w_kv_up_a": c(inputs["w_kv_up_a"]),
        "w_in_b": c(inputs["w_in_b"]), "b_f": c(inputs["b_f"]),
        "ropec": rope_const(),
    }
    maps = []
    for b in range(B):
        m = dict(shared)
        m["x"] = c(inputs["x"][b])
        m["mem"] = c(inputs["mem"][b])
        m["pos"] = c(inputs["positions"][b]).reshape(1, -1).astype(np.int32)
        maps.append(m)
    return maps


def kernel(**inputs):
    inputs = {k: np.asarray(v) for k, v in inputs.items()}
    B, S, _ = inputs["x"].shape
    key = (S, tuple(LAYER_TYPES))
    if key not in _NC_CACHE:
        _NC_CACHE[key] = build(S, LAYER_TYPES)
    nc = _NC_CACHE[key]
    maps = make_in_maps(inputs, B)
    res = run_bass_kernel_spmd(nc, maps, core_ids=list(range(B)))
    return np.stack([np.asarray(r["out"]) for r in res.results], axis=0).astype(np.float32)
```

```python
import math
import os
from contextlib import ExitStack

import numpy as np
import concourse.bass as bass
import concourse.mybir as mybir
from concourse.bass_utils import run_bass_kernel_spmd

F32 = mybir.dt.float32
BF16 = mybir.dt.bfloat16
I32 = mybir.dt.int32
AF = mybir.ActivationFunctionType
ALU = mybir.AluOpType

D = 1024
NMEM = 256
EPS = 1e-6
A_IN = 1952
B_IN = 3596
NEG = -30000.0
TWO_PI = 2.0 * math.pi

ENGS = ("pe", "act", "dve", "pool", "sp")


class _Rec:
    def __init__(self):
        self.call = None

    def __getattr__(self, name):
        def f(*args, **kwargs):
            self.call = (name, args, kwargs)
            return self
        return f


class DSem:
    __slots__ = ("sem", "total", "name")

    def __init__(self, sem, name):
        self.sem = sem
        self.total = 0
        self.name = name


class Buf:
    __slots__ = ("name", "w", "r", "ds")

    def __init__(self, name):
        self.name = name
        self.w = None
        self.r = []
        self.ds = None


class Prog:
    def __init__(self, nc, stack, n_dsem=72):
        self.nc = nc
        self.q = {e: [] for e in ENGS}
        self.sem = {e: stack.enter_context(nc.semaphore("s_" + e)) for e in ENGS}
        self.cnt = {e: 0 for e in ENGS}
        self.seen = {e: {} for e in ENGS}
        self.free_ds = [DSem(stack.enter_context(nc.semaphore("dq%d" % i)), "dq%d" % i) for i in range(n_dsem)]
        self.all_ds = list(self.free_ds)
        self.phase_bufs = []
        self.ninst = 0
        self.embed_default = True
        self.bar_sem = stack.enter_context(nc.semaphore("s_bar"))
        self.bar_total = 0
        self.bar_src = nc.dram_tensor("bar_src", [1, 16], F32).ap()[:, :]
        self.bar_dst = stack.enter_context(nc.sbuf_tensor("bar_dst", [1, 16], F32))[:]

    def _deps(self, eng, reads, writes):
        deps = {}

        def add(tok):
            if tok is None:
                return
            key, sem, val, ds = tok
            if ds is not None:
                val = ds.total
            if key == eng and eng == "pe":
                return
            cur = deps.get(key)
            if cur is None or cur[1] < val:
                deps[key] = (sem, val)

        for b in reads:
            add(b.w)
        for b in writes:
            add(b.w)
            for t in b.r:
                add(t)
        out = []
        seen = self.seen[eng]
        for key, (sem, val) in deps.items():
            if seen.get(key, 0) >= val:
                continue
            seen[key] = val
            out.append((sem, val))
        return out

    def _mark(self, tok, reads, writes):
        for b in writes:
            b.w = tok
            b.r = []
        for b in reads:
            b.r.append(tok)
            if len(b.r) > 16:
                last = {}
                for t in b.r:
                    k = t[0]
                    if k not in last or last[k][2] < t[2]:
                        last[k] = t
                b.r = list(last.values())

    def op(self, eng, fn, reads=(), writes=(), sig=True, embed=None):
        waits = self._deps(eng, reads, writes)
        if sig:
            self.cnt[eng] += 1
            val = self.cnt[eng]
        else:
            val = self.cnt[eng] + 1
        sem = self.sem[eng]
        tok = (eng, sem, val, None)
        self.ninst += 1 + len(waits)
        rec = _Rec()
        fn(rec)
        call = rec.call
        if embed is None:
            embed = (eng in ("act", "dve", "pool")) and call[2].get("accum_out") is None and self.embed_default
        emb = None
        if embed and waits:
            csems = [self.sem[e_] for e_ in ("pe", "act", "dve", "pool")]
            for wi in range(len(waits) - 1, -1, -1):
                if any(waits[wi][0] is cs_ for cs_ in csems):
                    emb = waits[wi]
                    waits = waits[:wi] + waits[wi + 1:]
                    break

        def emit(e, waits=waits, call=call, sig=sig, sem=sem, emb=emb):
            for (s, v) in waits:
                e.wait_ge(s, v)
            ins = getattr(e, call[0])(*call[1], **call[2])
            if emb is not None:
                ins = ins._wait_ge(emb[0], emb[1])
            if sig:
                ins.then_inc(sem, 1)

        self.q[eng].append(emit)
        self._mark(tok, reads, writes)
        return tok

    def dma(self, owner, out, in_, reads=(), writes=(), eng="sp", tag=None, **kw):
        import os
        if tag is not None and tag in os.environ.get("KDMA_POOL", "").split(","):
            eng = "pool"
        if tag is not None and tag in os.environ.get("KDMA_ACT", "").split(","):
            eng = "act"
        if tag is not None and tag in os.environ.get("KDMA_SKIP", "").split(","):
            return None
        waits = self._deps(eng, reads, writes)
        if owner.ds is None:
            owner.ds = self.free_ds.pop()
            self.phase_bufs.append(owner)
        ds = owner.ds
        ds.total += 16
        tok = (ds.name, ds.sem, ds.total, ds)
        self.ninst += 1 + len(waits)

        def emit(e, waits=waits, sem=ds.sem, out=out, in_=in_, kw=kw):
            for (s, v) in waits:
                e.wait_ge(s, v)
            e.dma_start(out=out, in_=in_, **kw).then_inc(sem, 16)

        self.q[eng].append(emit)
        self._mark(tok, reads, writes)
        return tok

    def barrier(self):
        finals = [(d.sem, d.total, d.name) for d in self.all_ds if d.total > 0]
        finals += [(self.sem[e], self.cnt[e], e) for e in ("pe", "act", "dve", "pool") if self.cnt[e] > 0]
        self.bar_total += 16
        bv = self.bar_total
        bsem = self.bar_sem
        bsrc, bdst = self.bar_src, self.bar_dst

        def emit_sp(e):
            for (s, v, _) in finals:
                e.wait_ge(s, v)
            e.dma_start(out=bdst, in_=bsrc).then_inc(bsem, 16)
            e.wait_ge(bsem, bv)

        self.q["sp"].append(emit_sp)
        for eng in ("pe", "act", "dve", "pool"):
            self.q[eng].append(lambda e: e.wait_ge(bsem, bv))
        for eng in ENGS:
            for (_, v, key) in finals:
                self.seen[eng][key] = v
        for b in self.phase_bufs:
            self.free_ds.append(b.ds)
            b.ds = None
        self.phase_bufs = []

    def flush(self):
        import os
        if os.environ.get("KCOUNT"):
            class _C:
                def __init__(s): s.n = 0
                def __getattr__(s, name):
                    def f(*a, **k):
                        s.n += 1
                        return s
                    return f
            for en in ENGS:
                c = _C()
                for f in self.q[en]:
                    f(c)
                self.tot = getattr(self, "tot", {})
                self.tot[en] = self.tot.get(en, 0) + c.n
            print("COUNT", self.tot, flush=True)
        q = self.q
        with self.nc.Block() as block:
            @block.tensor
            def _(e):
                for f in q["pe"]:
                    f(e)

            @block.scalar
            def _(e):
                for f in q["act"]:
                    f(e)

            @block.vector
            def _(e):
                for f in q["dve"]:
                    f(e)

            @block.gpsimd
            def _(e):
                for f in q["pool"]:
                    f(e)

            @block.sync
            def _(e):
                for f in q["sp"]:
                    f(e)
        self.q = {e: [] for e in ENGS}


class Ctx:
    pass


class StopBuild(Exception):
    pass


def ck(n):
    import os
    if n > float(os.environ.get("KA", "99")):
        raise StopBuild()


_UID = [0]


def U(name):
    _UID[0] += 1
    return "%s_%d" % (name, _UID[0])


def build(S, layer_types, debug_out=None):
    NT = S // 512
    NKT = S // 128
    depth = len(layer_types)
    n_a = max(1, sum(1 for t in layer_types if t == "a"))
    n_b = max(1, sum(1 for t in layer_types if t == "b"))
    nc = bass.Bass("TRN2", target_bir_lowering=False)
    dt_in = lambda name, shape, dt=F32: nc.dram_tensor(name, shape, dt, kind="ExternalInput").ap()
    x = dt_in("x", [S, D])
    mem = dt_in("mem", [NMEM, D])
    pos = dt_in("pos", [1, S], I32)
    norm_pre = dt_in("norm_pre", [depth, D])
    norm_post = dt_in("norm_post", [depth, D])
    mem_norm = dt_in("mem_norm", [1, D])
    w_mem_kv = dt_in("w_mem_kv", [depth, D, 512])
    w_out = dt_in("w_out", [depth, D, D])
    w_in_a = dt_in("w_in_a", [n_a, D, A_IN])
    q_norm_a = dt_in("q_norm_a", [n_a, 384])
    kv_norm_a = dt_in("kv_norm_a", [n_a, 256])
    w_q_up_a = dt_in("w_q_up_a", [n_a, 384, 1152])
    w_kv_up_a = dt_in("w_kv_up_a", [n_a, 256, 1536])
    w_in_b = dt_in("w_in_b", [n_b, D, B_IN])
    b_f = dt_in("b_f", [n_b, 12])
    ropec = dt_in("ropec", [128, 1])
    out = nc.dram_tensor("out", [S, D], F32, kind="ExternalOutput").ap()

    if debug_out:
        scr = lambda name, shape, dt: nc.dram_tensor(name, shape, dt, kind="ExternalOutput").ap()
    else:
        scr = lambda name, shape, dt: nc.dram_tensor(name, shape, dt).ap()
    hbuf = scr("hbuf", [S, D], F32)
    QT = scr("QT", [12, 96, S], BF16)
    KT = scr("KT", [12, 96, S], BF16)
    VS = scr("VS", [12, 128, NKT, 65], BF16)
    SG = scr("SG", [16, 64, S], BF16)
    QM = scr("QM", [4, 64, S], BF16)
    YT = scr("YT", [D, S], BF16)
    COS = scr("COS", [128, S], F32)
    SIN = scr("SIN", [128, S], F32)

    with ExitStack() as gst:
        P = Prog(nc, gst)
        gsb = lambda name, shape, dt: gst.enter_context(nc.sbuf_tensor(name, shape, dt))

        ident = gsb("ident", [128, 128], BF16)
        ones_bf = gsb("ones_bf", [128, 128], BF16)
        ones_f = gsb("ones_f", [128, 128], F32)
        maskb = gsb("maskb", [128, 128], BF16)
        gpre = gsb("gpre", [128, depth, 8], F32)
        gq = gsb("gq", [128, n_a, 3], F32)
        gkv = gsb("gkv", [128, n_a, 2], F32)
        gmem = gsb("gmem", [128, 8], F32)
        negbf = gsb("negbf", [12, n_b], F32)
        ropec_sb = gsb("ropec_sb", [128, 1], F32)
        memnT = gsb("memnT", [128, 8, NMEM], BF16)
        km_sb = gsb("km_sb", [64, 4, NMEM], BF16)
        vm_sb = gsb("vm_sb", [128, 2, 4, 65], BF16)
        b_const = Buf("const")
        b_memnT = Buf("memnT")
        b_km = Buf("km")
        b_vm = Buf("vm")

        with ExitStack() as st:
            sb = lambda name, shape, dt: st.enter_context(nc.sbuf_tensor(U(name), shape, dt))
            ps = lambda name, shape, dt: st.enter_context(nc.psum_tensor(U(name), shape, dt))
            tmpf = sb("tmpf", [128, 128], F32)
            b_tmpf = Buf("tmpf")
            P.op("pool", lambda e: e.memset(tmpf[:], 0.0), writes=[b_tmpf])
            P.op("pool", lambda e: e.affine_select(out=tmpf[:], in_=tmpf[:], pattern=[[-1, 128]],
                                                   compare_op=ALU.not_equal, fill=1.0, base=0, channel_multiplier=1),
                 reads=[b_tmpf], writes=[b_tmpf])
            P.op("pool", lambda e: e.tensor_copy(out=ident[:], in_=tmpf[:]), reads=[b_tmpf], writes=[b_const])
            tmpm = sb("tmpm", [128, 128], F32)
            b_tmpm = Buf("tmpm")
            P.op("pool", lambda e: e.memset(tmpm[:], 0.0), writes=[b_tmpm])
            P.op("pool", lambda e: e.affine_select(out=tmpm[:], in_=tmpm[:], pattern=[[1, 128]],
                                                   compare_op=ALU.is_ge, fill=NEG, base=0, channel_multiplier=-1),
                 reads=[b_tmpm], writes=[b_tmpm])
            P.op("pool", lambda e: e.tensor_copy(out=maskb[:], in_=tmpm[:]), reads=[b_tmpm], writes=[b_const])
            P.op("pool", lambda e: e.memset(ones_bf[:], 1.0), writes=[b_const])
            P.op("pool", lambda e: e.memset(ones_f[:], 1.0), writes=[b_const])
            P.op("pool", lambda e: e.memset(vm_sb[:], 1.0), writes=[b_vm])
            import os as _os
            _sstop = _os.environ.get("KSSTOP", "")
            b_g = Buf("gains")
            for l in range(depth if _sstop != "s1" else 0):
                P.dma(b_g, gpre[:, l, :], norm_pre[l, :].rearrange("(c p) -> p c", p=128), writes=[b_const],
                      allow_slow_non_contiguous=True)
            for j in range(n_a):
                P.dma(b_g, gq[:, j, :], q_norm_a[j, :].rearrange("(c p) -> p c", p=128), writes=[b_const],
                      allow_slow_non_contiguous=True)
                P.dma(b_g, gkv[:, j, :], kv_norm_a[j, :].rearrange("(c p) -> p c", p=128), writes=[b_const],
                      allow_slow_non_contiguous=True)
            P.dma(b_g, gmem[:, :], mem_norm[0, :].rearrange("(c p) -> p c", p=128), writes=[b_const],
                  allow_slow_non_contiguous=True)
            P.dma(b_g, negbf[:, :], b_f.rearrange("j h -> h j"), writes=[b_const], allow_slow_non_contiguous=True)
            P.dma(b_g, ropec_sb[:, :], ropec[:, :], writes=[b_const])
            P.op("dve", lambda e: e.tensor_scalar(out=negbf[:], in0=negbf[:], scalar1=-1.0, scalar2=None, op0=ALU.mult),
                 reads=[b_const], writes=[b_const])

            memt = sb("memt", [128, 2, D], F32)
            b_memt = Buf("memt")
            P.dma(b_memt, memt[:], mem.rearrange("(s p) d -> p s d", p=128), writes=[b_memt])
            junk = sb("junk", [128, D], BF16)
            b_junk = Buf("junk")
            mss = sb("mss", [128, 2], F32)
            b_mss = Buf("mss")
            mems = sb("mems", [128, 2, D], BF16)
            b_mems = Buf("mems")
            for s in range(2):
                P.op("act", lambda e, s=s: e.activation(out=junk[:], in_=memt[:, s, :], func=AF.Square,
                                                        accum_out=mss[:, s:s + 1]),
                     reads=[b_memt], writes=[b_junk, b_mss])
            mrs = sb("mrs", [128, 2], F32)
            b_mrs = Buf("mrs")
            P.op("dve", lambda e: e.tensor_scalar(out=mrs[:], in0=mss[:], scalar1=1.0 / D, scalar2=EPS, op0=ALU.mult, op1=ALU.add),
                 reads=[b_mss], writes=[b_mrs])
            P.op("act", lambda e: e.activation(out=mrs[:], in_=mrs[:], func=AF.Sqrt), reads=[b_mrs], writes=[b_mrs])
            P.op("dve", lambda e: e.reciprocal(out=mrs[:], in_=mrs[:]), reads=[b_mrs], writes=[b_mrs])
            tpm = ps("tpm", [128, 1024], BF16)
            b_tpm = Buf("tpm")
            for s in range(2):
                P.op("dve", lambda e, s=s: e.tensor_scalar(out=mems[:, s, :], in0=memt[:, s, :], scalar1=mrs[:, s:s + 1],
                                                           scalar2=None, op0=ALU.mult),
                     reads=[b_memt, b_mrs], writes=[b_mems])
                for c in range(8):
                    P.op("pe", lambda e, s=s, c=c: e.transpose(out=tpm[:, c * 128:(c + 1) * 128],
                                                               in_=mems[:, s, c * 128:(c + 1) * 128], identity=ident[:]),
                         reads=[b_mems, b_const], writes=[b_tpm], sig=(c == 7))
                P.op("dve", lambda e, s=s: e.tensor_copy(out=memnT[:, :, s * 128:(s + 1) * 128],
                                                         in_=tpm[:, :].rearrange("p (c t) -> p c t", c=8)),
                     reads=[b_tpm], writes=[b_memnT])

            if "a" in layer_types and _sstop not in ("s1", "s2", "s3"):
                CH = 1024 if S % 1024 == 0 else 512
                posi = sb("posi", [128, CH], I32)
                ang = sb("ang", [128, CH], F32)
                kf = sb("kf", [128, CH], F32)
                ki = sb("ki", [128, CH], I32)
                mk = sb("mk", [128, CH], F32)
                r2 = sb("r2", [128, CH], F32)
                sn = sb("sn", [128, CH], F32)
                cs_ = sb("cs_", [128, CH], F32)
                b_posi, b_ang, b_kf, b_ki, b_mk, b_r2, b_sn, b_cs = (Buf(n) for n in
                                                                     ("posi", "ang", "kf", "ki", "mk", "r2", "sn", "cs_"))
                PI_SAFE = 3.1415925
                C1 = 6.28125
                C2 = TWO_PI - 6.28125

                def wrap(r, b_r):
                    P.op("dve", lambda e: e.tensor_scalar(out=mk[:], in0=r[:], scalar1=math.pi, scalar2=None, op0=ALU.is_gt),
                         reads=[b_r], writes=[b_mk])
                    P.op("dve", lambda e: e.scalar_tensor_tensor(out=r[:], in0=mk[:], scalar=-TWO_PI, in1=r[:],
                                                                 op0=ALU.mult, op1=ALU.add),
                         reads=[b_mk, b_r], writes=[b_r])
                    P.op("dve", lambda e: e.tensor_scalar(out=mk[:], in0=r[:], scalar1=-math.pi, scalar2=None, op0=ALU.is_lt),
                         reads=[b_r], writes=[b_mk])
                    P.op("dve", lambda e: e.scalar_tensor_tensor(out=r[:], in0=mk[:], scalar=TWO_PI, in1=r[:],
                                                                 op0=ALU.mult, op1=ALU.add),
                         reads=[b_mk, b_r], writes=[b_r])
                    P.op("dve", lambda e: e.tensor_scalar(out=r[:], in0=r[:], scalar1=-PI_SAFE, scalar2=PI_SAFE,
                                                          op0=ALU.max, op1=ALU.min),
                         reads=[b_r], writes=[b_r])

                for ch in range(S // CH):
                    t0 = ch * CH
                    P.dma(b_posi, posi[:], pos[0, t0:t0 + CH].partition_broadcast(128), writes=[b_posi])
                    P.op("dve", lambda e: e.tensor_copy(out=ang[:], in_=posi[:]), reads=[b_posi], writes=[b_ang])
                    P.op("dve", lambda e: e.tensor_scalar(out=ang[:], in0=ang[:], scalar1=ropec_sb[:, 0:1], scalar2=None,
                                                          op0=ALU.mult), reads=[b_ang, b_const], writes=[b_ang])
                    P.op("dve", lambda e: e.tensor_scalar(out=ki[:], in0=ang[:], scalar1=1.0 / TWO_PI, scalar2=None,
                                                          op0=ALU.mult), reads=[b_ang], writes=[b_ki])
                    P.op("dve", lambda e: e.tensor_copy(out=kf[:], in_=ki[:]), reads=[b_ki], writes=[b_kf])
                    P.op("dve", lambda e: e.scalar_tensor_tensor(out=ang[:], in0=kf[:], scalar=-C1, in1=ang[:],
                                                                 op0=ALU.mult, op1=ALU.add),
                         reads=[b_kf, b_ang], writes=[b_ang])
                    P.op("dve", lambda e: e.scalar_tensor_tensor(out=ang[:], in0=kf[:], scalar=-C2, in1=ang[:],
                                                                 op0=ALU.mult, op1=ALU.add),
                         reads=[b_kf, b_ang], writes=[b_ang])
                    wrap(ang, b_ang)
                    P.op("dve", lambda e: e.tensor_scalar(out=r2[:], in0=ang[:], scalar1=0.5 * math.pi, scalar2=None,
                                                          op0=ALU.add), reads=[b_ang], writes=[b_r2])
                    wrap(r2, b_r2)
                    P.op("act", lambda e: e.activation(out=sn[:], in_=ang[:], func=AF.Sin), reads=[b_ang], writes=[b_sn])
                    P.op("act", lambda e: e.activation(out=cs_[:], in_=r2[:], func=AF.Sin), reads=[b_r2], writes=[b_cs])
                    P.dma(b_sn, SIN[:, t0:t0 + CH], sn[:], reads=[b_sn])
                    P.dma(b_cs, COS[:, t0:t0 + CH], cs_[:], reads=[b_cs])
            P.barrier()
            P.flush()

        import os as _os
        _stop = _os.environ.get("KSTOP", "")
        ia = ib = 0
        for l, lt in enumerate(layer_types):
            if _stop == "setup":
                break
            h_in = x if l == 0 else hbuf
            h_out = out if l == depth - 1 else hbuf
            if lt == "a":
                j = ia
                ia += 1
                DQ = 96
                scale = 96.0 ** -0.5
            else:
                j = ib
                ib += 1
                DQ = 70
                scale = 0.125
            with ExitStack() as st:
                sb = lambda name, shape, dt: st.enter_context(nc.sbuf_tensor(U(name), shape, dt))
                ps = lambda name, shape, dt: st.enter_context(nc.psum_tensor(U(name), shape, dt))
                NIN = A_IN if lt == "a" else B_IN
                w_in_d = w_in_a if lt == "a" else w_in_b
                w_in = sb("w_in", [128, 8, NIN], BF16)
                b_wd = Buf("wd")
                b_wp = Buf("wp")
                b_wa = Buf("wa")
                b_w = b_wp
                NSTG = 4
                stage = [sb("stage%d" % i, [128, 2048], F32) for i in range(NSTG)]
                b_stage = [Buf("stage%d" % i) for i in range(NSTG)]
                stg = [0]

                def stage_load(src_ap, ncols):
                    i = stg[0] % NSTG
                    stg[0] += 1
                    P.dma(b_stage[i], stage[i][:, 0:ncols], src_ap, writes=[b_stage[i]])
                    return stage[i], b_stage[i]

                alt = [0]

                def scale_to(out_ap, in_ap, g_ap, b_in, neg=False):
                    eng = ("dve", "act", "pool")[alt[0] % 3] if not neg else ("dve", "pool")[alt[0] % 2]
                    alt[0] += 1
                    bw_ = {"dve": b_wd, "pool": b_wp, "act": b_wa}[eng]
                    if eng == "act":
                        P.op("act", lambda e: e.activation(out=out_ap, in_=in_ap, func=AF.Copy, scale=g_ap),
                             reads=[b_in, b_const], writes=[bw_])
                        return
                    if neg:
                        P.op(eng, lambda e: e.tensor_scalar(out=out_ap, in0=in_ap, scalar1=g_ap, scalar2=-1.0,
                                                            op0=ALU.mult, op1=ALU.mult), reads=[b_in, b_const], writes=[bw_])
                    else:
                        P.op(eng, lambda e: e.tensor_scalar(out=out_ap, in0=in_ap, scalar1=g_ap, scalar2=None,
                                                            op0=ALU.mult), reads=[b_in, b_const], writes=[bw_])

                try:
                    if lt == "a":
                        wkr = sb("wkr", [128, 8, 96], BF16)
                        wkr_rot = sb("wkr_rot", [128, 8, 96], BF16)
                        wq_nope = sb("wq_nope", [128, 3, 768], BF16)
                        wq_rope = sb("wq_rope", [128, 3, 384], BF16)
                        wq_rot = sb("wq_rot", [128, 3, 384], BF16)
                        wkv_k = sb("wkv_k", [128, 2, 768], BF16)
                        wkv_v = sb("wkv_v", [128, 2, 768], BF16)
                        P.op("pool", lambda e: e.memset(wkr[:, :, 0:64], 0.0), writes=[b_wp])
                        P.op("pool", lambda e: e.memset(wkr_rot[:, :, 0:64], 0.0), writes=[b_wp])
                        for c in range(8):
                            stt, bst = stage_load(w_in_d[j, c * 128:(c + 1) * 128, :], A_IN)
                            g = gpre[:, l, c:c + 1]
                            scale_to(w_in[:, c, :], stt[:, 0:A_IN], g, bst)
                            scale_to(wkr[:, c, 64:96], stt[:, 640:672], g, bst)
                            scale_to(wkr_rot[:, c, 64:80], stt[:, 656:672], g, bst, neg=True)
                            scale_to(wkr_rot[:, c, 80:96], stt[:, 640:656], g, bst)
                        for c in range(3):
                            stt, bst = stage_load(w_q_up_a[j, c * 128:(c + 1) * 128, :], 1152)
                            g = gq[:, j, c:c + 1]
                            sv = stt[:, 0:1152].rearrange("p (h e) -> p h e", e=96)
                            scale_to(wq_nope[:, c, :].rearrange("p (h e) -> p h e", e=64), sv[:, :, 0:64], g, bst)
                            scale_to(wq_rope[:, c, :].rearrange("p (h e) -> p h e", e=32), sv[:, :, 64:96], g, bst)
                            rv = wq_rot[:, c, :].rearrange("p (h e) -> p h e", e=32)
                            scale_to(rv[:, :, 0:16], sv[:, :, 80:96], g, bst, neg=True)
                            scale_to(rv[:, :, 16:32], sv[:, :, 64:80], g, bst)
                        for c in range(2):
                            stt, bst = stage_load(w_kv_up_a[j, c * 128:(c + 1) * 128, :], 1536)
                            g = gkv[:, j, c:c + 1]
                            sv = stt[:, 0:1536].rearrange("p (h e) -> p h e", e=128)
                            scale_to(wkv_k[:, c, :].rearrange("p (h e) -> p h e", e=64), sv[:, :, 0:64], g, bst)
                            scale_to(wkv_v[:, c, :].rearrange("p (h e) -> p h e", e=64), sv[:, :, 64:128], g, bst)
                    else:
                        HB = B_IN // 2
                        for c in range(8):
                            for hf in range(2):
                                stt, bst = stage_load(w_in_d[j, c * 128:(c + 1) * 128, hf * HB:(hf + 1) * HB], HB)
                                scale_to(w_in[:, c, hf * HB:(hf + 1) * HB], stt[:, 0:HB], gpre[:, l, c:c + 1], bst)
                    ck(1)
                    wm = sb("wm", [128, 8, 512], BF16)
                    for c in range(8):
                        stt, bst = stage_load(w_mem_kv[l, c * 128:(c + 1) * 128, :], 512)
                        scale_to(wm[:, c, :], stt[:, 0:512], gmem[:, c:c + 1], bst)

                    NPS = 6
                    pbank = [ps("pb%d" % i, [128, 512], F32) for i in range(NPS)]
                    b_pbank = [Buf("pb%d" % i) for i in range(NPS)]
                    tp = [ps("tp%d" % i, [128, 1024], BF16) for i in range(2)]
                    b_tp = [Buf("tp%d" % i) for i in range(2)]
                    pbi = [0]

                    def next_bank():
                        i = pbi[0] % NPS
                        pbi[0] += 1
                        return pbank[i], b_pbank[i]

                    evi = [0]

                    def evac_copy(out_ap, in_ap, b_in, b_out, eng=None):
                        if eng is None:
                            eng = "act" if evi[0] % 2 == 0 else "dve"
                            evi[0] += 1
                        if eng == "act":
                            P.op("act", lambda e: e.activation(out=out_ap, in_=in_ap, func=AF.Copy), reads=[b_in], writes=[b_out])
                        else:
                            P.op("dve", lambda e: e.tensor_copy(out=out_ap, in_=in_ap), reads=[b_in], writes=[b_out])

                    for hm in range(4):
                        pb, bpb = next_bank()
                        for c in range(8):
                            P.op("pe", lambda e, c=c, hm=hm, pb=pb: e.matmul(pb[0:64, 0:NMEM], lhsT=wm[:, c, hm * 64:(hm + 1) * 64],
                                                                            rhs=memnT[:, c, :], start=(c == 0), stop=(c == 7)),
                                 reads=[b_wd, b_wp, b_wa, b_memnT], writes=[bpb], sig=(c == 7))
                        evac_copy(km_sb[:, hm, :], pb[0:64, 0:NMEM], bpb, b_km)
                    for kt in range(2):
                        pb, bpb = next_bank()
                        for c in range(8):
                            P.op("pe", lambda e, c=c, kt=kt, pb=pb: e.matmul(pb[:, 0:256], lhsT=memnT[:, c, kt * 128:(kt + 1) * 128],
                                                                            rhs=wm[:, c, 256:512], start=(c == 0), stop=(c == 7)),
                                 reads=[b_wd, b_wp, b_wa, b_memnT], writes=[bpb], sig=(c == 7))
                        evac_copy(vm_sb[:, kt, :, 0:64], pb[:, 0:256].rearrange("p (h d) -> p h d", d=64), bpb, b_vm)

                    ck(2)
                    NXT = 4
                    xt = [sb("xt%d" % i, [128, D], F32) for i in range(NXT)]
                    b_xt = [Buf("xt%d" % i) for i in range(NXT)]
                    xs = [sb("xs%d" % i, [128, D], BF16) for i in range(2)]
                    b_xs = [Buf("xs%d" % i) for i in range(2)]
                    junk = sb("junk", [128, D], BF16)
                    b_junk = Buf("junk")
                    ss = [sb("ss%d" % i, [128, 1], F32) for i in range(4)]
                    b_ss = [Buf("ss%d" % i) for i in range(4)]
                    hnT = [sb("hnT%d" % i, [128, 8, 512], BF16) for i in range(2)]
                    b_hnT = [Buf("hnT%d" % i) for i in range(2)]
                    qn_sb = sb("qn_sb", [128, 6, 512], BF16)
                    b_qn = [Buf("qn%d" % i) for i in range(6)]
                    kn_sb = sb("kn_sb", [128, 6, 512], BF16)
                    b_kn = [Buf("kn%d" % i) for i in range(6)]
                    v_sb = sb("v_sb", [128, 12, 4, 65], BF16)
                    b_v = Buf("v_sb")
                    P.op("pool", lambda e: e.memset(v_sb[:], 1.0), writes=[b_v])
                    sg_sb = sb("sg_sb", [128, 8, 512], BF16)
                    b_sg = [Buf("sg%d" % i) for i in range(8)]
                    qm_sb = sb("qm_sb", [128, 2, 512], BF16)
                    b_qm = [Buf("qm%d" % i) for i in range(2)]
                    if lt == "a":
                        cs = [sb("cs%d" % i, [128, 2, 512], F32) for i in range(2)]
                        b_cs = [Buf("cs%d" % i) for i in range(2)]
                        cq_sb = sb("cq_sb", [128, 5, 512], BF16)
                        sq_sb = sb("sq_sb", [128, 5, 512], BF16)
                        cn_sb = sb("cn_sb", [128, 5, 512], BF16)
                        b_cq = [Buf("cq%d" % i) for i in range(5)]
                        b_sq = [Buf("sq%d" % i) for i in range(5)]
                        b_cn = [Buf("cn%d" % i) for i in range(5)]
                        rsb = [sb("rsb%d" % i, [128, 512], F32) for i in range(2)]
                        b_rsb = [Buf("rsb%d" % i) for i in range(2)]
                        t1 = [sb("t1_%d" % i, [128, 512], F32) for i in range(2)]
                        t2 = [sb("t2_%d" % i, [128, 512], F32) for i in range(2)]
                        b_t1 = [Buf("t1_%d" % i) for i in range(2)]
                        b_t2 = [Buf("t2_%d" % i) for i in range(2)]
                        qro_sb = sb("qro_sb", [128, 3, 512], BF16)
                        b_qro = [Buf("qro%d" % i) for i in range(3)]
                        kro_sb = sb("kro_sb", [128, 512], BF16)
                        b_kro = Buf("kro")
                        rpi = [0]
                    else:
                        f1 = sb("f1", [12, 512], F32)
                        f2 = sb("f2", [12, 512], F32)
                        b_f1 = Buf("f1")
                        b_f2 = Buf("f2")
                        ones12 = sb("ones12", [12, 512], F32)
                        P.op("pool", lambda e: e.memset(ones12[:], 1.0), writes=[b_w])
                        Fc = [sb("Fc%d" % i, [12, 512], F32) for i in range(2)]
                        b_Fc = [Buf("Fc%d" % i) for i in range(2)]
                        x0 = sb("x0", [12, 512], F32)
                        x1 = sb("x1", [12, 512], F32)
                        b_x0 = Buf("x0")
                        b_x1 = Buf("x1")
                        fq = sb("fq", [12, 6, 512], BF16)
                        fk = sb("fk", [12, 6, 512], BF16)
                        b_fq = Buf("fq")
                        b_fk = Buf("fk")
                        P.op("pool", lambda e: e.memset(fq[:], 1.0), writes=[b_fq])
                        P.op("pool", lambda e: e.memset(fk[:], 1.0), writes=[b_fk])

                    def load_tile(tt):
                        import os
                        for s in range(4):
                            i = (tt * 4 + s) % NXT
                            r0 = tt * 512 + s * 128
                            if os.environ.get("KADDR0"):
                                r0 = s * 128
                            P.dma(b_xt[i], xt[i][:], h_in[r0:r0 + 128, :], writes=[b_xt[i]] + ([b_hnT[0], b_tp[0], b_tp[1]] if (os.environ.get("KSER") and tt > 0) else []), tag="xt%d" % tt)
                        if lt == "a":
                            k = tt % 2
                            P.dma(b_cs[k], cs[k][:, 0, :], COS[:, tt * 512:(tt + 1) * 512], writes=[b_cs[k]], tag="cs%d" % tt)
                            P.dma(b_cs[k], cs[k][:, 1, :], SIN[:, tt * 512:(tt + 1) * 512], writes=[b_cs[k]], tag="cs%d" % tt)

                    def mm_group(out_ap, bpb, lhs_fn, rhs_fn, nchunk, reads):
                        for c in range(nchunk):
                            P.op("pe", lambda e, c=c: e.matmul(out_ap, lhsT=lhs_fn(c), rhs=rhs_fn(c), start=(c == 0),
                                                               stop=(c == nchunk - 1)),
                                 reads=reads, writes=[bpb], sig=(c == nchunk - 1))

                    def prep(tt):
                        hT = hnT[tt % 2]
                        bhT = b_hnT[tt % 2]
                        for s in range(4):
                            i = (tt * 4 + s) % NXT
                            P.op("act", lambda e, i=i, s=s: e.activation(out=junk[:], in_=xt[i][:], func=AF.Square,
                                                                         accum_out=ss[s][:, 0:1]),
                                 reads=[b_xt[i]], writes=[b_junk, b_ss[s]])
                            P.op("dve", lambda e, s=s: e.tensor_scalar(out=ss[s][:], in0=ss[s][:], scalar1=1.0 / D, scalar2=EPS,
                                                                       op0=ALU.mult, op1=ALU.add), reads=[b_ss[s]], writes=[b_ss[s]])
                            P.op("act", lambda e, s=s: e.activation(out=ss[s][:], in_=ss[s][:], func=AF.Sqrt),
                                 reads=[b_ss[s]], writes=[b_ss[s]])
                            P.op("dve", lambda e, s=s: e.reciprocal(out=ss[s][:], in_=ss[s][:]), reads=[b_ss[s]], writes=[b_ss[s]])
                            ck(2.6)
                            k = s % 2
                            P.op("dve", lambda e, i=i, s=s, k=k: e.tensor_scalar(out=xs[k][:], in0=xt[i][:], scalar1=ss[s][:, 0:1],
                                                                                 scalar2=None, op0=ALU.mult),
                                 reads=[b_xt[i], b_ss[s]], writes=[b_xs[k]])
                            for c in range(8):
                                P.op("pe", lambda e, c=c, k=k: e.transpose(out=tp[k][:, c * 128:(c + 1) * 128],
                                                                           in_=xs[k][:, c * 128:(c + 1) * 128], identity=ident[:]),
                                     reads=[b_xs[k], b_const], writes=[b_tp[k]], sig=(c == 7))
                            ck(2.8)
                            evac_copy(hT[:, :, s * 128:(s + 1) * 128], tp[k][:, :].rearrange("p (c t) -> p c t", c=8),
                                      b_tp[k], bhT, eng="dve")
                            ck(2.9 + 0.01 * s)
                        if tt + 1 < NT:
                            load_tile(tt + 1)

                    ck(2.2)
                    load_tile(0)
                    ck(2.4)
                    prep(0)
                    for tt in range(NT):
                        t0 = tt * 512
                        tsl = slice(t0, t0 + 512)
                        hT = hnT[tt % 2]
                        bhT = b_hnT[tt % 2]
                        ck(3)

                        def inproj(col0, M, evac):
                            pb, bpb = next_bank()
                            mm_group(pb[0:M, :], bpb, lambda c: w_in[:, c, col0:col0 + M], lambda c: hT[:, c, :], 8, [b_wd, b_wp, b_wa, bhT])
                            evac(pb, bpb)

                        def store_heads(sb_t, bufs, dram, row0, nrow, per):
                            for g in range(len(bufs)):
                                for e_ in range(per):
                                    hh = g * per + e_
                                    P.dma(bufs[g], dram[hh, row0:row0 + nrow, tsl], sb_t[e_ * nrow:(e_ + 1) * nrow, g, :],
                                          reads=[bufs[g]], tag="heads")

                        def gate_and_qm(col_qm, col_g):
                            for pr in range(2):
                                inproj(col_qm + pr * 128, 128,
                                       lambda pb, bpb, pr=pr: evac_copy(qm_sb[:, pr, :], pb[:, :], bpb, b_qm[pr]))
                                for e_ in range(2):
                                    P.dma(b_qm[pr], QM[2 * pr + e_, :, tsl], qm_sb[e_ * 64:(e_ + 1) * 64, pr, :], reads=[b_qm[pr]], tag="qm")
                            for g in range(8):
                                def ev(pb, bpb, g=g):
                                    P.op("act", lambda e: e.activation(out=sg_sb[:, g, :], in_=pb[:, :], func=AF.Silu),
                                         reads=[bpb], writes=[b_sg[g]])
                                inproj(col_g + g * 128, 128, ev)
                                for e_ in range(2):
                                    P.dma(b_sg[g], SG[2 * g + e_, :, tsl], sg_sb[e_ * 64:(e_ + 1) * 64, g, :], reads=[b_sg[g]], tag="sg")

                        if lt == "a":
                            csk = cs[tt % 2]
                            bcsk = b_cs[tt % 2]
                            for g in range(5):
                                def ev(pb, bpb, g=g):
                                    P.op("dve", lambda e: e.tensor_copy(out=cq_sb[:, g, :], in_=pb[:, :]), reads=[bpb], writes=[b_cq[g]])
                                    P.op("act", lambda e: e.activation(out=sq_sb[:, g, :], in_=cq_sb[:, g, :], func=AF.Square),
                                         reads=[b_cq[g]], writes=[b_sq[g]])
                                inproj(g * 128, 128, ev)
                            ck(4)
                            pbA, bA = next_bank()
                            mm_group(pbA[0:96, :], bA, lambda c: wkr[:, c, :], lambda c: hT[:, c, :], 8, [b_wd, b_wp, b_wa, bhT])
                            pbB, bB = next_bank()
                            mm_group(pbB[0:96, :], bB, lambda c: wkr_rot[:, c, :], lambda c: hT[:, c, :], 8, [b_wd, b_wp, b_wa, bhT])
                            r = rpi[0] % 2
                            rpi[0] += 1
                            P.op("dve", lambda e, r=r, pbA=pbA: e.tensor_tensor(out=t1[r][64:96, :], in0=pbA[64:96, :], in1=csk[64:96, 0, :],
                                                                                op=ALU.mult), reads=[bA, bcsk], writes=[b_t1[r]])
                            P.op("dve", lambda e, r=r, pbB=pbB: e.tensor_tensor(out=t2[r][64:96, :], in0=pbB[64:96, :], in1=csk[64:96, 1, :],
                                                                                op=ALU.mult), reads=[bB, bcsk], writes=[b_t2[r]])
                            P.op("pool", lambda e, r=r: e.tensor_tensor(out=kro_sb[64:96, :], in0=t1[r][64:96, :], in1=t2[r][64:96, :],
                                                                        op=ALU.add), reads=[b_t1[r], b_t2[r]], writes=[b_kro])
                            for hh in range(12):
                                P.dma(b_kro, KT[hh, 64:96, tsl], kro_sb[64:96, :], reads=[b_kro], tag="kro")
                            ck(5)
                            for (g0, ng, dim) in ((0, 3, 384.0), (3, 2, 256.0)):
                                pb, bpb = next_bank()
                                mm_group(pb[:, :], bpb, lambda c: ones_bf[:, :], lambda c, g0=g0: sq_sb[:, g0 + c, :], ng,
                                         [b_const] + b_sq[g0:g0 + ng])
                                k = 0 if g0 == 0 else 1
                                P.op("dve", lambda e, k=k, pb=pb, dim=dim: e.tensor_scalar(out=rsb[k][:], in0=pb[:, :], scalar1=1.0 / dim,
                                                                                         scalar2=EPS, op0=ALU.mult, op1=ALU.add),
                                     reads=[bpb], writes=[b_rsb[k]])
                                P.op("act", lambda e, k=k: e.activation(out=rsb[k][:], in_=rsb[k][:], func=AF.Sqrt),
                                     reads=[b_rsb[k]], writes=[b_rsb[k]])
                                P.op("dve", lambda e, k=k: e.reciprocal(out=rsb[k][:], in_=rsb[k][:]), reads=[b_rsb[k]], writes=[b_rsb[k]])
                                for g in range(g0, g0 + ng):
                                    eng = "pool" if g % 2 == 0 else "dve"
                                    P.op(eng, lambda e, g=g, k=k: e.tensor_tensor(out=cn_sb[:, g, :], in0=cq_sb[:, g, :], in1=rsb[k][:],
                                                                                  op=ALU.mult),
                                         reads=[b_cq[g], b_rsb[k]], writes=[b_cn[g]])
                            ck(6)
                            for pr in range(6):
                                pb, bpb = next_bank()
                                mm_group(pb[:, :], bpb, lambda c, pr=pr: wq_nope[:, c, pr * 128:(pr + 1) * 128],
                                         lambda c: cn_sb[:, c, :], 3, [b_wd, b_wp, b_wa] + b_cn[0:3])
                                evac_copy(qn_sb[:, pr, :], pb[:, :], bpb, b_qn[pr])
                            store_heads(qn_sb, b_qn, QT, 0, 64, 2)
                            ck(7)
                            for g in range(3):
                                pbA, bA = next_bank()
                                mm_group(pbA[:, :], bA, lambda c, g=g: wq_rope[:, c, g * 128:(g + 1) * 128],
                                         lambda c: cn_sb[:, c, :], 3, [b_wd, b_wp, b_wa] + b_cn[0:3])
                                pbB, bB = next_bank()
                                mm_group(pbB[:, :], bB, lambda c, g=g: wq_rot[:, c, g * 128:(g + 1) * 128],
                                         lambda c: cn_sb[:, c, :], 3, [b_wd, b_wp, b_wa] + b_cn[0:3])
                                r = rpi[0] % 2
                                rpi[0] += 1
                                P.op("dve", lambda e, r=r, pbA=pbA: e.tensor_tensor(out=t1[r][:], in0=pbA[:, :], in1=csk[:, 0, :], op=ALU.mult),
                                     reads=[bA, bcsk], writes=[b_t1[r]])
                                P.op("dve", lambda e, r=r, pbB=pbB: e.tensor_tensor(out=t2[r][:], in0=pbB[:, :], in1=csk[:, 1, :], op=ALU.mult),
                                     reads=[bB, bcsk], writes=[b_t2[r]])
                                P.op("pool", lambda e, r=r, g=g: e.tensor_tensor(out=qro_sb[:, g, :], in0=t1[r][:], in1=t2[r][:], op=ALU.add),
                                     reads=[b_t1[r], b_t2[r]], writes=[b_qro[g]])
                            store_heads(qro_sb, b_qro, QT, 64, 32, 4)
                            ck(8)
                            for pr in range(6):
                                pb, bpb = next_bank()
                                mm_group(pb[:, :], bpb, lambda c, pr=pr: wkv_k[:, c, pr * 128:(pr + 1) * 128],
                                         lambda c: cn_sb[:, 3 + c, :], 2, [b_wd, b_wp, b_wa] + b_cn[3:5])
                                evac_copy(kn_sb[:, pr, :], pb[:, :], bpb, b_kn[pr])
                            store_heads(kn_sb, b_kn, KT, 0, 64, 2)
                            ck(9)
                            for s in range(4):
                                for hf in range(2):
                                    pb, bpb = next_bank()
                                    mm_group(pb[:, 0:384], bpb, lambda c, s=s: cn_sb[:, 3 + c, s * 128:(s + 1) * 128],
                                             lambda c, hf=hf: wkv_v[:, c, hf * 384:(hf + 1) * 384], 2, [b_wd, b_wp, b_wa] + b_cn[3:5])
                                    evac_copy(v_sb[:, hf * 6:(hf + 1) * 6, s, 0:64],
                                              pb[:, 0:384].rearrange("p (h d) -> p h d", d=64), bpb, b_v)
                            P.dma(b_v, VS[:, :, tt * 4:(tt + 1) * 4, :].rearrange("h p k d -> p h k d"), v_sb[:], reads=[b_v], tag="vs")
                            ck(10)
                            if tt + 1 < NT:
                                prep(tt + 1)
                            gate_and_qm(672, 928)
                        else:
                            for pr in range(6):
                                inproj(pr * 128, 128, lambda pb, bpb, pr=pr: evac_copy(qn_sb[:, pr, :], pb[:, :], bpb, b_qn[pr]))
                            store_heads(qn_sb, b_qn, QT, 0, 64, 2)
                            ck(4)
                            for pr in range(6):
                                inproj(768 + pr * 128, 128, lambda pb, bpb, pr=pr: evac_copy(kn_sb[:, pr, :], pb[:, :], bpb, b_kn[pr]))
                            store_heads(kn_sb, b_kn, KT, 0, 64, 2)
                            ck(5)
                            for s in range(4):
                                for hf in range(2):
                                    pb, bpb = next_bank()
                                    mm_group(pb[:, 0:384], bpb, lambda c, s=s: hT[:, c, s * 128:(s + 1) * 128],
                                             lambda c, hf=hf: w_in[:, c, 1536 + hf * 384:1536 + (hf + 1) * 384], 8, [b_wd, b_wp, b_wa, bhT])
                                    evac_copy(v_sb[:, hf * 6:(hf + 1) * 6, s, 0:64],
                                              pb[:, 0:384].rearrange("p (h d) -> p h d", d=64), bpb, b_v)
                            P.dma(b_v, VS[:, :, tt * 4:(tt + 1) * 4, :].rearrange("h p k d -> p h k d"), v_sb[:], reads=[b_v], tag="vs")
                            ck(6)
                            pb, bpb = next_bank()
                            mm_group(pb[0:12, :], bpb, lambda c: w_in[:, c, 2304:2316], lambda c: hT[:, c, :], 8, [b_wd, b_wp, b_wa, bhT])
                            P.op("act", lambda e, pb=pb: e.activation(out=f1[:], in_=pb[0:12, :], func=AF.Exp, scale=-1.0,
                                                                      bias=negbf[:, j:j + 1]), reads=[bpb, b_const], writes=[b_f1])
                            P.op("dve", lambda e: e.tensor_scalar(out=f1[:], in0=f1[:], scalar1=1.0, scalar2=None, op0=ALU.add),
                                 reads=[b_f1], writes=[b_f1])
                            P.op("act", lambda e: e.activation(out=f2[:], in_=f1[:], func=AF.Ln), reads=[b_f1], writes=[b_f2])
                            cur = Fc[tt % 2]
                            bcur = b_Fc[tt % 2]
                            prv = Fc[(tt + 1) % 2]
                            bprv = b_Fc[(tt + 1) % 2]
                            init = 0.0 if tt == 0 else prv[:, 511:512]
                            P.op("dve", lambda e, cur=cur, init=init: e.tensor_tensor_scan(out=cur[:], data0=ones12[:], data1=f2[:],
                                                                                           initial=init, op0=ALU.mult, op1=ALU.subtract),
                                 reads=[b_f2, b_wp, bprv], writes=[bcur])
                            inv_s = 1.0 / scale
                            P.op("dve", lambda e, cur=cur: e.tensor_scalar(out=x0[:], in0=cur[:], scalar1=inv_s, scalar2=None, op0=ALU.mult),
                                 reads=[bcur], writes=[b_x0])
                            P.op("dve", lambda e: e.tensor_copy(out=fq[:, 0, :], in_=x0[:]), reads=[b_x0], writes=[b_fq])
                            P.op("dve", lambda e: e.tensor_tensor(out=x1[:], in0=x0[:], in1=fq[:, 0, :], op=ALU.subtract),
                                 reads=[b_x0, b_fq], writes=[b_x1])
                            P.op("dve", lambda e: e.tensor_copy(out=fq[:, 1, :], in_=x1[:]), reads=[b_x1], writes=[b_fq])
                            P.op("dve", lambda e: e.tensor_tensor(out=x0[:], in0=x1[:], in1=fq[:, 1, :], op=ALU.subtract),
                                 reads=[b_x1, b_fq], writes=[b_x0])
                            P.op("dve", lambda e: e.tensor_copy(out=fq[:, 2, :], in_=x0[:]), reads=[b_x0], writes=[b_fq])
                            P.op("dve", lambda e: e.tensor_scalar(out=fk[:, 3:6, :], in0=fq[:, 0:3, :], scalar1=-1.0, scalar2=None,
                                                                  op0=ALU.mult), reads=[b_fq], writes=[b_fk])
                            ck(7)
                            P.dma(b_fq, QT[:, 64:70, tsl], fq[:], reads=[b_fq])
                            P.dma(b_fk, KT[:, 64:70, tsl], fk[:], reads=[b_fk])
                            ck(8)
                            if tt + 1 < NT:
                                prep(tt + 1)
                            gate_and_qm(2316, 2572)
                except StopBuild:
                    pass
                P.barrier()
                P.flush()

            if _stop == "A":
                break
            with ExitStack() as st:
                sb = lambda name, shape, dt: st.enter_context(nc.sbuf_tensor(U(name), shape, dt))
                ps = lambda name, shape, dt: st.enter_context(nc.psum_tensor(U(name), shape, dt))
                qT = [sb("qT%d" % i, [96, S], BF16) for i in range(2)]
                kT = [sb("kT%d" % i, [96, S], BF16) for i in range(2)]
                vv = [sb("vv%d" % i, [128, NKT, 65], BF16) for i in range(2)]
                sg = [sb("sg%d" % i, [64, S], BF16) for i in range(2)]
                b_qT = [Buf("qT%d" % i) for i in range(2)]
                b_kT = [Buf("kT%d" % i) for i in range(2)]
                b_vv = [Buf("vv%d" % i) for i in range(2)]
                b_sgh = [Buf("sgh%d" % i) for i in range(2)]
                NPT = 6
                pT = [sb("pT%d" % i, [128, 1024], BF16) for i in range(NPT)]
                b_pT = [Buf("pT%d" % i) for i in range(NPT)]
                NFR = 4
                rec = [sb("rec%d" % i, [128, 512], F32) for i in range(NFR)]
                b_rec = [Buf("rec%d" % i) for i in range(NFR)]
                rsum = [sb("rsum%d" % i, [128, 512], F32) for i in range(NFR)]
                b_rsum = [Buf("rsum%d" % i) for i in range(NFR)]
                rech = [sb("rech%d" % i, [128, 512], BF16) for i in range(NFR)]
                recl = [sb("recl%d" % i, [128, 512], BF16) for i in range(NFR)]
                b_rech = [Buf("rech%d" % i) for i in range(NFR)]
                tmpo = [sb("tmpo%d" % i, [64, 512], F32) for i in range(4)]
                b_tmpo = [Buf("tmpo%d" % i) for i in range(4)]
                ysb = [sb("ysb%d" % i, [64, 512], BF16) for i in range(3)]
                b_ysb = [Buf("ysb%d" % i) for i in range(3)]
                NSP = 3
                NOP = 1
                spp = [ps("spp%d" % i, [128, 1024], F32) for i in range(NSP)]
                b_spp = [Buf("spp%d" % i) for i in range(NSP)]
                ops_ = [ps("ops%d" % i, [128, 512], F32) for i in range(NOP)]
                b_ops = [Buf("ops%d" % i) for i in range(NOP)]
                bcp = ps("bcp", [128, 512], F32)
                b_bcp = Buf("bcp")

                def load_head(hd):
                    k = hd % 2
                    if hd < 12:
                        P.dma(b_qT[k], qT[k][0:DQ, :], QT[hd, 0:DQ, :], writes=[b_qT[k]])
                        P.dma(b_kT[k], kT[k][0:DQ, :], KT[hd, 0:DQ, :], writes=[b_kT[k]])
                        P.dma(b_vv[k], vv[k][:], VS[hd, :, :, :], writes=[b_vv[k]])
                    else:
                        P.dma(b_qT[k], qT[k][0:64, :], QM[hd - 12, :, :], writes=[b_qT[k]])
                    P.dma(b_sgh[k], sg[k][:], SG[hd, :, :], writes=[b_sgh[k]])

                PE_EMBED = True
                FDEFER = 5
                heads_done = []

                def tick():
                    for p_ in pending:
                        p_[0] -= 1
                    while pending and pending[0][0] <= 0:
                        pending.pop(0)[1]()

                cnt = Ctx()
                cnt.g = 0
                cnt.blk = 0
                pending = []

                load_head(0)
                for hd in range(16):
                    k = hd % 2
                    if hd + 1 < 16:
                        load_head(hd + 1)
                    causal = hd < 12
                    if causal:
                        dq = DQ
                        sc = scale
                        kT_ap = lambda kb, k=k, dq=dq: kT[k][0:dq, kb * 128:(kb + 1) * 128]
                        v_ap = lambda kb, k=k: vv[k][:, kb, :]
                        kv_reads = [b_kT[k], b_vv[k]]
                    else:
                        dq = 64
                        sc = 0.125
                        hm = hd - 12
                        kT_ap = lambda kb, hm=hm: km_sb[:, hm, kb * 128:(kb + 1) * 128]
                        v_ap = lambda kb, hm=hm: vm_sb[:, kb, hm, :]
                        kv_reads = [b_km, b_vm]
                    for qi in range(NT):
                        nkb = 4 * (qi + 1) if causal else 2
                        ob = cnt.blk % NFR
                        cnt.blk += 1
                        o_ps = ops_[ob % NOP]
                        b_o = b_ops[ob % NOP]
                        q0 = qi * 512
                        ngrp = nkb // 2
                        grp_info = []

                        def emit_qk(g):
                            gi = cnt.g
                            cnt.g += 1
                            sp_t = spp[gi % NSP]
                            b_sp = b_spp[gi % NSP]
                            offs = []
                            for e_ in range(2):
                                kb = 2 * g + e_
                                o = (kb - 4 * qi) * 128 if (causal and kb >= 4 * qi) else 0
                                diag = causal and kb >= 4 * qi
                                lo = e_ * 512 + o
                                hi = (e_ + 1) * 512
                                P.op("pe", lambda e, kb=kb, lo=lo, hi=hi, o=o, diag=diag, sp_t=sp_t: e.matmul(
                                    sp_t[:, lo:hi], lhsT=kT_ap(kb), rhs=qT[k][0:dq, q0 + o:q0 + 512], start=True, stop=(not diag)),
                                     reads=[b_qT[k]] + kv_reads, writes=[b_sp], sig=(e_ == 1 and not diag), embed=PE_EMBED)
                                if diag:
                                    P.op("pe", lambda e, lo=lo, sp_t=sp_t: e.matmul(sp_t[:, lo:lo + 128], lhsT=ident[:], rhs=maskb[:],
                                                                                  start=False, stop=True),
                                         reads=[b_const], writes=[b_sp], sig=(e_ == 1))
                                offs.append((kb, o, lo, hi))
                            grp_info.append((gi, sp_t, b_sp, offs))

                        def emit_exp_pv(g):
                            gi, sp_t, b_sp, offs = grp_info[g]
                            pt = pT[gi % NPT]
                            b_pt = b_pT[gi % NPT]
                            if offs[0][1] == 0 and offs[1][1] == 0:
                                P.op("act", lambda e: e.activation(out=pt[:, :], in_=sp_t[:, :], func=AF.Exp, scale=sc),
                                     reads=[b_sp], writes=[b_pt])
                            else:
                                for (kb, o, lo, hi) in offs:
                                    P.op("act", lambda e, lo=lo, hi=hi: e.activation(out=pt[:, lo:hi], in_=sp_t[:, lo:hi],
                                                                                     func=AF.Exp, scale=sc),
                                         reads=[b_sp], writes=[b_pt])
                            for (kb, o, lo, hi) in offs:
                                last = (kb == nkb - 1)
                                P.op("pe", lambda e, kb=kb, o=o, lo=lo, hi=hi, last=last: e.matmul(
                                    o_ps[0:65, o:512], lhsT=v_ap(kb), rhs=pt[:, lo:hi], start=(kb == 0), stop=last),
                                     reads=[b_pt] + kv_reads, writes=[b_o], sig=last, embed=PE_EMBED)

                        def fin1(hd=hd, qi=qi, ob=ob, o_ps=o_ps, b_o=b_o, k=k):
                            rc, b_rc, tm, b_tm, rs_, b_rs = rec[ob], b_rec[ob], tmpo[ob], b_tmpo[ob], rsum[ob], b_rsum[ob]
                            P.op("dve", lambda e: e.tensor_tensor(out=tm[:], in0=o_ps[0:64, :], in1=sg[k][:, qi * 512:(qi + 1) * 512],
                                                                  op=ALU.mult), reads=[b_o, b_sgh[k]], writes=[b_tm])
                            P.op("dve", lambda e: e.tensor_copy(out=rs_[64:65, :], in_=o_ps[64:65, :]), reads=[b_o], writes=[b_rs])
                            P.op("dve", lambda e: e.reciprocal(out=rc[64:65, :], in_=rs_[64:65, :]), reads=[b_rs], writes=[b_rc])
                            rh, rl, b_rh = rech[ob], recl[ob], b_rech[ob]
                            P.op("dve", lambda e: e.tensor_copy(out=rh[64:65, :], in_=rc[64:65, :]), reads=[b_rc], writes=[b_rh])
                            P.op("dve", lambda e: e.tensor_tensor(out=rl[64:65, :], in0=rc[64:65, :], in1=rh[64:65, :], op=ALU.subtract),
                                 reads=[b_rc, b_rh], writes=[b_rh])

                        def fin2(hd=hd, qi=qi, ob=ob):
                            tm, b_tm, rh, rl, b_rh = tmpo[ob], b_tmpo[ob], rech[ob], recl[ob], b_rech[ob]
                            P.op("pe", lambda e: e.matmul(bcp[0:64, :], lhsT=ones_bf[64:65, 0:64], rhs=rh[64:65, :], start=True, stop=False),
                                 reads=[b_const, b_rh], writes=[b_bcp], sig=False)
                            P.op("pe", lambda e: e.matmul(bcp[0:64, :], lhsT=ones_bf[64:65, 0:64], rhs=rl[64:65, :], start=False, stop=True),
                                 reads=[b_const, b_rh], writes=[b_bcp])
                            yi = (hd * NT + qi) % 3
                            P.op("dve", lambda e: e.tensor_tensor(out=ysb[yi][:], in0=tm[:], in1=bcp[0:64, :], op=ALU.mult),
                                 reads=[b_tm, b_bcp], writes=[b_ysb[yi]])
                            P.dma(b_ysb[yi], YT[hd * 64:(hd + 1) * 64, qi * 512:(qi + 1) * 512], ysb[yi][:], reads=[b_ysb[yi]])

                        for g_ in range(min(NSP, ngrp)):
                            emit_qk(g_)
                        for g in range(ngrp):
                            emit_exp_pv(g)
                            if g + NSP < ngrp:
                                emit_qk(g + NSP)
                            tick()
                        fin1()
                        while len(pending) >= NFR - 1:
                            pending.pop(0)[1]()
                        pending.append([FDEFER, fin2])
                        if qi == NT - 1:
                            heads_done.append(hd)
                while pending:
                    pending.pop(0)[1]()
                P.barrier()
                P.flush()

            if _stop == "B":
                break
            with ExitStack() as st:
                sb = lambda name, shape, dt: st.enter_context(nc.sbuf_tensor(U(name), shape, dt))
                ps = lambda name, shape, dt: st.enter_context(nc.psum_tensor(U(name), shape, dt))
                wo = sb("wo", [128, 8, D], BF16)
                b_wo = Buf("wo")
                stage = [sb("stagec%d" % i, [128, D], F32) for i in range(2)]
                b_stage = [Buf("stagec%d" % i) for i in range(2)]
                for c in range(8):
                    i = c % 2
                    P.dma(b_stage[i], stage[i][:], w_out[l, c * 128:(c + 1) * 128, :], writes=[b_stage[i]])
                    eng = "dve" if c % 2 == 0 else "pool"
                    P.op(eng, lambda e, c=c, i=i: e.tensor_copy(out=wo[:, c, :], in_=stage[i][:]), reads=[b_stage[i]], writes=[b_wo])
                grow = sb("grow", [1, D], F32)
                b_grow = Buf("grow")
                gpost = sb("gpost", [128, D], F32)
                b_gpost = Buf("gpost")
                P.dma(b_grow, grow[:], norm_post[l:l + 1, :], writes=[b_grow])
                pso = [ps("pso%d" % i, [128, 1024], F32) for i in range(3)]
                b_pso = [Buf("pso%d" % i) for i in range(3)]
                for hf in range(2):
                    P.op("pe", lambda e, hf=hf: e.matmul(pso[0][:, hf * 512:(hf + 1) * 512], lhsT=ones_f[0:1, :],
                                                         rhs=grow[0:1, hf * 512:(hf + 1) * 512], start=True, stop=True),
                         reads=[b_const, b_grow], writes=[b_pso[0]], sig=(hf == 1))
                P.op("dve", lambda e: e.tensor_copy(out=gpost[:], in_=pso[0][:, :]), reads=[b_pso[0]], writes=[b_gpost])
                yT = [sb("yT%d" % i, [128, 8, 512], BF16) for i in range(2)]
                b_yT = [Buf("yT%d" % i) for i in range(2)]
                NXT = 6
                xt = [sb("xtc%d" % i, [128, D], F32) for i in range(NXT)]
                b_xt = [Buf("xtc%d" % i) for i in range(NXT)]
                tsb = [sb("tsb%d" % i, [128, D], F32) for i in range(2)]
                b_tsb = [Buf("tsb%d" % i) for i in range(2)]
                usb = [sb("usb%d" % i, [128, D], F32) for i in range(2)]
                b_usb = [Buf("usb%d" % i) for i in range(2)]
                ho = [sb("ho%d" % i, [128, D], F32) for i in range(3)]
                b_ho = [Buf("ho%d" % i) for i in range(3)]
                junk = sb("junkc", [128, D], BF16)
                b_junk = Buf("junkc")
                ssc = [sb("ssc%d" % i, [128, 1], F32) for i in range(4)]
                b_ssc = [Buf("ssc%d" % i) for i in range(4)]

                def load_c(tt):
                    k = tt % 2
                    P.dma(b_yT[k], yT[k][:], YT[:, tt * 512:(tt + 1) * 512].rearrange("(c p) t -> p c t", p=128), writes=[b_yT[k]])

                def load_x(it):
                    i = it % NXT
                    P.dma(b_xt[i], xt[i][:], h_in[it * 128:(it + 1) * 128, :], writes=[b_xt[i]])

                load_c(0)
                for it in range(min(4, NKT)):
                    load_x(it)
                for tt in range(NT):
                    k = tt % 2
                    if tt + 1 < NT:
                        load_c(tt + 1)
                    for s in range(4):
                        it = tt * 4 + s
                        if it + 4 < NKT:
                            load_x(it + 4)
                        pi = it % 3
                        po = pso[pi]
                        b_po = b_pso[pi]
                        for hf in range(2):
                            for c in range(8):
                                P.op("pe", lambda e, c=c, hf=hf, s=s, k=k, po=po: e.matmul(
                                    po[:, hf * 512:(hf + 1) * 512], lhsT=yT[k][:, c, s * 128:(s + 1) * 128],
                                    rhs=wo[:, c, hf * 512:(hf + 1) * 512], start=(c == 0), stop=(c == 7)),
                                     reads=[b_yT[k], b_wo], writes=[b_po], sig=(c == 7 and hf == 1))
                        sc_ = ssc[it % 4]
                        b_sc = b_ssc[it % 4]
                        ti = it % 2
                        P.op("act", lambda e, po=po, sc_=sc_: e.activation(out=junk[:], in_=po[:, :], func=AF.Square, accum_out=sc_[:, 0:1]),
                             reads=[b_po], writes=[b_junk, b_sc, b_tsb[ti]])
                        P.op("dve", lambda e, po=po, ti=ti: e.tensor_tensor(out=tsb[ti][:], in0=po[:, :], in1=gpost[:], op=ALU.mult),
                             reads=[b_po, b_gpost], writes=[b_tsb[ti]])
                        P.op("dve", lambda e, sc_=sc_: e.tensor_scalar(out=sc_[:], in0=sc_[:], scalar1=1.0 / D, scalar2=EPS,
                                                                       op0=ALU.mult, op1=ALU.add), reads=[b_sc], writes=[b_sc])
                        P.op("act", lambda e, sc_=sc_: e.activation(out=sc_[:], in_=sc_[:], func=AF.Sqrt), reads=[b_sc], writes=[b_sc])
                        P.op("dve", lambda e, sc_=sc_: e.reciprocal(out=sc_[:], in_=sc_[:]), reads=[b_sc], writes=[b_sc])
                        xi = it % NXT
                        hi_ = it % 3
                        P.op("act", lambda e, ti=ti, sc_=sc_: e.activation(out=usb[ti][:], in_=tsb[ti][:], func=AF.Copy, scale=sc_[:, 0:1]),
                             reads=[b_tsb[ti], b_sc], writes=[b_usb[ti]])
                        P.op("pool", lambda e, ti=ti, xi=xi, hi_=hi_: e.tensor_tensor(out=ho[hi_][:], in0=usb[ti][:], in1=xt[xi][:], op=ALU.add),
                             reads=[b_usb[ti], b_xt[xi]], writes=[b_ho[hi_]])
                        P.dma(b_ho[hi_], h_out[it * 128:(it + 1) * 128, :], ho[hi_][:], reads=[b_ho[hi_]])
                P.barrier()
                P.flush()
        P_ninst = P.ninst
    nc._ninst = P_ninst
    return nc


LAYER_TYPES = ["a", "b", "a", "b"]
_NC_CACHE = {}


def rope_const():
    half = 16
    inv = (10000.0 ** (-np.arange(half, dtype=np.float32) / half)).astype(np.float32)
    return np.tile(inv, 8).reshape(128, 1).astype(np.float32)


def make_in_maps(inputs, B):
    c = lambda a: np.ascontiguousarray(a)
    shared = {
        "norm_pre": c(inputs["norm_pre"]), "norm_post": c(inputs["norm_post"]),
        "mem_norm": c(inputs["mem_norm"]).reshape(1, D),
        "w_mem_kv": c(inputs["w_mem_kv"]), "w_out": c(inputs["w_out"]),
        "w_in_a": c(inputs["w_in_a"]), "q_norm_a": c(inputs["q_norm_a"]), "kv_norm_a": c(inputs["kv_norm_a"]),
        "w_q_up_a": c(inputs["w_q_up_a"]), "w_kv_up_a": c(inputs["w_kv_up_a"]),
        "w_in_b": c(inputs["w_in_b"]), "b_f": c(inputs["b_f"]),
        "ropec": rope_const(),
    }
    maps = []
    for b in range(B):
        m = dict(shared)
        m["x"] = c(inputs["x"][b])
        m["mem"] = c(inputs["mem"][b])
        m["pos"] = c(inputs["positions"][b]).reshape(1, -1).astype(np.int32)
        maps.append(m)
    return maps


def kernel(**inputs):
    inputs = {k: np.asarray(v) for k, v in inputs.items()}
    B, S, _ = inputs["x"].shape
    key = (S, tuple(LAYER_TYPES))
    if key not in _NC_CACHE:
        _NC_CACHE[key] = build(S, LAYER_TYPES)
    nc = _NC_CACHE[key]
    maps = make_in_maps(inputs, B)
    res = run_bass_kernel_spmd(nc, maps, core_ids=list(range(B)))
    return np.stack([np.asarray(r["out"]) for r in res.results], axis=0).astype(np.float32)
```

```python
import math
import os
from contextlib import ExitStack

import numpy as np
import concourse.bass as bass
import concourse.mybir as mybir
from concourse.bass_utils import run_bass_kernel_spmd

F32 = mybir.dt.float32
BF16 = mybir.dt.bfloat16
I32 = mybir.dt.int32
AF = mybir.ActivationFunctionType
ALU = mybir.AluOpType

D = 1024
NMEM = 256
EPS = 1e-6
A_IN = 1952
B_IN = 3596
NEG = -30000.0
TWO_PI = 2.0 * math.pi

ENGS = ("pe", "act", "dve", "pool", "sp")


class _Rec:
    def __init__(self):
        self.call = None

    def __getattr__(self, name):
        def f(*args, **kwargs):
            self.call = (name, args, kwargs)
            return self
        return f


class DSem:
    __slots__ = ("sem", "total", "name")

    def __init__(self, sem, name):
        self.sem = sem
        self.total = 0
        self.name = name


class Buf:
    __slots__ = ("name", "w", "r", "ds")

    def __init__(self, name):
        self.name = name
        self.w = None
        self.r = []
        self.ds = None


class Prog:
    def __init__(self, nc, stack, n_dsem=72):
        self.nc = nc
        self.q = {e: [] for e in ENGS}
        self.sem = {e: stack.enter_context(nc.semaphore("s_" + e)) for e in ENGS}
        self.cnt = {e: 0 for e in ENGS}
        self.seen = {e: {} for e in ENGS}
        self.free_ds = [DSem(stack.enter_context(nc.semaphore("dq%d" % i)), "dq%d" % i) for i in range(n_dsem)]
        self.all_ds = list(self.free_ds)
        self.phase_bufs = []
        self.ninst = 0
        self.embed_default = True
        self.bar_sem = stack.enter_context(nc.semaphore("s_bar"))
        self.bar_total = 0
        self.bar_src = nc.dram_tensor("bar_src", [1, 16], F32).ap()[:, :]
        self.bar_dst = stack.enter_context(nc.sbuf_tensor("bar_dst", [1, 16], F32))[:]

    def _deps(self, eng, reads, writes):
        deps = {}

        def add(tok):
            if tok is None:
                return
            key, sem, val, ds = tok
            if ds is not None:
                val = ds.total
            if key == eng and eng == "pe":
                return
            cur = deps.get(key)
            if cur is None or cur[1] < val:
                deps[key] = (sem, val)

        for b in reads:
            add(b.w)
        for b in writes:
            add(b.w)
            for t in b.r:
                add(t)
        out = []
        seen = self.seen[eng]
        for key, (sem, val) in deps.items():
            if seen.get(key, 0) >= val:
                continue
            seen[key] = val
            out.append((sem, val))
        return out

    def _mark(self, tok, reads, writes):
        for b in writes:
            b.w = tok
            b.r = []
        for b in reads:
            b.r.append(tok)
            if len(b.r) > 16:
                last = {}
                for t in b.r:
                    k = t[0]
                    if k not in last or last[k][2] < t[2]:
                        last[k] = t
                b.r = list(last.values())

    def op(self, eng, fn, reads=(), writes=(), sig=True, embed=None):
        waits = self._deps(eng, reads, writes)
        if sig:
            self.cnt[eng] += 1
            val = self.cnt[eng]
        else:
            val = self.cnt[eng] + 1
        sem = self.sem[eng]
        tok = (eng, sem, val, None)
        self.ninst += 1 + len(waits)
        rec = _Rec()
        fn(rec)
        call = rec.call
        if embed is None:
            embed = (eng in ("act", "dve", "pool")) and call[2].get("accum_out") is None and self.embed_default
        emb = None
        if embed and waits:
            csems = [self.sem[e_] for e_ in ("pe", "act", "dve", "pool")]
            for wi in range(len(waits) - 1, -1, -1):
                if any(waits[wi][0] is cs_ for cs_ in csems):
                    emb = waits[wi]
                    waits = waits[:wi] + waits[wi + 1:]
                    break

        def emit(e, waits=waits, call=call, sig=sig, sem=sem, emb=emb):
            for (s, v) in waits:
                e.wait_ge(s, v)
            ins = getattr(e, call[0])(*call[1], **call[2])
            if emb is not None:
                ins = ins._wait_ge(emb[0], emb[1])
            if sig:
                ins.then_inc(sem, 1)

        self.q[eng].append(emit)
        self._mark(tok, reads, writes)
        return tok

    def dma(self, owner, out, in_, reads=(), writes=(), eng="sp", tag=None, **kw):
        import os
        if tag is not None and tag in os.environ.get("KDMA_POOL", "").split(","):
            eng = "pool"
        if tag is not None and tag in os.environ.get("KDMA_ACT", "").split(","):
            eng = "act"
        if tag is not None and tag in os.environ.get("KDMA_SKIP", "").split(","):
            return None
        waits = self._deps(eng, reads, writes)
        if owner.ds is None:
            owner.ds = self.free_ds.pop()
            self.phase_bufs.append(owner)
        ds = owner.ds
        ds.total += 16
        tok = (ds.name, ds.sem, ds.total, ds)
        self.ninst += 1 + len(waits)

        def emit(e, waits=waits, sem=ds.sem, out=out, in_=in_, kw=kw):
            for (s, v) in waits:
                e.wait_ge(s, v)
            e.dma_start(out=out, in_=in_, **kw).then_inc(sem, 16)

        self.q[eng].append(emit)
        self._mark(tok, reads, writes)
        return tok

    def barrier(self):
        finals = [(d.sem, d.total, d.name) for d in self.all_ds if d.total > 0]
        finals += [(self.sem[e], self.cnt[e], e) for e in ("pe", "act", "dve", "pool") if self.cnt[e] > 0]
        self.bar_total += 16
        bv = self.bar_total
        bsem = self.bar_sem
        bsrc, bdst = self.bar_src, self.bar_dst

        def emit_sp(e):
            for (s, v, _) in finals:
                e.wait_ge(s, v)
            e.dma_start(out=bdst, in_=bsrc).then_inc(bsem, 16)
            e.wait_ge(bsem, bv)

        self.q["sp"].append(emit_sp)
        for eng in ("pe", "act", "dve", "pool"):
            self.q[eng].append(lambda e: e.wait_ge(bsem, bv))
        for eng in ENGS:
            for (_, v, key) in finals:
                self.seen[eng][key] = v
        for b in self.phase_bufs:
            self.free_ds.append(b.ds)
            b.ds = None
        self.phase_bufs = []

    def flush(self):
        import os
        if os.environ.get("KCOUNT"):
            class _C:
                def __init__(s): s.n = 0
                def __getattr__(s, name):
                    def f(*a, **k):
                        s.n += 1
                        return s
                    return f
            for en in ENGS:
                c = _C()
                for f in self.q[en]:
                    f(c)
                self.tot = getattr(self, "tot", {})
                self.tot[en] = self.tot.get(en, 0) + c.n
            print("COUNT", self.tot, flush=True)
        q = self.q
        with self.nc.Block() as block:
            @block.tensor
            def _(e):
                for f in q["pe"]:
                    f(e)

            @block.scalar
            def _(e):
                for f in q["act"]:
                    f(e)

            @block.vector
            def _(e):
                for f in q["dve"]:
                    f(e)

            @block.gpsimd
            def _(e):
                for f in q["pool"]:
                    f(e)

            @block.sync
            def _(e):
                for f in q["sp"]:
                    f(e)
        self.q = {e: [] for e in ENGS}


class Ctx:
    pass


class StopBuild(Exception):
    pass


def ck(n):
    import os
    if n > float(os.environ.get("KA", "99")):
        raise StopBuild()


_UID = [0]


def U(name):
    _UID[0] += 1
    return "%s_%d" % (name, _UID[0])


def build(S, layer_types, debug_out=None):
    NT = S // 512
    NKT = S // 128
    depth = len(layer_types)
    n_a = max(1, sum(1 for t in layer_types if t == "a"))
    n_b = max(1, sum(1 for t in layer_types if t == "b"))
    nc = bass.Bass("TRN2", target_bir_lowering=False)
    dt_in = lambda name, shape, dt=F32: nc.dram_tensor(name, shape, dt, kind="ExternalInput").ap()
    x = dt_in("x", [S, D])
    mem = dt_in("mem", [NMEM, D])
    pos = dt_in("pos", [1, S], I32)
    norm_pre = dt_in("norm_pre", [depth, D])
    norm_post = dt_in("norm_post", [depth, D])
    mem_norm = dt_in("mem_norm", [1, D])
    w_mem_kv = dt_in("w_mem_kv", [depth, D, 512])
    w_out = dt_in("w_out", [depth, D, D])
    w_in_a = dt_in("w_in_a", [n_a, D, A_IN])
    q_norm_a = dt_in("q_norm_a", [n_a, 384])
    kv_norm_a = dt_in("kv_norm_a", [n_a, 256])
    w_q_up_a = dt_in("w_q_up_a", [n_a, 384, 1152])
    w_kv_up_a = dt_in("w_kv_up_a", [n_a, 256, 1536])
    w_in_b = dt_in("w_in_b", [n_b, D, B_IN])
    b_f = dt_in("b_f", [n_b, 12])
    ropec = dt_in("ropec", [128, 1])
    out = nc.dram_tensor("out", [S, D], F32, kind="ExternalOutput").ap()

    if debug_out:
        scr = lambda name, shape, dt: nc.dram_tensor(name, shape, dt, kind="ExternalOutput").ap()
    else:
        scr = lambda name, shape, dt: nc.dram_tensor(name, shape, dt).ap()
    hbuf = scr("hbuf", [S, D], F32)
    QT = scr("QT", [12, 96, S], BF16)
    KT = scr("KT", [12, 96, S], BF16)
    VS = scr("VS", [12, 128, NKT, 65], BF16)
    SG = scr("SG", [16, 64, S], BF16)
    QM = scr("QM", [4, 64, S], BF16)
    YT = scr("YT", [D, S], BF16)
    COS = scr("COS", [128, S], F32)
    SIN = scr("SIN", [128, S], F32)

    with ExitStack() as gst:
        P = Prog(nc, gst)
        gsb = lambda name, shape, dt: gst.enter_context(nc.sbuf_tensor(name, shape, dt))

        ident = gsb("ident", [128, 128], BF16)
        ones_bf = gsb("ones_bf", [128, 128], BF16)
        ones_f = gsb("ones_f", [128, 128], F32)
        maskb = gsb("maskb", [128, 128], BF16)
        gpre = gsb("gpre", [128, depth, 8], F32)
        gq = gsb("gq", [128, n_a, 3], F32)
        gkv = gsb("gkv", [128, n_a, 2], F32)
        gmem = gsb("gmem", [128, 8], F32)
        negbf = gsb("negbf", [12, n_b], F32)
        ropec_sb = gsb("ropec_sb", [128, 1], F32)
        memnT = gsb("memnT", [128, 8, NMEM], BF16)
        km_sb = gsb("km_sb", [64, 4, NMEM], BF16)
        vm_sb = gsb("vm_sb", [128, 2, 4, 65], BF16)
        b_const = Buf("const")
        b_memnT = Buf("memnT")
        b_km = Buf("km")
        b_vm = Buf("vm")

        with ExitStack() as st:
            sb = lambda name, shape, dt: st.enter_context(nc.sbuf_tensor(U(name), shape, dt))
            ps = lambda name, shape, dt: st.enter_context(nc.psum_tensor(U(name), shape, dt))
            tmpf = sb("tmpf", [128, 128], F32)
            b_tmpf = Buf("tmpf")
            P.op("pool", lambda e: e.memset(tmpf[:], 0.0), writes=[b_tmpf])
            P.op("pool", lambda e: e.affine_select(out=tmpf[:], in_=tmpf[:], pattern=[[-1, 128]],
                                                   compare_op=ALU.not_equal, fill=1.0, base=0, channel_multiplier=1),
                 reads=[b_tmpf], writes=[b_tmpf])
            P.op("pool", lambda e: e.tensor_copy(out=ident[:], in_=tmpf[:]), reads=[b_tmpf], writes=[b_const])
            tmpm = sb("tmpm", [128, 128], F32)
            b_tmpm = Buf("tmpm")
            P.op("pool", lambda e: e.memset(tmpm[:], 0.0), writes=[b_tmpm])
            P.op("pool", lambda e: e.affine_select(out=tmpm[:], in_=tmpm[:], pattern=[[1, 128]],
                                                   compare_op=ALU.is_ge, fill=NEG, base=0, channel_multiplier=-1),
                 reads=[b_tmpm], writes=[b_tmpm])
            P.op("pool", lambda e: e.tensor_copy(out=maskb[:], in_=tmpm[:]), reads=[b_tmpm], writes=[b_const])
            P.op("pool", lambda e: e.memset(ones_bf[:], 1.0), writes=[b_const])
            P.op("pool", lambda e: e.memset(ones_f[:], 1.0), writes=[b_const])
            P.op("pool", lambda e: e.memset(vm_sb[:], 1.0), writes=[b_vm])
            import os as _os
            _sstop = _os.environ.get("KSSTOP", "")
            b_g = Buf("gains")
            for l in range(depth if _sstop != "s1" else 0):
                P.dma(b_g, gpre[:, l, :], norm_pre[l, :].rearrange("(c p) -> p c", p=128), writes=[b_const],
                      allow_slow_non_contiguous=True)
            for j in range(n_a):
                P.dma(b_g, gq[:, j, :], q_norm_a[j, :].rearrange("(c p) -> p c", p=128), writes=[b_const],
                      allow_slow_non_contiguous=True)
                P.dma(b_g, gkv[:, j, :], kv_norm_a[j, :].rearrange("(c p) -> p c", p=128), writes=[b_const],
                      allow_slow_non_contiguous=True)
            P.dma(b_g, gmem[:, :], mem_norm[0, :].rearrange("(c p) -> p c", p=128), writes=[b_const],
                  allow_slow_non_contiguous=True)
            P.dma(b_g, negbf[:, :], b_f.rearrange("j h -> h j"), writes=[b_const], allow_slow_non_contiguous=True)
            P.dma(b_g, ropec_sb[:, :], ropec[:, :], writes=[b_const])
            P.op("dve", lambda e: e.tensor_scalar(out=negbf[:], in0=negbf[:], scalar1=-1.0, scalar2=None, op0=ALU.mult),
                 reads=[b_const], writes=[b_const])

            memt = sb("memt", [128, 2, D], F32)
            b_memt = Buf("memt")
            P.dma(b_memt, memt[:], mem.rearrange("(s p) d -> p s d", p=128), writes=[b_memt])
            junk = sb("junk", [128, D], BF16)
            b_junk = Buf("junk")
            mss = sb("mss", [128, 2], F32)
            b_mss = Buf("mss")
            mems = sb("mems", [128, 2, D], BF16)
            b_mems = Buf("mems")
            for s in range(2):
                P.op("act", lambda e, s=s: e.activation(out=junk[:], in_=memt[:, s, :], func=AF.Square,
                                                        accum_out=mss[:, s:s + 1]),
                     reads=[b_memt], writes=[b_junk, b_mss])
            mrs = sb("mrs", [128, 2], F32)
            b_mrs = Buf("mrs")
            P.op("dve", lambda e: e.tensor_scalar(out=mrs[:], in0=mss[:], scalar1=1.0 / D, scalar2=EPS, op0=ALU.mult, op1=ALU.add),
                 reads=[b_mss], writes=[b_mrs])
            P.op("act", lambda e: e.activation(out=mrs[:], in_=mrs[:], func=AF.Sqrt), reads=[b_mrs], writes=[b_mrs])
            P.op("dve", lambda e: e.reciprocal(out=mrs[:], in_=mrs[:]), reads=[b_mrs], writes=[b_mrs])
            tpm = ps("tpm", [128, 1024], BF16)
            b_tpm = Buf("tpm")
            for s in range(2):
                P.op("dve", lambda e, s=s: e.tensor_scalar(out=mems[:, s, :], in0=memt[:, s, :], scalar1=mrs[:, s:s + 1],
                                                           scalar2=None, op0=ALU.mult),
                     reads=[b_memt, b_mrs], writes=[b_mems])
                for c in range(8):
                    P.op("pe", lambda e, s=s, c=c: e.transpose(out=tpm[:, c * 128:(c + 1) * 128],
                                                               in_=mems[:, s, c * 128:(c + 1) * 128], identity=ident[:]),
                         reads=[b_mems, b_const], writes=[b_tpm], sig=(c == 7))
                P.op("dve", lambda e, s=s: e.tensor_copy(out=memnT[:, :, s * 128:(s + 1) * 128],
                                                         in_=tpm[:, :].rearrange("p (c t) -> p c t", c=8)),
                     reads=[b_tpm], writes=[b_memnT])

            if "a" in layer_types and _sstop not in ("s1", "s2", "s3"):
                CH = 1024 if S % 1024 == 0 else 512
                posi = sb("posi", [128, CH], I32)
                ang = sb("ang", [128, CH], F32)
                kf = sb("kf", [128, CH], F32)
                ki = sb("ki", [128, CH], I32)
                mk = sb("mk", [128, CH], F32)
                r2 = sb("r2", [128, CH], F32)
                sn = sb("sn", [128, CH], F32)
                cs_ = sb("cs_", [128, CH], F32)
                b_posi, b_ang, b_kf, b_ki, b_mk, b_r2, b_sn, b_cs = (Buf(n) for n in
                                                                     ("posi", "ang", "kf", "ki", "mk", "r2", "sn", "cs_"))
                PI_SAFE = 3.1415925
                C1 = 6.28125
                C2 = TWO_PI - 6.28125

                def wrap(r, b_r):
                    P.op("dve", lambda e: e.tensor_scalar(out=mk[:], in0=r[:], scalar1=math.pi, scalar2=None, op0=ALU.is_gt),
                         reads=[b_r], writes=[b_mk])
                    P.op("dve", lambda e: e.scalar_tensor_tensor(out=r[:], in0=mk[:], scalar=-TWO_PI, in1=r[:],
                                                                 op0=ALU.mult, op1=ALU.add),
                         reads=[b_mk, b_r], writes=[b_r])
                    P.op("dve", lambda e: e.tensor_scalar(out=mk[:], in0=r[:], scalar1=-math.pi, scalar2=None, op0=ALU.is_lt),
                         reads=[b_r], writes=[b_mk])
                    P.op("dve", lambda e: e.scalar_tensor_tensor(out=r[:], in0=mk[:], scalar=TWO_PI, in1=r[:],
                                                                 op0=ALU.mult, op1=ALU.add),
                         reads=[b_mk, b_r], writes=[b_r])
                    P.op("dve", lambda e: e.tensor_scalar(out=r[:], in0=r[:], scalar1=-PI_SAFE, scalar2=PI_SAFE,
                                                          op0=ALU.max, op1=ALU.min),
                         reads=[b_r], writes=[b_r])

                for ch in range(S // CH):
                    t0 = ch * CH
                    P.dma(b_posi, posi[:], pos[0, t0:t0 + CH].partition_broadcast(128), writes=[b_posi])
                    P.op("dve", lambda e: e.tensor_copy(out=ang[:], in_=posi[:]), reads=[b_posi], writes=[b_ang])
                    P.op("dve", lambda e: e.tensor_scalar(out=ang[:], in0=ang[:], scalar1=ropec_sb[:, 0:1], scalar2=None,
                                                          op0=ALU.mult), reads=[b_ang, b_const], writes=[b_ang])
                    P.op("dve", lambda e: e.tensor_scalar(out=ki[:], in0=ang[:], scalar1=1.0 / TWO_PI, scalar2=None,
                                                          op0=ALU.mult), reads=[b_ang], writes=[b_ki])
                    P.op("dve", lambda e: e.tensor_copy(out=kf[:], in_=ki[:]), reads=[b_ki], writes=[b_kf])
                    P.op("dve", lambda e: e.scalar_tensor_tensor(out=ang[:], in0=kf[:], scalar=-C1, in1=ang[:],
                                                                 op0=ALU.mult, op1=ALU.add),
                         reads=[b_kf, b_ang], writes=[b_ang])
                    P.op("dve", lambda e: e.scalar_tensor_tensor(out=ang[:], in0=kf[:], scalar=-C2, in1=ang[:],
                                                                 op0=ALU.mult, op1=ALU.add),
                         reads=[b_kf, b_ang], writes=[b_ang])
                    wrap(ang, b_ang)
                    P.op("dve", lambda e: e.tensor_scalar(out=r2[:], in0=ang[:], scalar1=0.5 * math.pi, scalar2=None,
                                                          op0=ALU.add), reads=[b_ang], writes=[b_r2])
                    wrap(r2, b_r2)
                    P.op("act", lambda e: e.activation(out=sn[:], in_=ang[:], func=AF.Sin), reads=[b_ang], writes=[b_sn])
                    P.op("act", lambda e: e.activation(out=cs_[:], in_=r2[:], func=AF.Sin), reads=[b_r2], writes=[b_cs])
                    P.dma(b_sn, SIN[:, t0:t0 + CH], sn[:], reads=[b_sn])
                    P.dma(b_cs, COS[:, t0:t0 + CH], cs_[:], reads=[b_cs])
            P.barrier()
            P.flush()

        import os as _os
        _stop = _os.environ.get("KSTOP", "")
        ia = ib = 0
        for l, lt in enumerate(layer_types):
            if _stop == "setup":
                break
            h_in = x if l == 0 else hbuf
            h_out = out if l == depth - 1 else hbuf
            if lt == "a":
                j = ia
                ia += 1
                DQ = 96
                scale = 96.0 ** -0.5
            else:
                j = ib
                ib += 1
                DQ = 70
                scale = 0.125
            with ExitStack() as st:
                sb = lambda name, shape, dt: st.enter_context(nc.sbuf_tensor(U(name), shape, dt))
                ps = lambda name, shape, dt: st.enter_context(nc.psum_tensor(U(name), shape, dt))
                NIN = A_IN if lt == "a" else B_IN
                w_in_d = w_in_a if lt == "a" else w_in_b
                w_in = sb("w_in", [128, 8, NIN], BF16)
                b_wd = Buf("wd")
                b_wp = Buf("wp")
                b_wa = Buf("wa")
                b_w = b_wp
                NSTG = 4
                stage = [sb("stage%d" % i, [128, 2048], F32) for i in range(NSTG)]
                b_stage = [Buf("stage%d" % i) for i in range(NSTG)]
                stg = [0]

                def stage_load(src_ap, ncols):
                    i = stg[0] % NSTG
                    stg[0] += 1
                    P.dma(b_stage[i], stage[i][:, 0:ncols], src_ap, writes=[b_stage[i]])
                    return stage[i], b_stage[i]

                alt = [0]

                def scale_to(out_ap, in_ap, g_ap, b_in, neg=False):
                    eng = ("dve", "act", "pool")[alt[0] % 3] if not neg else ("dve", "pool")[alt[0] % 2]
                    alt[0] += 1
                    bw_ = {"dve": b_wd, "pool": b_wp, "act": b_wa}[eng]
                    if eng == "act":
                        P.op("act", lambda e: e.activation(out=out_ap, in_=in_ap, func=AF.Copy, scale=g_ap),
                             reads=[b_in, b_const], writes=[bw_])
                        return
                    if neg:
                        P.op(eng, lambda e: e.tensor_scalar(out=out_ap, in0=in_ap, scalar1=g_ap, scalar2=-1.0,
                                                            op0=ALU.mult, op1=ALU.mult), reads=[b_in, b_const], writes=[bw_])
                    else:
                        P.op(eng, lambda e: e.tensor_scalar(out=out_ap, in0=in_ap, scalar1=g_ap, scalar2=None,
                                                            op0=ALU.mult), reads=[b_in, b_const], writes=[bw_])

                try:
                    if lt == "a":
                        wkr = sb("wkr", [128, 8, 96], BF16)
                        wkr_rot = sb("wkr_rot", [128, 8, 96], BF16)
                        wq_nope = sb("wq_nope", [128, 3, 768], BF16)
                        wq_rope = sb("wq_rope", [128, 3, 384], BF16)
                        wq_rot = sb("wq_rot", [128, 3, 384], BF16)
                        wkv_k = sb("wkv_k", [128, 2, 768], BF16)
                        wkv_v = sb("wkv_v", [128, 2, 768], BF16)
                        P.op("pool", lambda e: e.memset(wkr[:, :, 0:64], 0.0), writes=[b_wp])
                        P.op("pool", lambda e: e.memset(wkr_rot[:, :, 0:64], 0.0), writes=[b_wp])
                        for c in range(8):
                            stt, bst = stage_load(w_in_d[j, c * 128:(c + 1) * 128, :], A_IN)
                            g = gpre[:, l, c:c + 1]
                            scale_to(w_in[:, c, :], stt[:, 0:A_IN], g, bst)
                            scale_to(wkr[:, c, 64:96], stt[:, 640:672], g, bst)
                            scale_to(wkr_rot[:, c, 64:80], stt[:, 656:672], g, bst, neg=True)
                            scale_to(wkr_rot[:, c, 80:96], stt[:, 640:656], g, bst)
                        for c in range(3):
                            stt, bst = stage_load(w_q_up_a[j, c * 128:(c + 1) * 128, :], 1152)
                            g = gq[:, j, c:c + 1]
                            sv = stt[:, 0:1152].rearrange("p (h e) -> p h e", e=96)
                            scale_to(wq_nope[:, c, :].rearrange("p (h e) -> p h e", e=64), sv[:, :, 0:64], g, bst)
                            scale_to(wq_rope[:, c, :].rearrange("p (h e) -> p h e", e=32), sv[:, :, 64:96], g, bst)
                            rv = wq_rot[:, c, :].rearrange("p (h e) -> p h e", e=32)
                            scale_to(rv[:, :, 0:16], sv[:, :, 80:96], g, bst, neg=True)
                            scale_to(rv[:, :, 16:32], sv[:, :, 64:80], g, bst)
                        for c in range(2):
                            stt, bst = stage_load(w_kv_up_a[j, c * 128:(c + 1) * 128, :], 1536)
                            g = gkv[:, j, c:c + 1]
                            sv = stt[:, 0:1536].rearrange("p (h e) -> p h e", e=128)
                            scale_to(wkv_k[:, c, :].rearrange("p (h e) -> p h e", e=64), sv[:, :, 0:64], g, bst)
                            scale_to(wkv_v[:, c, :].rearrange("p (h e) -> p h e", e=64), sv[:, :, 64:128], g, bst)
                    else:
                        HB = B_IN // 2
                        for c in range(8):
                            for hf in range(2):
                                stt, bst = stage_load(w_in_d[j, c * 128:(c + 1) * 128, hf * HB:(hf + 1) * HB], HB)
                                scale_to(w_in[:, c, hf * HB:(hf + 1) * HB], stt[:, 0:HB], gpre[:, l, c:c + 1], bst)
                    ck(1)
                    wm = sb("wm", [128, 8, 512], BF16)
                    for c in range(8):
                        stt, bst = stage_load(w_mem_kv[l, c * 128:(c + 1) * 128, :], 512)
                        scale_to(wm[:, c, :], stt[:, 0:512], gmem[:, c:c + 1], bst)

                    NPS = 6
                    pbank = [ps("pb%d" % i, [128, 512], F32) for i in range(NPS)]
                    b_pbank = [Buf("pb%d" % i) for i in range(NPS)]
                    tp = [ps("tp%d" % i, [128, 1024], BF16) for i in range(2)]
                    b_tp = [Buf("tp%d" % i) for i in range(2)]
                    pbi = [0]

                    def next_bank():
                        i = pbi[0] % NPS
                        pbi[0] += 1
                        return pbank[i], b_pbank[i]

                    evi = [0]

                    def evac_copy(out_ap, in_ap, b_in, b_out, eng=None):
                        if eng is None:
                            eng = "act" if evi[0] % 2 == 0 else "dve"
                            evi[0] += 1
                        if eng == "act":
                            P.op("act", lambda e: e.activation(out=out_ap, in_=in_ap, func=AF.Copy), reads=[b_in], writes=[b_out])
                        else:
                            P.op("dve", lambda e: e.tensor_copy(out=out_ap, in_=in_ap), reads=[b_in], writes=[b_out])

                    for hm in range(4):
                        pb, bpb = next_bank()
                        for c in range(8):
                            P.op("pe", lambda e, c=c, hm=hm, pb=pb: e.matmul(pb[0:64, 0:NMEM], lhsT=wm[:, c, hm * 64:(hm + 1) * 64],
                                                                            rhs=memnT[:, c, :], start=(c == 0), stop=(c == 7)),
                                 reads=[b_wd, b_wp, b_wa, b_memnT], writes=[bpb], sig=(c == 7))
                        evac_copy(km_sb[:, hm, :], pb[0:64, 0:NMEM], bpb, b_km)
                    for kt in range(2):
                        pb, bpb = next_bank()
                        for c in range(8):
                            P.op("pe", lambda e, c=c, kt=kt, pb=pb: e.matmul(pb[:, 0:256], lhsT=memnT[:, c, kt * 128:(kt + 1) * 128],
                                                                            rhs=wm[:, c, 256:512], start=(c == 0), stop=(c == 7)),
                                 reads=[b_wd, b_wp, b_wa, b_memnT], writes=[bpb], sig=(c == 7))
                        evac_copy(vm_sb[:, kt, :, 0:64], pb[:, 0:256].rearrange("p (h d) -> p h d", d=64), bpb, b_vm)

                    ck(2)
                    NXT = 4
                    xt = [sb("xt%d" % i, [128, D], F32) for i in range(NXT)]
                    b_xt = [Buf("xt%d" % i) for i in range(NXT)]
                    xs = [sb("xs%d" % i, [128, D], BF16) for i in range(2)]
                    b_xs = [Buf("xs%d" % i) for i in range(2)]
                    junk = sb("junk", [128, D], BF16)
                    b_junk = Buf("junk")
                    ss = [sb("ss%d" % i, [128, 1], F32) for i in range(4)]
                    b_ss = [Buf("ss%d" % i) for i in range(4)]
                    hnT = [sb("hnT%d" % i, [128, 8, 512], BF16) for i in range(2)]
                    b_hnT = [Buf("hnT%d" % i) for i in range(2)]
                    qn_sb = sb("qn_sb", [128, 6, 512], BF16)
                    b_qn = [Buf("qn%d" % i) for i in range(6)]
                    kn_sb = sb("kn_sb", [128, 6, 512], BF16)
                    b_kn = [Buf("kn%d" % i) for i in range(6)]
                    v_sb = sb("v_sb", [128, 12, 4, 65], BF16)
                    b_v = Buf("v_sb")
                    P.op("pool", lambda e: e.memset(v_sb[:], 1.0), writes=[b_v])
                    sg_sb = sb("sg_sb", [128, 8, 512], BF16)
                    b_sg = [Buf("sg%d" % i) for i in range(8)]
                    qm_sb = sb("qm_sb", [128, 2, 512], BF16)
                    b_qm = [Buf("qm%d" % i) for i in range(2)]
                    if lt == "a":
                        cs = [sb("cs%d" % i, [128, 2, 512], F32) for i in range(2)]
                        b_cs = [Buf("cs%d" % i) for i in range(2)]
                        cq_sb = sb("cq_sb", [128, 5, 512], BF16)
                        sq_sb = sb("sq_sb", [128, 5, 512], BF16)
                        cn_sb = sb("cn_sb", [128, 5, 512], BF16)
                        b_cq = [Buf("cq%d" % i) for i in range(5)]
                        b_sq = [Buf("sq%d" % i) for i in range(5)]
                        b_cn = [Buf("cn%d" % i) for i in range(5)]
                        rsb = [sb("rsb%d" % i, [128, 512], F32) for i in range(2)]
                        b_rsb = [Buf("rsb%d" % i) for i in range(2)]
                        t1 = [sb("t1_%d" % i, [128, 512], F32) for i in range(2)]
                        t2 = [sb("t2_%d" % i, [128, 512], F32) for i in range(2)]
                        b_t1 = [Buf("t1_%d" % i) for i in range(2)]
                        b_t2 = [Buf("t2_%d" % i) for i in range(2)]
                        qro_sb = sb("qro_sb", [128, 3, 512], BF16)
                        b_qro = [Buf("qro%d" % i) for i in range(3)]
                        kro_sb = sb("kro_sb", [128, 512], BF16)
                        b_kro = Buf("kro")
                        rpi = [0]
                    else:
                        f1 = sb("f1", [12, 512], F32)
                        f2 = sb("f2", [12, 512], F32)
                        b_f1 = Buf("f1")
                        b_f2 = Buf("f2")
                        ones12 = sb("ones12", [12, 512], F32)
                        P.op("pool", lambda e: e.memset(ones12[:], 1.0), writes=[b_w])
                        Fc = [sb("Fc%d" % i, [12, 512], F32) for i in range(2)]
                        b_Fc = [Buf("Fc%d" % i) for i in range(2)]
                        x0 = sb("x0", [12, 512], F32)
                        x1 = sb("x1", [12, 512], F32)
                        b_x0 = Buf("x0")
                        b_x1 = Buf("x1")
                        fq = sb("fq", [12, 6, 512], BF16)
                        fk = sb("fk", [12, 6, 512], BF16)
                        b_fq = Buf("fq")
                        b_fk = Buf("fk")
                        P.op("pool", lambda e: e.memset(fq[:], 1.0), writes=[b_fq])
                        P.op("pool", lambda e: e.memset(fk[:], 1.0), writes=[b_fk])

                    def load_tile(tt):
                        import os
                        for s in range(4):
                            i = (tt * 4 + s) % NXT
                            r0 = tt * 512 + s * 128
                            if os.environ.get("KADDR0"):
                                r0 = s * 128
                            P.dma(b_xt[i], xt[i][:], h_in[r0:r0 + 128, :], writes=[b_xt[i]] + ([b_hnT[0], b_tp[0], b_tp[1]] if (os.environ.get("KSER") and tt > 0) else []), tag="xt%d" % tt)
                        if lt == "a":
                            k = tt % 2
                            P.dma(b_cs[k], cs[k][:, 0, :], COS[:, tt * 512:(tt + 1) * 512], writes=[b_cs[k]], tag="cs%d" % tt)
                            P.dma(b_cs[k], cs[k][:, 1, :], SIN[:, tt * 512:(tt + 1) * 512], writes=[b_cs[k]], tag="cs%d" % tt)

                    def mm_group(out_ap, bpb, lhs_fn, rhs_fn, nchunk, reads):
                        for c in range(nchunk):
                            P.op("pe", lambda e, c=c: e.matmul(out_ap, lhsT=lhs_fn(c), rhs=rhs_fn(c), start=(c == 0),
                                                               stop=(c == nchunk - 1)),
                                 reads=reads, writes=[bpb], sig=(c == nchunk - 1))

                    def prep(tt):
                        hT = hnT[tt % 2]
                        bhT = b_hnT[tt % 2]
                        for s in range(4):
                            i = (tt * 4 + s) % NXT
                            P.op("act", lambda e, i=i, s=s: e.activation(out=junk[:], in_=xt[i][:], func=AF.Square,
                                                                         accum_out=ss[s][:, 0:1]),
                                 reads=[b_xt[i]], writes=[b_junk, b_ss[s]])
                            P.op("dve", lambda e, s=s: e.tensor_scalar(out=ss[s][:], in0=ss[s][:], scalar1=1.0 / D, scalar2=EPS,
                                                                       op0=ALU.mult, op1=ALU.add), reads=[b_ss[s]], writes=[b_ss[s]])
                            P.op("act", lambda e, s=s: e.activation(out=ss[s][:], in_=ss[s][:], func=AF.Sqrt),
                                 reads=[b_ss[s]], writes=[b_ss[s]])
                            P.op("dve", lambda e, s=s: e.reciprocal(out=ss[s][:], in_=ss[s][:]), reads=[b_ss[s]], writes=[b_ss[s]])
                            ck(2.6)
                            k = s % 2
                            P.op("dve", lambda e, i=i, s=s, k=k: e.tensor_scalar(out=xs[k][:], in0=xt[i][:], scalar1=ss[s][:, 0:1],
                                                                                 scalar2=None, op0=ALU.mult),
                                 reads=[b_xt[i], b_ss[s]], writes=[b_xs[k]])
                            for c in range(8):
                                P.op("pe", lambda e, c=c, k=k: e.transpose(out=tp[k][:, c * 128:(c + 1) * 128],
                                                                           in_=xs[k][:, c * 128:(c + 1) * 128], identity=ident[:]),
                                     reads=[b_xs[k], b_const], writes=[b_tp[k]], sig=(c == 7))
                            ck(2.8)
                            evac_copy(hT[:, :, s * 128:(s + 1) * 128], tp[k][:, :].rearrange("p (c t) -> p c t", c=8),
                                      b_tp[k], bhT, eng="dve")
                            ck(2.9 + 0.01 * s)
                        if tt + 1 < NT:
                            load_tile(tt + 1)

                    ck(2.2)
                    load_tile(0)
                    ck(2.4)
                    prep(0)
                    for tt in range(NT):
                        t0 = tt * 512
                        tsl = slice(t0, t0 + 512)
                        hT = hnT[tt % 2]
                        bhT = b_hnT[tt % 2]
                        ck(3)

                        def inproj(col0, M, evac):
                            pb, bpb = next_bank()
                            mm_group(pb[0:M, :], bpb, lambda c: w_in[:, c, col0:col0 + M], lambda c: hT[:, c, :], 8, [b_wd, b_wp, b_wa, bhT])
                            evac(pb, bpb)

                        def store_heads(sb_t, bufs, dram, row0, nrow, per):
                            for g in range(len(bufs)):
                                for e_ in range(per):
                                    hh = g * per + e_
                                    P.dma(bufs[g], dram[hh, row0:row0 + nrow, tsl], sb_t[e_ * nrow:(e_ + 1) * nrow, g, :],
                                          reads=[bufs[g]], tag="heads")

                        def gate_and_qm(col_qm, col_g):
                            for pr in range(2):
                                inproj(col_qm + pr * 128, 128,
                                       lambda pb, bpb, pr=pr: evac_copy(qm_sb[:, pr, :], pb[:, :], bpb, b_qm[pr]))
                                for e_ in range(2):
                                    P.dma(b_qm[pr], QM[2 * pr + e_, :, tsl], qm_sb[e_ * 64:(e_ + 1) * 64, pr, :], reads=[b_qm[pr]], tag="qm")
                            for g in range(8):
                                def ev(pb, bpb, g=g):
                                    P.op("act", lambda e: e.activation(out=sg_sb[:, g, :], in_=pb[:, :], func=AF.Silu),
                                         reads=[bpb], writes=[b_sg[g]])
                                inproj(col_g + g * 128, 128, ev)
                                for e_ in range(2):
                                    P.dma(b_sg[g], SG[2 * g + e_, :, tsl], sg_sb[e_ * 64:(e_ + 1) * 64, g, :], reads=[b_sg[g]], tag="sg")

                        if lt == "a":
                            csk = cs[tt % 2]
                            bcsk = b_cs[tt % 2]
                            for g in range(5):
                                def ev(pb, bpb, g=g):
                                    P.op("dve", lambda e: e.tensor_copy(out=cq_sb[:, g, :], in_=pb[:, :]), reads=[bpb], writes=[b_cq[g]])
                                    P.op("act", lambda e: e.activation(out=sq_sb[:, g, :], in_=cq_sb[:, g, :], func=AF.Square),
                                         reads=[b_cq[g]], writes=[b_sq[g]])
                                inproj(g * 128, 128, ev)
                            ck(4)
                            pbA, bA = next_bank()
                            mm_group(pbA[0:96, :], bA, lambda c: wkr[:, c, :], lambda c: hT[:, c, :], 8, [b_wd, b_wp, b_wa, bhT])
                            pbB, bB = next_bank()
                            mm_group(pbB[0:96, :], bB, lambda c: wkr_rot[:, c, :], lambda c: hT[:, c, :], 8, [b_wd, b_wp, b_wa, bhT])
                            r = rpi[0] % 2
                            rpi[0] += 1
                            P.op("dve", lambda e, r=r, pbA=pbA: e.tensor_tensor(out=t1[r][64:96, :], in0=pbA[64:96, :], in1=csk[64:96, 0, :],
                                                                                op=ALU.mult), reads=[bA, bcsk], writes=[b_t1[r]])
                            P.op("dve", lambda e, r=r, pbB=pbB: e.tensor_tensor(out=t2[r][64:96, :], in0=pbB[64:96, :], in1=csk[64:96, 1, :],
                                                                                op=ALU.mult), reads=[bB, bcsk], writes=[b_t2[r]])
                            P.op("pool", lambda e, r=r: e.tensor_tensor(out=kro_sb[64:96, :], in0=t1[r][64:96, :], in1=t2[r][64:96, :],
                                                                        op=ALU.add), reads=[b_t1[r], b_t2[r]], writes=[b_kro])
                            for hh in range(12):
                                P.dma(b_kro, KT[hh, 64:96, tsl], kro_sb[64:96, :], reads=[b_kro], tag="kro")
                            ck(5)
                            for (g0, ng, dim) in ((0, 3, 384.0), (3, 2, 256.0)):
                                pb, bpb = next_bank()
                                mm_group(pb[:, :], bpb, lambda c: ones_bf[:, :], lambda c, g0=g0: sq_sb[:, g0 + c, :], ng,
                                         [b_const] + b_sq[g0:g0 + ng])
                                k = 0 if g0 == 0 else 1
                                P.op("dve", lambda e, k=k, pb=pb, dim=dim: e.tensor_scalar(out=rsb[k][:], in0=pb[:, :], scalar1=1.0 / dim,
                                                                                         scalar2=EPS, op0=ALU.mult, op1=ALU.add),
                                     reads=[bpb], writes=[b_rsb[k]])
                                P.op("act", lambda e, k=k: e.activation(out=rsb[k][:], in_=rsb[k][:], func=AF.Sqrt),
                                     reads=[b_rsb[k]], writes=[b_rsb[k]])
                                P.op("dve", lambda e, k=k: e.reciprocal(out=rsb[k][:], in_=rsb[k][:]), reads=[b_rsb[k]], writes=[b_rsb[k]])
                                for g in range(g0, g0 + ng):
                                    eng = "pool" if g % 2 == 0 else "dve"
                                    P.op(eng, lambda e, g=g, k=k: e.tensor_tensor(out=cn_sb[:, g, :], in0=cq_sb[:, g, :], in1=rsb[k][:],
                                                                                  op=ALU.mult),
                                         reads=[b_cq[g], b_rsb[k]], writes=[b_cn[g]])
                            ck(6)
                            for pr in range(6):
                                pb, bpb = next_bank()
                                mm_group(pb[:, :], bpb, lambda c, pr=pr: wq_nope[:, c, pr * 128:(pr + 1) * 128],
                                         lambda c: cn_sb[:, c, :], 3, [b_wd, b_wp, b_wa] + b_cn[0:3])
                                evac_copy(qn_sb[:, pr, :], pb[:, :], bpb, b_qn[pr])
                            store_heads(qn_sb, b_qn, QT, 0, 64, 2)
                            ck(7)
                            for g in range(3):
                                pbA, bA = next_bank()
                                mm_group(pbA[:, :], bA, lambda c, g=g: wq_rope[:, c, g * 128:(g + 1) * 128],
                                         lambda c: cn_sb[:, c, :], 3, [b_wd, b_wp, b_wa] + b_cn[0:3])
                                pbB, bB = next_bank()
                                mm_group(pbB[:, :], bB, lambda c, g=g: wq_rot[:, c, g * 128:(g + 1) * 128],
                                         lambda c: cn_sb[:, c, :], 3, [b_wd, b_wp, b_wa] + b_cn[0:3])
                                r = rpi[0] % 2
                                rpi[0] += 1
                                P.op("dve", lambda e, r=r, pbA=pbA: e.tensor_tensor(out=t1[r][:], in0=pbA[:, :], in1=csk[:, 0, :], op=ALU.mult),
                                     reads=[bA, bcsk], writes=[b_t1[r]])
                                P.op("dve", lambda e, r=r, pbB=pbB: e.tensor_tensor(out=t2[r][:], in0=pbB[:, :], in1=csk[:, 1, :], op=ALU.mult),
                                     reads=[bB, bcsk], writes=[b_t2[r]])
                                P.op("pool", lambda e, r=r, g=g: e.tensor_tensor(out=qro_sb[:, g, :], in0=t1[r][:], in1=t2[r][:], op=ALU.add),
                                     reads=[b_t1[r], b_t2[r]], writes=[b_qro[g]])
                            store_heads(qro_sb, b_qro, QT, 64, 32, 4)
                            ck(8)
                            for pr in range(6):
                                pb, bpb = next_bank()
                                mm_group(pb[:, :], bpb, lambda c, pr=pr: wkv_k[:, c, pr * 128:(pr + 1) * 128],
                                         lambda c: cn_sb[:, 3 + c, :], 2, [b_wd, b_wp, b_wa] + b_cn[3:5])
                                evac_copy(kn_sb[:, pr, :], pb[:, :], bpb, b_kn[pr])
                            store_heads(kn_sb, b_kn, KT, 0, 64, 2)
                            ck(9)
                            for s in range(4):
                                for hf in range(2):
                                    pb, bpb = next_bank()
                                    mm_group(pb[:, 0:384], bpb, lambda c, s=s: cn_sb[:, 3 + c, s * 128:(s + 1) * 128],
                                             lambda c, hf=hf: wkv_v[:, c, hf * 384:(hf + 1) * 384], 2, [b_wd, b_wp, b_wa] + b_cn[3:5])
                                    evac_copy(v_sb[:, hf * 6:(hf + 1) * 6, s, 0:64],
                                              pb[:, 0:384].rearrange("p (h d) -> p h d", d=64), bpb, b_v)
                            P.dma(b_v, VS[:, :, tt * 4:(tt + 1) * 4, :].rearrange("h p k d -> p h k d"), v_sb[:], reads=[b_v], tag="vs")
                            ck(10)
                            if tt + 1 < NT:
                                prep(tt + 1)
                            gate_and_qm(672, 928)
                        else:
                            for pr in range(6):
                                inproj(pr * 128, 128, lambda pb, bpb, pr=pr: evac_copy(qn_sb[:, pr, :], pb[:, :], bpb, b_qn[pr]))
                            store_heads(qn_sb, b_qn, QT, 0, 64, 2)
                            ck(4)
                            for pr in range(6):
                                inproj(768 + pr * 128, 128, lambda pb, bpb, pr=pr: evac_copy(kn_sb[:, pr, :], pb[:, :], bpb, b_kn[pr]))
                            store_heads(kn_sb, b_kn, KT, 0, 64, 2)
                            ck(5)
                            for s in range(4):
                                for hf in range(2):
                                    pb, bpb = next_bank()
                                    mm_group(pb[:, 0:384], bpb, lambda c, s=s: hT[:, c, s * 128:(s + 1) * 128],
                                             lambda c, hf=hf: w_in[:, c, 1536 + hf * 384:1536 + (hf + 1) * 384], 8, [b_wd, b_wp, b_wa, bhT])
                                    evac_copy(v_sb[:, hf * 6:(hf + 1) * 6, s, 0:64],
                                              pb[:, 0:384].rearrange("p (h d) -> p h d", d=64), bpb, b_v)
                            P.dma(b_v, VS[:, :, tt * 4:(tt + 1) * 4, :].rearrange("h p k d -> p h k d"), v_sb[:], reads=[b_v], tag="vs")
                            ck(6)
                            pb, bpb = next_bank()
                            mm_group(pb[0:12, :], bpb, lambda c: w_in[:, c, 2304:2316], lambda c: hT[:, c, :], 8, [b_wd, b_wp, b_wa, bhT])
                            P.op("act", lambda e, pb=pb: e.activation(out=f1[:], in_=pb[0:12, :], func=AF.Exp, scale=-1.0,
                                                                      bias=negbf[:, j:j + 1]), reads=[bpb, b_const], writes=[b_f1])
                            P.op("dve", lambda e: e.tensor_scalar(out=f1[:], in0=f1[:], scalar1=1.0, scalar2=None, op0=ALU.add),
                                 reads=[b_f1], writes=[b_f1])
                            P.op("act", lambda e: e.activation(out=f2[:], in_=f1[:], func=AF.Ln), reads=[b_f1], writes=[b_f2])
                            cur = Fc[tt % 2]
                            bcur = b_Fc[tt % 2]
                            prv = Fc[(tt + 1) % 2]
                            bprv = b_Fc[(tt + 1) % 2]
                            init = 0.0 if tt == 0 else prv[:, 511:512]
                            P.op("dve", lambda e, cur=cur, init=init: e.tensor_tensor_scan(out=cur[:], data0=ones12[:], data1=f2[:],
                                                                                           initial=init, op0=ALU.mult, op1=ALU.subtract),
                                 reads=[b_f2, b_wp, bprv], writes=[bcur])
                            inv_s = 1.0 / scale
                            P.op("dve", lambda e, cur=cur: e.tensor_scalar(out=x0[:], in0=cur[:], scalar1=inv_s, scalar2=None, op0=ALU.mult),
                                 reads=[bcur], writes=[b_x0])
                            P.op("dve", lambda e: e.tensor_copy(out=fq[:, 0, :], in_=x0[:]), reads=[b_x0], writes=[b_fq])
                            P.op("dve", lambda e: e.tensor_tensor(out=x1[:], in0=x0[:], in1=fq[:, 0, :], op=ALU.subtract),
                                 reads=[b_x0, b_fq], writes=[b_x1])
                            P.op("dve", lambda e: e.tensor_copy(out=fq[:, 1, :], in_=x1[:]), reads=[b_x1], writes=[b_fq])
                            P.op("dve", lambda e: e.tensor_tensor(out=x0[:], in0=x1[:], in1=fq[:, 1, :], op=ALU.subtract),
                                 reads=[b_x1, b_fq], writes=[b_x0])
                            P.op("dve", lambda e: e.tensor_copy(out=fq[:, 2, :], in_=x0[:]), reads=[b_x0], writes=[b_fq])
                            P.op("dve", lambda e: e.tensor_scalar(out=fk[:, 3:6, :], in0=fq[:, 0:3, :], scalar1=-1.0, scalar2=None,
                                                                  op0=ALU.mult), reads=[b_fq], writes=[b_fk])
                            ck(7)
                            P.dma(b_fq, QT[:, 64:70, tsl], fq[:], reads=[b_fq])
                            P.dma(b_fk, KT[:, 64:70, tsl], fk[:], reads=[b_fk])
                            ck(8)
                            if tt + 1 < NT:
                                prep(tt + 1)
                            gate_and_qm(2316, 2572)
                except StopBuild:
                    pass
                P.barrier()
                P.flush()

            if _stop == "A":
                break
            with ExitStack() as st:
                sb = lambda name, shape, dt: st.enter_context(nc.sbuf_tensor(U(name), shape, dt))
                ps = lambda name, shape, dt: st.enter_context(nc.psum_tensor(U(name), shape, dt))
                qT = [sb("qT%d" % i, [96, S], BF16) for i in range(2)]
                kT = [sb("kT%d" % i, [96, S], BF16) for i in range(2)]
                vv = [sb("vv%d" % i, [128, NKT, 65], BF16) for i in range(2)]
                sg = [sb("sg%d" % i, [64, S], BF16) for i in range(2)]
                b_qT = [Buf("qT%d" % i) for i in range(2)]
                b_kT = [Buf("kT%d" % i) for i in range(2)]
                b_vv = [Buf("vv%d" % i) for i in range(2)]
                b_sgh = [Buf("sgh%d" % i) for i in range(2)]
                NPT = 6
                pT = [sb("pT%d" % i, [128, 1024], BF16) for i in range(NPT)]
                b_pT = [Buf("pT%d" % i) for i in range(NPT)]
                NFR = 4
                rec = [sb("rec%d" % i, [128, 512], F32) for i in range(NFR)]
                b_rec = [Buf("rec%d" % i) for i in range(NFR)]
                rsum = [sb("rsum%d" % i, [128, 512], F32) for i in range(NFR)]
                b_rsum = [Buf("rsum%d" % i) for i in range(NFR)]
                rech = [sb("rech%d" % i, [128, 512], BF16) for i in range(NFR)]
                recl = [sb("recl%d" % i, [128, 512], BF16) for i in range(NFR)]
                b_rech = [Buf("rech%d" % i) for i in range(NFR)]
                tmpo = [sb("tmpo%d" % i, [64, 512], F32) for i in range(4)]
                b_tmpo = [Buf("tmpo%d" % i) for i in range(4)]
                ysb = [sb("ysb%d" % i, [64, 512], BF16) for i in range(3)]
                b_ysb = [Buf("ysb%d" % i) for i in range(3)]
                NSP = 3
                NOP = 1
                spp = [ps("spp%d" % i, [128, 1024], F32) for i in range(NSP)]
                b_spp = [Buf("spp%d" % i) for i in range(NSP)]
                ops_ = [ps("ops%d" % i, [128, 512], F32) for i in range(NOP)]
                b_ops = [Buf("ops%d" % i) for i in range(NOP)]
                bcp = ps("bcp", [128, 512], F32)
                b_bcp = Buf("bcp")

                def load_head(hd):
                    k = hd % 2
                    if hd < 12:
                        P.dma(b_qT[k], qT[k][0:DQ, :], QT[hd, 0:DQ, :], writes=[b_qT[k]])
                        P.dma(b_kT[k], kT[k][0:DQ, :], KT[hd, 0:DQ, :], writes=[b_kT[k]])
                        P.dma(b_vv[k], vv[k][:], VS[hd, :, :, :], writes=[b_vv[k]])
                    else:
                        P.dma(b_qT[k], qT[k][0:64, :], QM[hd - 12, :, :], writes=[b_qT[k]])
                    P.dma(b_sgh[k], sg[k][:], SG[hd, :, :], writes=[b_sgh[k]])

                PE_EMBED = True
                FDEFER = 5
                heads_done = []

                def tick():
                    for p_ in pending:
                        p_[0] -= 1
                    while pending and pending[0][0] <= 0:
                        pending.pop(0)[1]()

                cnt = Ctx()
                cnt.g = 0
                cnt.blk = 0
                pending = []

                load_head(0)
                for hd in range(16):
                    k = hd % 2
                    if hd + 1 < 16:
                        load_head(hd + 1)
                    causal = hd < 12
                    if causal:
                        dq = DQ
                        sc = scale
                        kT_ap = lambda kb, k=k, dq=dq: kT[k][0:dq, kb * 128:(kb + 1) * 128]
                        v_ap = lambda kb, k=k: vv[k][:, kb, :]
                        kv_reads = [b_kT[k], b_vv[k]]
                    else:
                        dq = 64
                        sc = 0.125
                        hm = hd - 12
                        kT_ap = lambda kb, hm=hm: km_sb[:, hm, kb * 128:(kb + 1) * 128]
                        v_ap = lambda kb, hm=hm: vm_sb[:, kb, hm, :]
                        kv_reads = [b_km, b_vm]
                    for qi in range(NT):
                        nkb = 4 * (qi + 1) if causal else 2
                        ob = cnt.blk % NFR
                        cnt.blk += 1
                        o_ps = ops_[ob % NOP]
                        b_o = b_ops[ob % NOP]
                        q0 = qi * 512
                        ngrp = nkb // 2
                        grp_info = []

                        def emit_qk(g):
                            gi = cnt.g
                            cnt.g += 1
                            sp_t = spp[gi % NSP]
                            b_sp = b_spp[gi % NSP]
                            offs = []
                            for e_ in range(2):
                                kb = 2 * g + e_
                                o = (kb - 4 * qi) * 128 if (causal and kb >= 4 * qi) else 0
                                diag = causal and kb >= 4 * qi
                                lo = e_ * 512 + o
                                hi = (e_ + 1) * 512
                                P.op("pe", lambda e, kb=kb, lo=lo, hi=hi, o=o, diag=diag, sp_t=sp_t: e.matmul(
                                    sp_t[:, lo:hi], lhsT=kT_ap(kb), rhs=qT[k][0:dq, q0 + o:q0 + 512], start=True, stop=(not diag)),
                                     reads=[b_qT[k]] + kv_reads, writes=[b_sp], sig=(e_ == 1 and not diag), embed=PE_EMBED)
                                if diag:
                                    P.op("pe", lambda e, lo=lo, sp_t=sp_t: e.matmul(sp_t[:, lo:lo + 128], lhsT=ident[:], rhs=maskb[:],
                                                                                  start=False, stop=True),
                                         reads=[b_const], writes=[b_sp], sig=(e_ == 1))
                                offs.append((kb, o, lo, hi))
                            grp_info.append((gi, sp_t, b_sp, offs))

                        def emit_exp_pv(g):
                            gi, sp_t, b_sp, offs = grp_info[g]
                            pt = pT[gi % NPT]
                            b_pt = b_pT[gi % NPT]
                            if offs[0][1] == 0 and offs[1][1] == 0:
                                P.op("act", lambda e: e.activation(out=pt[:, :], in_=sp_t[:, :], func=AF.Exp, scale=sc),
                                     reads=[b_sp], writes=[b_pt])
                            else:
                                for (kb, o, lo, hi) in offs:
                                    P.op("act", lambda e, lo=lo, hi=hi: e.activation(out=pt[:, lo:hi], in_=sp_t[:, lo:hi],
                                                                                     func=AF.Exp, scale=sc),
                                         reads=[b_sp], writes=[b_pt])
                            for (kb, o, lo, hi) in offs:
                                last = (kb == nkb - 1)
                                P.op("pe", lambda e, kb=kb, o=o, lo=lo, hi=hi, last=last: e.matmul(
                                    o_ps[0:65, o:512], lhsT=v_ap(kb), rhs=pt[:, lo:hi], start=(kb == 0), stop=last),
                                     reads=[b_pt] + kv_reads, writes=[b_o], sig=last, embed=PE_EMBED)

                        def fin1(hd=hd, qi=qi, ob=ob, o_ps=o_ps, b_o=b_o, k=k, act_recip=(not causal)):
                            rc, b_rc, tm, b_tm, rs_, b_rs = rec[ob], b_rec[ob], tmpo[ob], b_tmpo[ob], rsum[ob], b_rsum[ob]
                            P.op("dve", lambda e: e.tensor_tensor(out=tm[:], in0=o_ps[0:64, :], in1=sg[k][:, qi * 512:(qi + 1) * 512],
                                                                  op=ALU.mult), reads=[b_o, b_sgh[k]], writes=[b_tm])
                            P.op("dve", lambda e: e.tensor_copy(out=rs_[64:65, :], in_=o_ps[64:65, :]), reads=[b_o], writes=[b_rs])
                            if act_recip:
                                P.op("act", lambda e: e.activation(out=rc[64:65, :], in_=rs_[64:65, :], func=AF.Ln),
                                     reads=[b_rs], writes=[b_rc])
                                P.op("act", lambda e: e.activation(out=rc[64:65, :], in_=rc[64:65, :], func=AF.Exp, scale=-1.0),
                                     reads=[b_rc], writes=[b_rc])
                            else:
                                P.op("dve", lambda e: e.reciprocal(out=rc[64:65, :], in_=rs_[64:65, :]), reads=[b_rs], writes=[b_rc])
                            rh, rl, b_rh = rech[ob], recl[ob], b_rech[ob]
                            P.op("dve", lambda e: e.tensor_copy(out=rh[64:65, :], in_=rc[64:65, :]), reads=[b_rc], writes=[b_rh])
                            P.op("dve", lambda e: e.tensor_tensor(out=rl[64:65, :], in0=rc[64:65, :], in1=rh[64:65, :], op=ALU.subtract),
                                 reads=[b_rc, b_rh], writes=[b_rh])

                        def fin2(hd=hd, qi=qi, ob=ob):
                            tm, b_tm, rh, rl, b_rh = tmpo[ob], b_tmpo[ob], rech[ob], recl[ob], b_rech[ob]
                            P.op("pe", lambda e: e.matmul(bcp[0:64, :], lhsT=ones_bf[64:65, 0:64], rhs=rh[64:65, :], start=True, stop=False),
                                 reads=[b_const, b_rh], writes=[b_bcp], sig=False)
                            P.op("pe", lambda e: e.matmul(bcp[0:64, :], lhsT=ones_bf[64:65, 0:64], rhs=rl[64:65, :], start=False, stop=True),
                                 reads=[b_const, b_rh], writes=[b_bcp])
                            yi = (hd * NT + qi) % 3
                            P.op("dve", lambda e: e.tensor_tensor(out=ysb[yi][:], in0=tm[:], in1=bcp[0:64, :], op=ALU.mult),
                                 reads=[b_tm, b_bcp], writes=[b_ysb[yi]])
                            P.dma(b_ysb[yi], YT[hd * 64:(hd + 1) * 64, qi * 512:(qi + 1) * 512], ysb[yi][:], reads=[b_ysb[yi]])

                        for g_ in range(min(NSP, ngrp)):
                            emit_qk(g_)
                        for g in range(ngrp):
                            emit_exp_pv(g)
                            if g + NSP < ngrp:
                                emit_qk(g + NSP)
                            tick()
                        fin1()
                        while len(pending) >= NFR - 1:
                            pending.pop(0)[1]()
                        pending.append([FDEFER, fin2])
                        if qi == NT - 1:
                            heads_done.append(hd)
                while pending:
                    pending.pop(0)[1]()
                P.barrier()
                P.flush()

            if _stop == "B":
                break
            with ExitStack() as st:
                sb = lambda name, shape, dt: st.enter_context(nc.sbuf_tensor(U(name), shape, dt))
                ps = lambda name, shape, dt: st.enter_context(nc.psum_tensor(U(name), shape, dt))
                wo = sb("wo", [128, 8, D], BF16)
                b_wo = Buf("wo")
                stage = [sb("stagec%d" % i, [128, D], F32) for i in range(2)]
                b_stage = [Buf("stagec%d" % i) for i in range(2)]
                for c in range(8):
                    i = c % 2
                    P.dma(b_stage[i], stage[i][:], w_out[l, c * 128:(c + 1) * 128, :], writes=[b_stage[i]])
                    eng = "dve" if c % 2 == 0 else "pool"
                    P.op(eng, lambda e, c=c, i=i: e.tensor_copy(out=wo[:, c, :], in_=stage[i][:]), reads=[b_stage[i]], writes=[b_wo])
                grow = sb("grow", [1, D], F32)
                b_grow = Buf("grow")
                gpost = sb("gpost", [128, D], F32)
                b_gpost = Buf("gpost")
                P.dma(b_grow, grow[:], norm_post[l:l + 1, :], writes=[b_grow])
                pso = [ps("pso%d" % i, [128, 1024], F32) for i in range(3)]
                b_pso = [Buf("pso%d" % i) for i in range(3)]
                for hf in range(2):
                    P.op("pe", lambda e, hf=hf: e.matmul(pso[0][:, hf * 512:(hf + 1) * 512], lhsT=ones_f[0:1, :],
                                                         rhs=grow[0:1, hf * 512:(hf + 1) * 512], start=True, stop=True),
                         reads=[b_const, b_grow], writes=[b_pso[0]], sig=(hf == 1))
                P.op("dve", lambda e: e.tensor_copy(out=gpost[:], in_=pso[0][:, :]), reads=[b_pso[0]], writes=[b_gpost])
                yT = [sb("yT%d" % i, [128, 8, 512], BF16) for i in range(2)]
                b_yT = [Buf("yT%d" % i) for i in range(2)]
                NXT = 6
                xt = [sb("xtc%d" % i, [128, D], F32) for i in range(NXT)]
                b_xt = [Buf("xtc%d" % i) for i in range(NXT)]
                tsb = [sb("tsb%d" % i, [128, D], F32) for i in range(2)]
                b_tsb = [Buf("tsb%d" % i) for i in range(2)]
                usb = [sb("usb%d" % i, [128, D], F32) for i in range(2)]
                b_usb = [Buf("usb%d" % i) for i in range(2)]
                ho = [sb("ho%d" % i, [128, D], F32) for i in range(3)]
                b_ho = [Buf("ho%d" % i) for i in range(3)]
                junk = sb("junkc", [128, D], BF16)
                b_junk = Buf("junkc")
                ssc = [sb("ssc%d" % i, [128, 1], F32) for i in range(4)]
                b_ssc = [Buf("ssc%d" % i) for i in range(4)]

                def load_c(tt):
                    k = tt % 2
                    P.dma(b_yT[k], yT[k][:], YT[:, tt * 512:(tt + 1) * 512].rearrange("(c p) t -> p c t", p=128), writes=[b_yT[k]])

                def load_x(it):
                    i = it % NXT
                    P.dma(b_xt[i], xt[i][:], h_in[it * 128:(it + 1) * 128, :], writes=[b_xt[i]])

                load_c(0)
                for it in range(min(4, NKT)):
                    load_x(it)
                for tt in range(NT):
                    k = tt % 2
                    if tt + 1 < NT:
                        load_c(tt + 1)
                    for s in range(4):
                        it = tt * 4 + s
                        if it + 4 < NKT:
                            load_x(it + 4)
                        pi = it % 3
                        po = pso[pi]
                        b_po = b_pso[pi]
                        for hf in range(2):
                            for c in range(8):
                                P.op("pe", lambda e, c=c, hf=hf, s=s, k=k, po=po: e.matmul(
                                    po[:, hf * 512:(hf + 1) * 512], lhsT=yT[k][:, c, s * 128:(s + 1) * 128],
                                    rhs=wo[:, c, hf * 512:(hf + 1) * 512], start=(c == 0), stop=(c == 7)),
                                     reads=[b_yT[k], b_wo], writes=[b_po], sig=(c == 7 and hf == 1))
                        sc_ = ssc[it % 4]
                        b_sc = b_ssc[it % 4]
                        ti = it % 2
                        P.op("act", lambda e, po=po, sc_=sc_: e.activation(out=junk[:], in_=po[:, :], func=AF.Square, accum_out=sc_[:, 0:1]),
                             reads=[b_po], writes=[b_junk, b_sc, b_tsb[ti]])
                        P.op("dve", lambda e, po=po, ti=ti: e.tensor_tensor(out=tsb[ti][:], in0=po[:, :], in1=gpost[:], op=ALU.mult),
                             reads=[b_po, b_gpost], writes=[b_tsb[ti]])
                        P.op("dve", lambda e, sc_=sc_: e.tensor_scalar(out=sc_[:], in0=sc_[:], scalar1=1.0 / D, scalar2=EPS,
                                                                       op0=ALU.mult, op1=ALU.add), reads=[b_sc], writes=[b_sc])
                        P.op("act", lambda e, sc_=sc_: e.activation(out=sc_[:], in_=sc_[:], func=AF.Sqrt), reads=[b_sc], writes=[b_sc])
                        P.op("dve", lambda e, sc_=sc_: e.reciprocal(out=sc_[:], in_=sc_[:]), reads=[b_sc], writes=[b_sc])
                        xi = it % NXT
                        hi_ = it % 3
                        P.op("act", lambda e, ti=ti, sc_=sc_: e.activation(out=usb[ti][:], in_=tsb[ti][:], func=AF.Copy, scale=sc_[:, 0:1]),
                             reads=[b_tsb[ti], b_sc], writes=[b_usb[ti]])
                        P.op("pool", lambda e, ti=ti, xi=xi, hi_=hi_: e.tensor_tensor(out=ho[hi_][:], in0=usb[ti][:], in1=xt[xi][:], op=ALU.add),
                             reads=[b_usb[ti], b_xt[xi]], writes=[b_ho[hi_]])
                        P.dma(b_ho[hi_], h_out[it * 128:(it + 1) * 128, :], ho[hi_][:], reads=[b_ho[hi_]])
                P.barrier()
                P.flush()
        P_ninst = P.ninst
    nc._ninst = P_ninst
    return nc


LAYER_TYPES = ["a", "b", "a", "b"]
_NC_CACHE = {}


def rope_const():
    half = 16
    inv = (10000.0 ** (-np.arange(half, dtype=np.float32) / half)).astype(np.float32)
    return np.tile(inv, 8).reshape(128, 1).astype(np.float32)


def make_in_maps(inputs, B):
    c = lambda a: np.ascontiguousarray(a)
    shared = {
        "norm_pre": c(inputs["norm_pre"]), "norm_post": c(inputs["norm_post"]),
        "mem_norm": c(inputs["mem_norm"]).reshape(1, D),
        "w_mem_kv": c(inputs["w_mem_kv"]), "w_out": c(inputs["w_out"]),
        "w_in_a": c(inputs["w_in_a"]), "q_norm_a": c(inputs["q_norm_a"]), "kv_norm_a": c(inputs["kv_norm_a"]),
        "w_q_up_a": c(inputs["w_q_up_a"]), "w_kv_up_a": c(inputs["w_kv_up_a"]),
        "w_in_b": c(inputs["w_in_b"]), "b_f": c(inputs["b_f"]),
        "ropec": rope_const(),
    }
    maps = []
    for b in range(B):
        m = dict(shared)
        m["x"] = c(inputs["x"][b])
        m["mem"] = c(inputs["mem"][b])
        m["pos"] = c(inputs["positions"][b]).reshape(1, -1).astype(np.int32)
        maps.append(m)
    return maps


def kernel(**inputs):
    inputs = {k: np.asarray(v) for k, v in inputs.items()}
    B, S, _ = inputs["x"].shape
    key = (S, tuple(LAYER_TYPES))
    if key not in _NC_CACHE:
        _NC_CACHE[key] = build(S, LAYER_TYPES)
    nc = _NC_CACHE[key]
    maps = make_in_maps(inputs, B)
    res = run_bass_kernel_spmd(nc, maps, core_ids=list(range(B)))
    return np.stack([np.asarray(r["out"]) for r in res.results], axis=0).astype(np.float32)
```
